# Optimizing a Trainium2 kernel written in Bass

```python
import math
import jax, jax.numpy as jnp
from jax import lax
import numpy as np

D_MODEL = 1024
BATCH = 4
SEQ = 4096
DEPTH = 2

M_HEADS = 4
M_HEAD_DIM = 256
M_WIDTH = M_HEADS * M_HEAD_DIM
M_CHUNK = 128
CONV_WIDTH = 5
ATT_SLOTS = 8
ATT_HEAD_DIM = 64
ATT_PATTERNS = ((128, 1), (512, 4), (2048, 16))
N_PATTERNS = 3
ATT_HEADS = ATT_SLOTS * N_PATTERNS
ATT_WIDTH = ATT_HEADS * ATT_HEAD_DIM
ATT_OUT = ATT_SLOTS * ATT_HEAD_DIM
D_FF = -(-8 * D_MODEL // (3 * 256)) * 256
DN_ALPHA = (2 * DEPTH) ** 0.25
DN_BETA = (8 * DEPTH) ** -0.25
LN_EPS = 1e-5
NEG = -1e30
IN_SECTIONS = (2 * M_WIDTH, M_WIDTH, M_WIDTH, 4 * M_HEADS, 3 * ATT_WIDTH, 2 * D_MODEL)
N_IN = sum(IN_SECTIONS)

kernel_name = "hybrid_mlstm_dilated_alibi_deepnorm_adaln"


def layer_norm(x, g=None, b=None):
    xf = x.astype(jnp.float32)
    mu = xf.mean(-1, keepdims=True)
    var = jnp.square(xf - mu).mean(-1, keepdims=True)
    y = (xf - mu) * lax.rsqrt(var + LN_EPS)
    if g is not None:
        y = y * g.astype(jnp.float32) + b.astype(jnp.float32)
    return y.astype(x.dtype)


def alibi_slopes():
    h = np.arange(1, ATT_HEADS + 1, dtype=np.float32)
    s = np.exp2(-8.0 * h / ATT_HEADS).astype(np.float32)
    return jnp.asarray(s.reshape(N_PATTERNS, ATT_SLOTS))


def centred_dwconv(x, w, b):
    k, ch = w.shape
    y = lax.conv_general_dilated(x, w[:, None, :], window_strides=(1,),
                                 padding=((k // 2, k // 2),),
                                 dimension_numbers=('NWC', 'WIO', 'NWC'),
                                 feature_group_count=ch)
    return y + b


def mlstm_scan(q, k, v, i_pre, f_pre):
    B, H, S, dk = q.shape
    dv = v.shape[-1]
    nc = S // M_CHUNK

    def chunks(a):
        return jnp.moveaxis(a.reshape(B, H, nc, M_CHUNK, *a.shape[3:]), 2, 0)

    xs = (chunks(q), chunks(k), chunks(v), chunks(i_pre), chunks(jax.nn.log_sigmoid(f_pre)))
    tri = jnp.tril(jnp.ones((M_CHUNK, M_CHUNK), dtype=bool))

    def step(carry, inp):
        C, n, m = carry
        qc, kc, vc, ic, fc = inp
        b = jnp.cumsum(fc, axis=-1)
        log_d = jnp.where(tri, b[..., :, None] - b[..., None, :] + ic[..., None, :], NEG)
        m_inter = b + m[..., None]
        m_t = jnp.maximum(log_d.max(-1), m_inter)
        w_inter = jnp.exp(m_inter - m_t)
        s = jnp.einsum('bhtd,bhsd->bhts', qc, kc) * jnp.exp(log_d - m_t[..., None])
        num = jnp.einsum('bhts,bhsv->bhtv', s, vc) + w_inter[..., None] * jnp.einsum('bhtd,bhdv->bhtv', qc, C)
        den = s.sum(-1) + w_inter * jnp.einsum('bhtd,bhd->bht', qc, n)
        h = num / jnp.maximum(jnp.abs(den), jnp.exp(-m_t))[..., None]
        b_last = b[..., -1]
        log_w = b_last[..., None] - b + ic
        m_new = jnp.maximum(b_last + m, log_w.max(-1))
        w_state = jnp.exp(log_w - m_new[..., None])
        decay = jnp.exp(b_last + m - m_new)
        C = decay[..., None, None] * C + jnp.einsum('bhs,bhsd,bhsv->bhdv', w_state, kc, vc)
        n = decay[..., None] * n + jnp.einsum('bhs,bhsd->bhd', w_state, kc)
        return (C, n, m_new), h

    init = (jnp.zeros((B, H, dk, dv), jnp.float32), jnp.zeros((B, H, dk), jnp.float32),
            jnp.full((B, H), NEG, jnp.float32))
    _, h = lax.scan(step, init, xs)
    return jnp.moveaxis(h, 0, 2).reshape(B, H, S, dv)


def mlstm_branch(qk, v, o_pre, gate_pre, b_gates, conv_w, conv_b, gn_w):
    B, S, _ = v.shape
    f32 = jnp.float32
    qk = jax.nn.silu(centred_dwconv(qk, conv_w.astype(qk.dtype), conv_b.astype(qk.dtype)))

    def heads(a):
        return a.astype(f32).reshape(B, S, M_HEADS, M_HEAD_DIM).transpose(0, 2, 1, 3)

    q, k = jnp.split(qk, 2, axis=-1)
    q = heads(q) * M_HEAD_DIM ** -0.5
    k = heads(k)
    vh = heads(v)
    g = (gate_pre.astype(f32) + b_gates.astype(f32)).reshape(B, S, 4, M_HEADS).transpose(2, 0, 3, 1)
    i_f, f_f, i_b, f_b = g[0], g[1], g[2], g[3]
    flip = lambda a: jnp.flip(a, axis=2)
    h = mlstm_scan(q, k, vh, i_f, f_f) + flip(mlstm_scan(flip(q), flip(k), flip(vh), flip(i_b), flip(f_b)))
    mu = h.mean(-1, keepdims=True)
    var = jnp.square(h - mu).mean(-1, keepdims=True)
    hn = ((h - mu) * lax.rsqrt(var + LN_EPS)).transpose(0, 2, 1, 3).reshape(B, S, M_WIDTH)
    return (jax.nn.sigmoid(o_pre.astype(f32)) * hn * gn_w.astype(f32)).astype(v.dtype)


def banded_attention(q, k, v, slopes, dil, half):
    B, H, G, n, dh = q.shape
    nb = -(-n // half)
    n_pad = nb * half
    pad0 = ((0, 0),) * 3
    qb = jnp.pad(q, pad0 + ((0, n_pad - n), (0, 0))).reshape(B, H, G, nb, half, dh)

    def kblocks(a):
        ap = jnp.pad(a, pad0 + ((half, n_pad - n + half), (0, 0))).reshape(B, H, G, nb + 2, half, dh)
        return jnp.concatenate([ap[:, :, :, :-2], ap[:, :, :, 1:-1], ap[:, :, :, 2:]], axis=-2)

    kb, vb = kblocks(k), kblocks(v)
    qi = jnp.arange(n_pad).reshape(nb, half)
    kj = (jnp.arange(nb) * half - half)[:, None] + jnp.arange(3 * half)[None, :]
    dist = jnp.abs(qi[:, :, None] - kj[:, None, :])
    valid = (dist <= half) & (kj[:, None, :] >= 0) & (kj[:, None, :] < n)
    bias = -slopes[:, None, None, None] * (dil * dist).astype(jnp.float32)
    s = jnp.einsum('bhgnqd,bhgnkd->bhgnqk', qb, kb) * ATT_HEAD_DIM ** -0.5 + bias[None, :, None]
    s = jnp.where(valid, s, NEG)
    lse = jax.nn.logsumexp(s, axis=-1)
    o = jnp.einsum('bhgnqk,bhgnkd->bhgnqd', jnp.exp(s - lse[..., None]), vb)
    return o.reshape(B, H, G, n_pad, dh)[:, :, :, :n], lse.reshape(B, H, G, n_pad)[:, :, :, :n]


def dilated_attention(qkv, slopes):
    B, S, _ = qkv.shape
    qkv = qkv.astype(jnp.float32).reshape(B, S, 3, N_PATTERNS, ATT_SLOTS, ATT_HEAD_DIM)
    outs, lses = [], []
    for g, (win, dil) in enumerate(ATT_PATTERNS):
        n = S // dil

        def to_res(a):
            return a.reshape(B, n, dil, ATT_SLOTS, ATT_HEAD_DIM).transpose(0, 3, 2, 1, 4)

        o, lse = banded_attention(to_res(qkv[:, :, 0, g]), to_res(qkv[:, :, 1, g]),
                                  to_res(qkv[:, :, 2, g]), slopes[g], dil, win // (2 * dil))
        outs.append(o.transpose(0, 3, 2, 1, 4).reshape(B, S, ATT_SLOTS, ATT_HEAD_DIM))
        lses.append(lse.transpose(0, 3, 2, 1).reshape(B, S, ATT_SLOTS))
    w = jax.nn.softmax(jnp.stack(lses, axis=0), axis=0)
    o = (w[..., None] * jnp.stack(outs, axis=0)).sum(0)
    return o.reshape(B, S, ATT_OUT)


def mixer(u, w_in, b_gates, conv_w, conv_b, gn_w, w_a, w_b, w_out, slopes):
    z = u @ w_in
    idx = np.cumsum(IN_SECTIONS)[:-1].tolist()
    qk_m, v_m, o_m, gate_m, qkv_a, gates_br = jnp.split(z, idx, axis=-1)
    y_a = mlstm_branch(qk_m, v_m, o_m, gate_m, b_gates, conv_w, conv_b, gn_w) @ w_a
    y_b = dilated_attention(qkv_a, slopes).astype(u.dtype) @ w_b
    g_a, g_b = jnp.split(gates_br, 2, axis=-1)
    return (jax.nn.sigmoid(g_a) * y_a + jax.nn.sigmoid(g_b) * y_b) @ w_out


def swiglu(u, w1, w3, w2):
    return (jax.nn.silu(u @ w1) * (u @ w3)) @ w2


def setup_inputs(seed: int = 0) -> dict:
    key = jax.random.key(seed)
    ks = jax.random.split(key, 24)

    def nrm(k, shape, scale):
        return jax.random.normal(k, shape, jnp.float32) * scale

    f_bias = jnp.linspace(3.0, 6.0, M_HEADS, dtype=jnp.float32)
    b_gates = jnp.concatenate([
        nrm(ks[3], (DEPTH, M_HEADS), 0.1),
        f_bias + nrm(ks[4], (DEPTH, M_HEADS), 0.1),
        nrm(ks[5], (DEPTH, M_HEADS), 0.1),
        f_bias + nrm(ks[6], (DEPTH, M_HEADS), 0.1)], axis=-1)
    return {
        "x": nrm(ks[0], (BATCH, SEQ, D_MODEL), 1.0),
        "c": nrm(ks[1], (BATCH, D_MODEL), 1.0),
        "w_in": nrm(ks[2], (DEPTH, D_MODEL, N_IN), D_MODEL ** -0.5),
        "b_gates": b_gates,
        "conv_w": nrm(ks[7], (DEPTH, CONV_WIDTH, 2 * M_WIDTH), CONV_WIDTH ** -0.5),
        "conv_b": nrm(ks[8], (DEPTH, 2 * M_WIDTH), 0.02),
        "gn_w": 1.0 + nrm(ks[9], (DEPTH, M_WIDTH), 0.02),
        "w_a": nrm(ks[10], (DEPTH, M_WIDTH, D_MODEL), M_WIDTH ** -0.5),
        "w_b": nrm(ks[11], (DEPTH, ATT_OUT, D_MODEL), ATT_OUT ** -0.5),
        "w_out": nrm(ks[12], (DEPTH, D_MODEL, D_MODEL), DN_BETA * D_MODEL ** -0.5),
        "w_ada": nrm(ks[13], (DEPTH, D_MODEL, 6 * D_MODEL), 0.5 * D_MODEL ** -0.5),
        "b_ada": nrm(ks[14], (DEPTH, 6 * D_MODEL), 0.02),
        "ln1_g": 1.0 + nrm(ks[15], (DEPTH, D_MODEL), 0.02),
        "ln1_b": nrm(ks[16], (DEPTH, D_MODEL), 0.02),
        "w1": nrm(ks[17], (DEPTH, D_MODEL, D_FF), D_MODEL ** -0.5),
        "w3": nrm(ks[18], (DEPTH, D_MODEL, D_FF), D_MODEL ** -0.5),
        "w2": nrm(ks[19], (DEPTH, D_FF, D_MODEL), DN_BETA * D_FF ** -0.5),
        "ln2_g": 1.0 + nrm(ks[20], (DEPTH, D_MODEL), 0.02),
        "ln2_b": nrm(ks[21], (DEPTH, D_MODEL), 0.02),
    }


def reference(x, c, w_in, b_gates, conv_w, conv_b, gn_w, w_a, w_b, w_out, w_ada, b_ada,
              ln1_g, ln1_b, w1, w3, w2, ln2_g, ln2_b):
    slopes = alibi_slopes()
    c_act = jax.nn.silu(c)
    h = layer_norm(x)
    for l in range(DEPTH):
        ada = (c_act @ w_ada[l] + b_ada[l])[:, None, :]
        sh1, sc1, g1, sh2, sc2, g2 = jnp.split(ada, 6, axis=-1)
        u = h * (1.0 + sc1) + sh1
        y = mixer(u, w_in[l], b_gates[l], conv_w[l], conv_b[l], gn_w[l], w_a[l], w_b[l], w_out[l], slopes)
        h = layer_norm(DN_ALPHA * h + g1 * y, ln1_g[l], ln1_b[l])
        u = h * (1.0 + sc2) + sh2
        y = swiglu(u, w1[l], w3[l], w2[l])
        h = layer_norm(DN_ALPHA * h + g2 * y, ln2_g[l], ln2_b[l])
    return h
```

```python
import numpy as np
import ml_dtypes
from contextlib import ExitStack
import concourse.bass as bass
import concourse.mybir as mybir
from concourse.bass_utils import run_bass_kernel_spmd

F32, BF16 = mybir.dt.float32, mybir.dt.bfloat16
AF = mybir.ActivationFunctionType
ALU = mybir.AluOpType

S = 4096
D = 1024
NT = 32
KC = 8
DFF = 2816
NIN = 10768
DEPTH = 2
ALPHA = (2 * DEPTH) ** 0.25
EPS = 1e-5
DILS = (1, 4, 16)
NCORES = 4


class Res:
    __slots__ = ("w", "r", "name")

    def __init__(self, name=""):
        self.w = None
        self.r = {}
        self.name = name


class Prog:
    ENG = ("pe", "act", "dve", "pool", "sp")

    def __init__(self):
        self.ops = {e: [] for e in self.ENG}
        self.cnt = {e: 0 for e in self.ENG}
        self.seen = {e: {} for e in self.ENG}
        self.ndsem = {"sp": 20, "pool": 20}
        self.dcount = {q: 0 for q in self.ndsem}
        self.last = {}
        self.bar = {e: None for e in self.ENG}

    def _wait(self, eng, key, val, waits):
        if key == ("e", "pe") and eng == "pe":
            return
        if self.seen[eng].get(key, 0) >= val:
            return
        self.seen[eng][key] = val
        waits.append((key, val))

    def add(self, eng, fn, reads=(), writes=(), dma=False):
        waits = []
        if self.bar[eng] is not None:
            for key, val in self.bar[eng].items():
                self._wait(eng, key, val, waits)
            self.bar[eng] = None
        for r in reads:
            if r.w is not None:
                self._wait(eng, r.w[0], r.w[1], waits)
        for w in writes:
            if w.w is not None:
                self._wait(eng, w.w[0], w.w[1], waits)
            for key, val in w.r.items():
                self._wait(eng, key, val, waits)
        if dma:
            j = self.dcount[eng]
            self.dcount[eng] += 1
            n = self.ndsem[eng]
            idx, rnd = j % n, j // n
            key = ("d", eng, idx)
            if rnd > 0:
                self._wait(eng, key, 16 * rnd, waits)
            ev = (key, 16 * (rnd + 1))
        else:
            self.cnt[eng] += 1
            ev = (("e", eng), self.cnt[eng])
        self.last[ev[0]] = ev[1]
        self.ops[eng].append((waits, fn, ev))
        for r in reads:
            if r.r.get(ev[0], 0) < ev[1]:
                r.r[ev[0]] = ev[1]
        for w in writes:
            w.w = ev
            w.r = {}
        return ev

    def barrier(self):
        snap = dict(self.last)
        for e in self.ENG:
            self.bar[e] = dict(snap)

    def keys(self):
        ks = [("e", e) for e in self.ENG]
        for q, n in self.ndsem.items():
            ks += [("d", q, i) for i in range(n)]
        return ks

    def emit(self, block, sems):
        dec = {"pe": block.tensor, "act": block.scalar, "dve": block.vector,
               "pool": block.gpsimd, "sp": block.sync}
        final = dict(self.last)
        for eng in self.ENG:
            ops = self.ops[eng]

            def body(e, ops=ops, eng=eng):
                for waits, fn, ev in ops:
                    for key, val in waits:
                        e.wait_ge(sems[key], val)
                    ins = fn(e)
                    ins.then_inc(sems[ev[0]], 16 if ev[0][0] == "d" else 1)
                if eng == "sp":
                    for key, val in final.items():
                        e.wait_ge(sems[key], val)

            dec[eng](body)


def build(n_layers=DEPTH, debug=None):
    debug = debug or set()
    nc = bass.Bass("TRN2", target_bir_lowering=False)
    P = Prog()
    L = n_layers

    def din(name, shape, dt=F32):
        return nc.dram_tensor(name, list(shape), dt, kind="ExternalInput").ap()

    def dscr(name, shape, dt=F32):
        kind = "ExternalOutput" if name in debug else "Internal"
        return nc.dram_tensor(name, list(shape), dt, kind=kind).ap()

    x_d = din("x", [S, D])
    ccol_d = din("ccol", [128, KC])
    w_in_d = din("w_in", [L, D, NIN])
    wg_d = din("wg", [L, D, 2, 36])
    bg_d = din("bg", [L, 36, 2])
    convw_d = din("convw", [L, 128, 16, 5])
    convb_d = din("convb", [L, 128, 16])
    gnw_d = din("gn_w", [L, D])
    w_a_d = din("w_a", [L, D, D])
    w_b_d = din("w_b", [L, 512, D])
    w_out_d = din("w_out", [L, D, D])
    w_ada_d = din("w_ada", [L, D, 6 * D])
    b_ada_d = din("b_ada", [L, 6 * D])
    ln1g_d = din("ln1_g", [L, D]); ln1b_d = din("ln1_b", [L, D])
    ln2g_d = din("ln2_g", [L, D]); ln2b_d = din("ln2_b", [L, D])
    w1_d = din("w1", [L, D, DFF]); w3_d = din("w3", [L, D, DFF]); w2_d = din("w2", [L, DFF, D])
    identf_d = din("identf", [128, 128])
    identb_d = din("identb", [128, 128], BF16)
    tri_d = din("tri", [128, 2, 128], BF16)
    sel_d = din("sel", [128, 2, 128])
    wdec_d = din("wdec", [128, 72, 128], BF16)
    out_d = nc.dram_tensor("out", [S, D], F32, kind="ExternalOutput").ap()

    h_scr = dscr("h_scr", [S, D])
    mh_scr = dscr("mh_scr", [S, D], BF16)
    ao_scr = [dscr(f"ao_scr{g}", [S, 520]) for g in range(3)]
    w1b_d = dscr("w1b", [L, D, DFF], BF16)
    w3b_d = dscr("w3b", [L, D, DFF], BF16)
    w2b_d = dscr("w2b", [L, DFF, D], BF16)
    ut_dbg = dscr("ut_dbg", [128, KC, S], BF16) if "ut_dbg" in debug else None
    R_h = [Res() for _ in range(NT)]
    R_mh = [Res() for _ in range(NT)]
    R_ao = [[Res() for _ in range(NT)] for _ in range(3)]
    R_ffw = Res()
    u2_scr = dscr("u2_scr", [128, KC, S], BF16)
    R_u2 = [Res() for _ in range(NT)]

    es = ExitStack()
    with es:
        CAP = 196608
        big = es.enter_context(nc.sbuf_tensor("big", [128, CAP], mybir.dt.uint8))
        alloc = {"off": 0, "peak": 0}

        def sb(name, shape, dt=F32, stack=None):
            esz = 4 if dt == F32 else 2
            nel = 1
            for d_ in shape[1:]:
                nel *= d_
            nb = nel * esz
            off = alloc["off"]
            alloc["off"] = off + ((nb + 63) // 64) * 64
            alloc["peak"] = max(alloc["peak"], alloc["off"])
            assert alloc["off"] <= CAP, (name, alloc["off"])
            ap = big[0:shape[0], off:off + nb].bitcast(dt)
            if len(shape) == 3:
                ap = ap.rearrange("p (a b) -> p a b", b=shape[2])
            elif len(shape) == 4:
                ap = ap.rearrange("p (a b c) -> p a b c", b=shape[2], c=shape[3])
            return ap

        class Scope:
            def __enter__(self):
                self.m = alloc["off"]
                return self

            def enter_context(self, x):
                return x

            def __exit__(self, *a):
                alloc["off"] = self.m
                P.barrier()
                return False

        def pst(name, shape, dt=F32):
            return es.enter_context(nc.psum_tensor(name, list(shape), dt))

        ps = [pst(f"ps{i}", [128, 512]) for i in range(6)]
        psR = [Res() for _ in range(6)]
        psb = [pst(f"psb{i}", [128, 1024], BF16) for i in range(2)]
        psbR = [Res() for _ in range(2)]
        rr = {"ps": 0, "psb": 0}

        def nps():
            i = rr["ps"]; rr["ps"] = (i + 1) % 6
            return ps[i], psR[i]

        def npsb():
            i = rr["psb"]; rr["psb"] = (i + 1) % 2
            return psb[i], psbR[i]

        def pe_mm(out, lhsT, rhs, start, stop, reads, writes):
            P.add("pe", lambda e: e.matmul(out, lhsT=lhsT, rhs=rhs, start=start, stop=stop), reads, writes)

        def pe_tr(out, in_, ident, reads, writes):
            P.add("pe", lambda e: e.transpose(out, in_, ident), reads, writes)

        def act(out, in_, func, reads, writes, bias=None, scale=None):
            kw = {}
            if bias is not None: kw["bias"] = bias
            if scale is not None: kw["scale"] = scale
            P.add("act", lambda e: e.activation(out=out, in_=in_, func=func, **kw), reads, writes)

        def tt(out, in0, in1, op, reads, writes, eng="dve"):
            P.add(eng, lambda e: e.tensor_tensor(out=out, in0=in0, in1=in1, op=op), reads, writes)

        def ts(out, in0, s1, s2, op0, op1, reads, writes, eng="dve"):
            if op1 is None:
                P.add(eng, lambda e: e.tensor_scalar(out=out, in0=in0, scalar1=s1, scalar2=None, op0=op0), reads, writes)
            else:
                P.add(eng, lambda e: e.tensor_scalar(out=out, in0=in0, scalar1=s1, scalar2=s2, op0=op0, op1=op1), reads, writes)

        def stt(out, in0, scalar, in1, op0, op1, reads, writes):
            P.add("dve", lambda e: e.scalar_tensor_tensor(out=out, in0=in0, scalar=scalar, in1=in1, op0=op0, op1=op1), reads, writes)

        def cp(out, in_, reads, writes, eng="dve"):
            P.add(eng, lambda e: e.tensor_copy(out=out, in_=in_), reads, writes)

        def mset(ap, val, writes, eng="dve"):
            P.add(eng, lambda e: e.memset(ap, val), (), writes)

        def bnstats(out, in_, reads, writes):
            P.add("dve", lambda e: e.bn_stats(out=out, in_=in_), reads, writes)

        def bnaggr(out, in_, reads, writes):
            P.add("dve", lambda e: e.bn_aggr(out=out, in_=in_), reads, writes)

        def recip(out, in_, reads, writes):
            P.add("dve", lambda e: e.reciprocal(out=out, in_=in_), reads, writes)

        def scan(out, d0, d1, init, op0, op1, reads, writes):
            P.add("dve", lambda e: e.tensor_tensor_scan(out=out, data0=d0, data1=d1, initial=init, op0=op0, op1=op1), reads, writes)

        def dma(out, in_, reads, writes, q="sp"):
            P.add(q, lambda e: e.dma_start(out=out, in_=in_), reads, writes, dma=True)

        UT = sb("UT", [128, KC, S], BF16)
        R_UT = [Res() for _ in range(NT)]
        identf = sb("identf", [128, 128]); R_c = Res()
        identb = sb("identb", [128, 128], BF16)
        zer = sb("zer", [128, 128])
        ones1 = sb("ones1", [1, 128])
        adac = sb("adac", [128, L, 4, KC])
        gb = sb("gb", [128, 2, D])
        R_ada = Res()
        cact = sb("cact", [128, KC]); R_cact = Res()

        dma(identf[:], identf_d[:, :], (), [R_c])
        dma(identb[:], identb_d[:, :], (), [R_c])
        dma(cact[:], ccol_d[:, :], (), [R_cact])
        epsc = sb("epsc", [128, 1])
        mset(epsc[:], EPS, [R_c])
        mset(zer[:], 0.0, [R_c])
        mset(ones1[:], 1.0, [R_c])
        act(cact[:], cact[:], AF.Silu, [R_cact], [R_cact])

        for l in range(L):
            for (src, dst, rows, cols) in ((w1_d, w1b_d, D, DFF), (w3_d, w3b_d, D, DFF), (w2_d, w2b_d, DFF, D)):
                for r0 in range(0, rows, 128):
                    dma(dst[l, r0:r0 + 128, :], src[l, r0:r0 + 128, :], (), [R_ffw], q="pool")

        def compute_ada(l, do_cols=True, do_gb=True):
            with Scope() as st:
                Cb = sb("Cb", [128, KC, 128], F32, st); R_Cb = Res()
                wad = [sb(f"wad{i}", [128, KC, 512], F32, st) for i in range(2)]; R_wad = [Res(), Res()]
                brow = [sb(f"brow{i}", [1, 512], F32, st) for i in range(2)]
                tmp = sb("adatmp", [128, 512], F32, st); R_tmp = Res()
                for kc in range(KC):
                    act(Cb[:, kc, :], zer[:], AF.Identity, [R_cact, R_c], [R_Cb], bias=cact[:, kc:kc + 1], scale=0.0)
                wv = w_ada_d[l].rearrange("(kc p) n -> p kc n", p=128)
                for j in range(12):
                    i = j % 2
                    isgb = (j // 2) in (2, 5)
                    if (isgb and not do_gb) or ((not isgb) and not do_cols):
                        continue
                    dma(wad[i][:], wv[:, :, j * 512:(j + 1) * 512], (), [R_wad[i]])
                    dma(brow[i][:], b_ada_d[l:l + 1, j * 512:(j + 1) * 512], (), [R_wad[i]])
                    pt, pr = nps()
                    for kc in range(KC):
                        pe_mm(pt[:, :], Cb[:, kc, :], wad[i][:, kc, :], kc == 0, False, [R_Cb, R_wad[i]], [pr])
                    pe_mm(pt[:, :], ones1[:, :], brow[i][:, :], False, True, [R_wad[i], R_c], [pr])
                    sec, half = j // 2, j % 2
                    if sec in (2, 5):
                        cp(gb[:, 0 if sec == 2 else 1, half * 512:(half + 1) * 512], pt[:, :], [pr], [R_ada])
                    else:
                        cp(tmp[:], pt[:, :], [pr], [R_tmp])
                        si = {0: 0, 1: 1, 3: 2, 4: 3}[sec]
                        p2, p2r = nps()
                        for q in range(4):
                            pe_tr(p2[:, q * 128:(q + 1) * 128], tmp[:, q * 128:(q + 1) * 128], identf[:], [R_tmp, R_c], [p2r])
                        for q in range(4):
                            kc = half * 4 + q
                            if si in (1, 3):
                                ts(adac[:, l, si, kc:kc + 1], p2[:, q * 128:q * 128 + 1], 1.0, None, ALU.add, None, [p2r], [R_ada])
                            else:
                                cp(adac[:, l, si, kc:kc + 1], p2[:, q * 128:q * 128 + 1], [p2r], [R_ada])
            P.barrier()

        def ln_tile(src, dst, st6, mv, R_src, R_dst, R_st, gamma=None, beta=None, R_gb=None):
            bnstats(st6[:, 0:6], src[:, 0:512], [R_src], [R_st])
            bnstats(st6[:, 6:12], src[:, 512:1024], [R_src], [R_st])
            bnaggr(mv[:, 0:2], st6[:, 0:12], [R_st], [R_st])
            act(mv[:, 2:3], mv[:, 1:2], AF.Sqrt, [R_st], [R_st], bias=epsc[:, 0:1])
            recip(mv[:, 3:4], mv[:, 2:3], [R_st], [R_st])
            ts(dst[:, :], src[:, :], mv[:, 0:1], mv[:, 3:4], ALU.subtract, ALU.mult, [R_src, R_st], [R_dst])
            if gamma is not None:
                tt(dst[:, :], dst[:, :], gamma, ALU.mult, [R_dst, R_gb], [R_dst], eng="pool")
                tt(dst[:, :], dst[:, :], beta, ALU.add, [R_dst, R_gb], [R_dst], eng="pool")

        def mod_to_T(ht, R_ht, dstT, col0, R_dst, which, l):
            for half in range(2):
                pt, pr = nps()
                for q in range(4):
                    kc = half * 4 + q
                    pe_tr(pt[:, q * 128:(q + 1) * 128], ht[:, kc * 128:(kc + 1) * 128], identf[:], [R_ht, R_c], [pr])
                for q in range(4):
                    kc = half * 4 + q
                    act(dstT[:, kc, col0:col0 + 128], pt[:, q * 128:(q + 1) * 128], AF.Identity, [pr, R_ada], [R_dst],
                        bias=adac[:, l, 2 * which, kc:kc + 1], scale=adac[:, l, 2 * which + 1, kc:kc + 1])

        compute_ada(0)
        if L > 1:
            compute_ada(1, do_gb=False)
        with Scope() as st:
            xt = [sb(f"xt{i}", [128, D], F32, st) for i in range(2)]; R_xt = [Res(), Res()]
            ht = [sb(f"ht{i}", [128, D], F32, st) for i in range(2)]; R_ht = [Res(), Res()]
            st6 = [sb(f"st6{i}", [128, 12], F32, st) for i in range(2)]
            mv = [sb(f"mv{i}", [128, 4], F32, st) for i in range(2)]; R_st = [Res(), Res()]
            for t in range(NT):
                i = t % 2
                dma(xt[i][:], x_d[t * 128:(t + 1) * 128, :], (), [R_xt[i]])
                ln_tile(xt[i], ht[i], st6[i], mv[i], R_xt[i], R_ht[i], R_st[i])
                dma(h_scr[t * 128:(t + 1) * 128, :], ht[i][:], [R_ht[i]], [R_h[t]], q="pool")
                mod_to_T(ht[i], R_ht[i], UT, t * 128, R_UT[t], 0, 0)
        P.barrier()

        for l in range(L):
            if "stopA" in debug:
                break
            with Scope() as stB:
                TA = sb("TA", [128, NT, 8], F32, stB); TB = sb("TB", [128, NT, 8], F32, stB)
                TG = sb("TG", [128, NT, 8], F32, stB); GL = sb("GL", [128, NT, 8], F32, stB)
                MP = sb("MP", [128, NT, 8], F32, stB); DEC = sb("DEC", [128, NT, 8], F32, stB)
                DECn = sb("DECn", [128, NT, 8], F32, stB); WS = sb("WS", [128, NT, 8], F32, stB)
                WS2 = sb("WS2", [128, NT, 8], F32, stB); ET = sb("ET", [128, NT, 8], F32, stB)
                R_tab = Res()
                tri = sb("tri", [128, 2, 128], BF16, stB)
                gnwb = sb("gnwb", [128, D], F32, stB)
                cw = sb("cw", [128, 16, 5], F32, stB); cb = sb("cb", [128, 16], F32, stB)
                R_mc = Res()
                dma(tri[:], tri_d[:, :, :], (), [R_mc])
                dma(gnwb[:], gnw_d[l:l + 1, :].partition_broadcast(128), (), [R_mc])
                dma(cw[:], convw_d[l], (), [R_mc])
                dma(cb[:], convb_d[l], (), [R_mc])
                with Scope() as st:
                    T1 = sb("T1", [36, S], F32, st); T2 = sb("T2", [36, S], F32, st); T3 = sb("T3", [36, S], F32, st)
                    R_T1, R_T2, R_T3 = Res(), Res(), Res()
                    wgf = sb("wgf", [128, KC, 2, 36], F32, st); wgb = sb("wgb", [128, KC, 2, 36], BF16, st); R_wg = Res()
                    bgc = sb("bgc", [36, 2], F32, st)
                    sel = sb("sel", [128, 2, 128], F32, st)
                    dma(wgf[:], wg_d[l].rearrange("(kc p) a b -> p kc a b", p=128), (), [R_wg])
                    dma(bgc[:], bg_d[l], (), [R_wg])
                    dma(sel[:], sel_d[:, :, :], (), [R_wg])
                    cp(wgb[:], wgf[:], [R_wg], [R_wg])
                    mset(T3[:], 0.0, [R_T3])
                    for tg in range(8):
                        for gi, (T, RT) in enumerate(((T1, R_T1), (T2, R_T2))):
                            pt, pr = nps()
                            for kc in range(KC):
                                pe_mm(pt[0:36, :], wgb[:, kc, gi, :], UT[:, kc, tg * 512:(tg + 1) * 512], kc == 0, kc == KC - 1,
                                      [R_wg] + R_UT[tg * 4:tg * 4 + 4], [pr])
                            act(T[:, tg * 512:(tg + 1) * 512], pt[0:36, :], AF.Identity, [pr, R_wg], [RT], bias=bgc[:, gi:gi + 1])
                    act(T2[:], T2[:], AF.Exp, [R_T2], [R_T2], scale=-1.0)
                    act(T2[:], T2[:], AF.Ln, [R_T2], [R_T2], bias=1.0)
                    ts(T2[:], T2[:], -0.5, None, ALU.mult, None, [R_T2], [R_T2])
                    scan(T3[0:4, :], T2[0:4, :], T2[0:4, :], 0.0, ALU.add, ALU.add, [R_T2], [R_T3])
                    scan(T3[32:36, ::-1], T2[32:36, ::-1], T2[32:36, ::-1], 0.0, ALU.add, ALU.add, [R_T2], [R_T3])
                    tt(T1[:], T1[:], T3[:], ALU.subtract, [R_T1, R_T3], [R_T1])
                    scan(T2[0:4, :], T1[0:4, :], T1[0:4, :], -1e30, ALU.max, ALU.max, [R_T1], [R_T2])
                    scan(T2[32:36, ::-1], T1[32:36, ::-1], T1[32:36, ::-1], -1e30, ALU.max, ALU.max, [R_T1], [R_T2])
                    for (T, RT, TT) in ((T1, R_T1, TA), (T3, R_T3, TB), (T2, R_T2, TG)):
                        for c0 in range(0, NT, 8):
                            pt, pr = nps()
                            for cc in range(8):
                                c = c0 + cc
                                pe_tr(pt[:, cc * 36:(cc + 1) * 36], T[0:36, c * 128:(c + 1) * 128], identf[0:36, 0:36], [RT, R_c], [pr])
                            pv = pt[:, 0:288].rearrange("p (c k) -> p c k", k=36)
                            cp(TT[:, c0:c0 + 8, 0:4], pv[:, :, 0:4], [pr], [R_tab])
                            cp(TT[:, c0:c0 + 8, 4:8], pv[:, :, 32:36], [pr], [R_tab])
                    TGf = TG[:, :, :].rearrange("p c k -> p (c k)")
                    for d_ in range(2):
                        pt, pr = nps()
                        pe_mm(pt[:, 0:256], sel[:, d_, :], TGf, True, True, [R_tab, R_wg], [pr])
                        pv = pt[:, 0:256].rearrange("p (c k) -> p c k", k=8)
                        cp(GL[:, :, d_ * 4:(d_ + 1) * 4], pv[:, :, d_ * 4:(d_ + 1) * 4], [pr], [R_tab])
                    cp(MP[:], GL[:], [R_tab], [R_tab])
                    cp(MP[:, 1:NT, 0:4], GL[:, 0:NT - 1, 0:4], [R_tab], [R_tab])
                    cp(MP[:, 0:NT - 1, 4:8], GL[:, 1:NT, 4:8], [R_tab], [R_tab])
                    tt(DEC[:], MP[:], GL[:], ALU.subtract, [R_tab], [R_tab])
                    act(DEC[:], DEC[:], AF.Exp, [R_tab], [R_tab])
                    tt(WS[:], TA[:], GL[:], ALU.subtract, [R_tab], [R_tab])
                    act(WS[:], WS[:], AF.Exp, [R_tab], [R_tab])
                    tt(ET[:], TB[:], GL[:], ALU.add, [R_tab], [R_tab])
                    act(ET[:], ET[:], AF.Exp, [R_tab], [R_tab], scale=-1.0)
                    mset(DECn[:], 1.0, [R_tab])
                    cp(DECn[:, 0:NT - 1, 0:4], DEC[:, 1:NT, 0:4], [R_tab], [R_tab])
                    cp(DECn[:, 1:NT, 4:8], DEC[:, 0:NT - 1, 4:8], [R_tab], [R_tab])
                    tt(WS2[:], WS[:], DECn[:], ALU.mult, [R_tab], [R_tab])
                P.barrier()

                for hd in range(4):
                    if "skip_mlstm" in debug:
                        break
                    with Scope() as stH:
                        qT = sb("qT", [128, 2, S], BF16, stH); kT = sb("kT", [128, 2, S], BF16, stH)
                        V1 = sb("V1", [128, NT, 257], BF16, stH)
                        R_qT, R_kT = Res(), Res()
                        R_V1 = [Res() for _ in range(NT)]
                        wo = sb("wo", [128, KC, 256], BF16, stH)
                        R_w = Res()
                        wi = w_in_d[l].rearrange("(kc p) n -> p kc n", p=128)
                        dma(wo[:], wi[:, :, 3072 + hd * 256:3072 + hd * 256 + 256], (), [R_w], q="pool")
                        mset(V1[:, :, 256:257], 1.0, R_V1)
                        with Scope() as st:
                            wq = sb("wq", [128, KC, 256], BF16, st); wk = sb("wk", [128, KC, 256], BF16, st)
                            wv = sb("wv", [128, KC, 256], BF16, st)
                            for (wt, c0) in ((wq, hd * 256), (wk, 1024 + hd * 256), (wv, 2048 + hd * 256)):
                                dma(wt[:], wi[:, :, c0:c0 + 256], (), [R_w], q="pool")
                            X = sb("X", [128, S + 4], F32, st); ACC = sb("ACC", [128, S], F32, st)
                            R_X, R_ACC = Res(), Res()
                            mset(X[:, 0:2], 0.0, [R_X]); mset(X[:, S + 2:S + 4], 0.0, [R_X])
                            for isk, (wt, dstT, R_dst) in enumerate(((wq, qT, R_qT), (wk, kT, R_kT))):
                                for j in range(2):
                                    cc = isk * 8 + hd * 2 + j
                                    for tg in range(8):
                                        pt, pr = nps()
                                        for kc in range(KC):
                                            pe_mm(pt[:, :], wt[:, kc, j * 128:(j + 1) * 128], UT[:, kc, tg * 512:(tg + 1) * 512],
                                                  kc == 0, kc == KC - 1, [R_w] + R_UT[tg * 4:tg * 4 + 4], [pr])
                                        act(X[:, 2 + tg * 512:2 + (tg + 1) * 512], pt[:, :], AF.Copy, [pr], [R_X])
                                    ts(ACC[:], X[:, 0:S], cw[:, cc, 0:1], cb[:, cc:cc + 1], ALU.mult, ALU.add, [R_X, R_mc], [R_ACC])
                                    for k in range(1, 5):
                                        stt(ACC[:], X[:, k:k + S], cw[:, cc, k:k + 1], ACC[:], ALU.mult, ALU.add, [R_X, R_mc, R_ACC], [R_ACC])
                                    if isk == 0:
                                        act(ACC[:], ACC[:], AF.Silu, [R_ACC], [R_ACC])
                                        ts(dstT[:, j, :], ACC[:], 1.0 / 16.0, None, ALU.mult, None, [R_ACC], [R_dst], eng="pool")
                                    else:
                                        act(dstT[:, j, :], ACC[:], AF.Silu, [R_ACC], [R_dst])
                            for t in range(NT):
                                pt, pr = nps()
                                for kc in range(KC):
                                    pe_mm(pt[:, 0:256], UT[:, kc, t * 128:(t + 1) * 128], wv[:, kc, :], kc == 0, kc == KC - 1,
                                          [R_w, R_UT[t]], [pr])
                                act(V1[:, t, 0:256], pt[:, 0:256], AF.Copy, [pr], [R_V1[t]])
                        P.barrier()
                        with Scope() as st:
                            H = sb("H", [128, NT, 256], F32, st); R_H = [Res() for _ in range(NT)]
                            hw = [False] * NT
                            C32 = [sb(f"C32_{d_}", [128, 2, 257], F32, st) for d_ in range(2)]
                            Cbf = [sb(f"Cbf_{d_}", [128, 2, 257], BF16, st) for d_ in range(2)]
                            R_C = [Res(), Res()]
                            PT = [sb(f"PT{d_}", [128, 128], BF16, st) for d_ in range(2)]; R_PT = [Res(), Res()]
                            ktok = [sb(f"ktok{d_}", [128, 256], BF16, st) for d_ in range(2)]; R_kt = [Res(), Res()]
                            sm = [sb(f"sm{d_}", [128, 4], F32, st) for d_ in range(2)]; R_sm = [Res(), Res()]
                            for d_ in range(2):
                                mset(C32[d_][:], 0.0, [R_C[d_]])
                                mset(Cbf[d_][:], 0.0, [R_C[d_]])
                            for step in range(NT):
                                for d_ in range(2):
                                    c = step if d_ == 0 else NT - 1 - step
                                    col = d_ * 4 + hd
                                    sl = slice(c * 128, (c + 1) * 128)
                                    last = step == NT - 1
                                    pS, pSr = nps()
                                    for dc in range(2):
                                        pe_mm(pS[:, 0:128], kT[:, dc, sl], qT[:, dc, sl], dc == 0, dc == 1, [R_kT, R_qT], [pSr])
                                    stt(PT[d_][:], pS[:, 0:128], WS[:, c, col:col + 1], tri[:, d_, :], ALU.mult, ALU.mult,
                                        [pSr, R_tab, R_mc], [R_PT[d_]])
                                    if not last:
                                        pK, pKr = npsb()
                                        for dc in range(2):
                                            pe_tr(pK[:, dc * 128:(dc + 1) * 128], kT[:, dc, sl], identb[:], [R_kT, R_c], [pKr])
                                        ts(ktok[d_][:], pK[:, 0:256], WS2[:, c, col:col + 1], None, ALU.mult, None, [pKr, R_tab], [R_kt[d_]])
                                    pO, pOr = nps()
                                    pe_mm(pO[:, 0:257], PT[d_][:], V1[:, c, :], True, False, [R_PT[d_], R_V1[c]], [pOr])
                                    for dc in range(2):
                                        pe_mm(pO[:, 0:257], qT[:, dc, sl], Cbf[d_][:, dc, :], False, dc == 1, [R_qT, R_C[d_]], [pOr])
                                    act(sm[d_][:, 0:1], pO[:, 256:257], AF.Abs, [pOr], [R_sm[d_]])
                                    tt(sm[d_][:, 1:2], sm[d_][:, 0:1], ET[:, c, col:col + 1], ALU.max, [R_sm[d_], R_tab], [R_sm[d_]])
                                    recip(sm[d_][:, 2:3], sm[d_][:, 1:2], [R_sm[d_]], [R_sm[d_]])
                                    if not hw[c]:
                                        ts(H[:, c, :], pO[:, 0:256], sm[d_][:, 2:3], None, ALU.mult, None, [pOr, R_sm[d_]], [R_H[c]])
                                        hw[c] = True
                                    else:
                                        stt(H[:, c, :], pO[:, 0:256], sm[d_][:, 2:3], H[:, c, :], ALU.mult, ALU.add, [pOr, R_sm[d_], R_H[c]], [R_H[c]])
                                    if not last:
                                        for dc in range(2):
                                            pU, pUr = nps()
                                            pe_mm(pU[:, 0:257], ktok[d_][:, dc * 128:(dc + 1) * 128], V1[:, c, :], True, True,
                                                  [R_kt[d_], R_V1[c]], [pUr])
                                            stt(C32[d_][:, dc, :], C32[d_][:, dc, :], DECn[:, c, col:col + 1], pU[:, 0:257], ALU.mult, ALU.add,
                                                [pUr, R_tab, R_C[d_]], [R_C[d_]])
                                        cp(Cbf[d_][:], C32[d_][:], [R_C[d_]], [R_C[d_]], eng="pool")
                            stg = sb("stg", [128, NT, 6], F32, st); mvg = sb("mvg", [128, NT, 2], F32, st)
                            rs = sb("rs", [128, NT], F32, st); R_g = Res()
                            for c in range(NT):
                                bnstats(stg[:, c, :], H[:, c, :], [R_H[c]], [R_g])
                            for c in range(NT):
                                bnaggr(mvg[:, c, :], stg[:, c, :], [R_g], [R_g])
                            act(rs[:], mvg[:, :, 1], AF.Sqrt, [R_g], [R_g], bias=epsc[:, 0:1])
                            recip(rs[:], rs[:], [R_g], [R_g])
                            sg = [sb(f"sg{i}", [128, 256], F32, st) for i in range(2)]; R_sg = [Res(), Res()]
                            mo = [sb(f"mo{i}", [128, 256], BF16, st) for i in range(2)]; R_mo = [Res(), Res()]
                            for t in range(NT):
                                i = t % 2
                                pt, pr = nps()
                                for kc in range(KC):
                                    pe_mm(pt[:, 0:256], UT[:, kc, t * 128:(t + 1) * 128], wo[:, kc, :], kc == 0, kc == KC - 1, [R_w, R_UT[t]], [pr])
                                act(sg[i][:], pt[:, 0:256], AF.Sigmoid, [pr], [R_sg[i]])
                                ts(H[:, t, :], H[:, t, :], mvg[:, t, 0:1], rs[:, t:t + 1], ALU.subtract, ALU.mult, [R_H[t], R_g], [R_H[t]])
                                tt(H[:, t, :], H[:, t, :], gnwb[:, hd * 256:(hd + 1) * 256], ALU.mult, [R_H[t], R_mc], [R_H[t]], eng="pool")
                                tt(mo[i][:], H[:, t, :], sg[i][:], ALU.mult, [R_H[t], R_sg[i]], [R_mo[i]])
                                dma(mh_scr[t * 128:(t + 1) * 128, hd * 256:(hd + 1) * 256], mo[i][:], [R_mo[i]], [R_mh[t]], q="pool")
                        P.barrier()
                P.barrier()

            with Scope() as stA:
                wdec = sb("wdec", [128, 72, 128], BF16, stA); R_wd = Res()
                dma(wdec[:], wdec_d[:, :, :], (), [R_wd])
                waq = sb("waq", [128, KC, 512], BF16, stA); wak = sb("wak", [128, KC, 512], BF16, stA)
                wav = sb("wav", [128, KC, 512], BF16, stA); R_wa = Res()
                kTa = sb("kTa", [128, 4, 768], BF16, stA); R_kTa = Res()
                qAB = sb("qAB", [128, 4, 2, 512], BF16, stA); R_q = Res()
                V1a = sb("V1a", [128, 6, 8, 65], BF16, stA); R_Va = Res()
                Eb = [sb(f"Eb{i}", [128, 4, 128], BF16, stA) for i in range(2)]; R_E = [Res(), Res()]
                PTa = [sb(f"PTa{i}", [128, 4, 128], BF16, stA) for i in range(3)]; R_PTa = [Res() for _ in range(3)]
                ost = [sb(f"ost{i}", [128, 520], F32, stA) for i in range(2)]; R_ost = [Res(), Res()]
                mset(qAB[:], 0.0, [R_q], eng="pool")
                mset(V1a[:, :, :, 64:65], 1.0, [R_Va])
                wi = w_in_d[l].rearrange("(kc p) n -> p kc n", p=128)
                ecount = 0
                ocount = 0
                for g, dil in enumerate(DILS):
                    if "skip_attn" in debug:
                        break
                    base = 4112 + g * 512
                    for (wt, c0) in ((waq, base), (wak, base + 1536), (wav, base + 3072)):
                        dma(wt[:], wi[:, :, c0:c0 + 512], (), [R_wa], q="pool")
                    n = S // dil
                    NJ = n // 128
                    for r in range(dil):
                        for J0 in range(0, NJ, 4):
                            J1 = min(J0 + 4, NJ)
                            a0 = max(J0 - 1, 0); a1 = min(J1, NJ - 1)
                            nkt = a1 - a0 + 1

                            def cols(a_start, ntile):
                                b0 = (128 * a_start) * dil + r
                                return slice(b0, b0 + (128 * ntile - 1) * dil + 1, dil) if dil > 1 else slice(b0, b0 + 128 * ntile)
                            for pair in range(4):
                                for t0 in range(0, nkt, 4):
                                    nt_ = min(4, nkt - t0)
                                    pt, pr = nps()
                                    for kc in range(KC):
                                        pe_mm(pt[:, 0:128 * nt_], wak[:, kc, pair * 128:(pair + 1) * 128], UT[:, kc, cols(a0 + t0, nt_)],
                                              kc == 0, kc == KC - 1, [R_wa] + R_UT, [pr])
                                    cp(kTa[:, pair, t0 * 128:(t0 + nt_) * 128], pt[:, 0:128 * nt_], [pr], [R_kTa])
                            nq = J1 - J0
                            for pair in range(4):
                                pt, pr = nps()
                                for kc in range(KC):
                                    pe_mm(pt[:, 0:128 * nq], waq[:, kc, pair * 128:(pair + 1) * 128], UT[:, kc, cols(J0, nq)],
                                          kc == 0, kc == KC - 1, [R_wa] + R_UT, [pr])
                                act(qAB[0:64, pair, 0, 0:128 * nq], pt[0:64, 0:128 * nq], AF.Copy, [pr], [R_q], scale=0.125)
                                act(qAB[64:128, pair, 1, 0:128 * nq], pt[64:128, 0:128 * nq], AF.Copy, [pr], [R_q], scale=0.125)
                            for ai in range(nkt):
                                pt, pr = nps()
                                for kc in range(KC):
                                    pe_mm(pt[:, :], UT[:, kc, cols(a0 + ai, 1)], wav[:, kc, :], kc == 0, kc == KC - 1, [R_wa] + R_UT, [pr])
                                cp(V1a[:, ai, :, 0:64], pt[:, :].rearrange("p (h d) -> p h d", d=64), [pr], [R_Va])
                            for J in range(J0, J1):
                                jl = J - J0
                                oi = ocount % 2; ocount += 1
                                for hg in range(2):
                                    chs = [ch for ch in range(3) if 0 <= J - 1 + ch < NJ]
                                    for ch in chs:
                                        ai = J - 1 + ch - a0
                                        pS, pSr = nps()
                                        for hh in range(4):
                                            h = hg * 4 + hh
                                            pe_mm(pS[:, hh * 128:(hh + 1) * 128], kTa[:, h // 2, ai * 128:(ai + 1) * 128],
                                                  qAB[:, h // 2, h % 2, jl * 128:(jl + 1) * 128], True, True, [R_kTa, R_q], [pSr])
                                        ei = ecount % 2; ecount += 1
                                        act(Eb[ei][:], pS[:, :].rearrange("p (h q) -> p h q", q=128), AF.Exp, [pSr], [R_E[ei]])
                                        wb0 = (g * 8 + hg * 4) * 3 + ch
                                        tt(PTa[ch][:], Eb[ei][:], wdec[:, wb0:wb0 + 10:3, :], ALU.mult, [R_E[ei], R_wd], [R_PTa[ch]], eng="pool")
                                    pO, pOr = nps()
                                    for hh in range(4):
                                        h = hg * 4 + hh
                                        for ci, ch in enumerate(chs):
                                            ai = J - 1 + ch - a0
                                            pe_mm(pO[:, hh * 65:(hh + 1) * 65], PTa[ch][:, hh, :], V1a[:, ai, h, :], ci == 0, ci == len(chs) - 1,
                                                  [R_PTa[ch], R_Va], [pOr])
                                    cp(ost[oi][:, hg * 260:(hg + 1) * 260], pO[:, 0:260], [pOr], [R_ost[oi]])
                                rb = (128 * J) * dil + r
                                rows = slice(rb, rb + 127 * dil + 1, dil) if dil > 1 else slice(rb, rb + 128)
                                tl0 = rb // 128; tl1 = (rb + 127 * dil) // 128
                                dma(ao_scr[g][rows, :], ost[oi][:], [R_ost[oi]], [R_ao[g][t] for t in range(tl0, tl1 + 1)], q="sp")
            P.barrier()
            if "stopB" in debug:
                break

            with Scope() as st:
                wa = sb("wa", [128, KC, D], BF16, st); wb = sb("wb", [128, 4, D], BF16, st)
                wgt = sb("wgt", [128, KC, 2048], BF16, st); R_w = Res()
                wi = w_in_d[l].rearrange("(kc p) n -> p kc n", p=128)
                dma(wa[:], w_a_d[l].rearrange("(kc p) n -> p kc n", p=128), (), [R_w], q="pool")
                dma(wb[:], w_b_d[l].rearrange("(kc p) n -> p kc n", p=128), (), [R_w], q="pool")
                for q4 in range(4):
                    dma(wgt[:, :, q4 * 512:(q4 + 1) * 512], wi[:, :, 8720 + q4 * 512:8720 + (q4 + 1) * 512], (), [R_w], q="pool")
                mht = [sb(f"mht{i}", [128, D], BF16, st) for i in range(2)]; R_mht = [Res(), Res()]
                aot = [[sb(f"aot{i}_{g}", [128, 520], F32, st) for g in range(3)] for i in range(2)]; R_aot = [Res(), Res()]
                rden = [sb(f"rden{i}", [128, 8], F32, st) for i in range(2)]
                aon = [sb(f"aon{i}", [128, 512], BF16, st) for i in range(2)]; R_aon = [Res(), Res()]
                mhT = [sb(f"mhT{i}", [128, KC, 128], BF16, st) for i in range(2)]; R_mhT = [Res(), Res()]
                aoT = [sb(f"aoT{i}", [128, 4, 128], BF16, st) for i in range(2)]; R_aoT = [Res(), Res()]
                sga = [sb(f"sga{i}", [128, 512], F32, st) for i in range(2)]; sgb = [sb(f"sgb{i}", [128, 512], F32, st) for i in range(2)]
                R_sgab = [Res(), Res()]
                gat = [sb(f"gat{i}", [128, D], BF16, st) for i in range(2)]; R_gat = [Res(), Res()]
                k2 = 0
                for t in range(NT):
                    i = t % 2
                    rows = slice(t * 128, (t + 1) * 128)
                    dma(mht[i][:], mh_scr[rows, :], [R_mh[t]], [R_mht[i]])
                    for g in range(3):
                        dma(aot[i][g][:], ao_scr[g][rows, :], [R_ao[g][t]], [R_aot[i]])
                    tt(aot[i][0][:], aot[i][0][:], aot[i][1][:], ALU.add, [R_aot[i]], [R_aot[i]], eng="pool")
                    tt(aot[i][0][:], aot[i][0][:], aot[i][2][:], ALU.add, [R_aot[i]], [R_aot[i]], eng="pool")
                    av = aot[i][0][:, :].rearrange("p (h d) -> p h d", d=65)
                    recip(rden[i][:, :], av[:, :, 64], [R_aot[i]], [R_aot[i]])
                    for h in range(8):
                        ts(aon[i][:, h * 64:(h + 1) * 64], av[:, h, 0:64], rden[i][:, h:h + 1], None, ALU.mult, None, [R_aot[i]], [R_aon[i]])
                    pB, pBr = npsb()
                    for kc in range(KC):
                        pe_tr(pB[:, kc * 128:(kc + 1) * 128], mht[i][:, kc * 128:(kc + 1) * 128], identb[:], [R_mht[i], R_c], [pBr])
                    cp(mhT[i][:], pB[:, :].rearrange("p (k t) -> p k t", t=128), [pBr], [R_mhT[i]])
                    pB, pBr = npsb()
                    for kc in range(4):
                        pe_tr(pB[:, kc * 128:(kc + 1) * 128], aon[i][:, kc * 128:(kc + 1) * 128], identb[:], [R_aon[i], R_c], [pBr])
                    cp(aoT[i][:], pB[:, 0:512].rearrange("p (k t) -> p k t", t=128), [pBr], [R_aoT[i]])
                    for half in range(2):
                        hs = slice(half * 512, (half + 1) * 512)
                        j2 = k2 % 2; k2 += 1
                        pga, pgar = nps()
                        for kc in range(KC):
                            pe_mm(pga[:, :], UT[:, kc, rows], wgt[:, kc, half * 512:(half + 1) * 512], kc == 0, kc == KC - 1, [R_UT[t], R_w], [pgar])
                        pgb, pgbr = nps()
                        for kc in range(KC):
                            pe_mm(pgb[:, :], UT[:, kc, rows], wgt[:, kc, 1024 + half * 512:1024 + (half + 1) * 512], kc == 0, kc == KC - 1, [R_UT[t], R_w], [pgbr])
                        act(sga[j2][:], pga[:, :], AF.Sigmoid, [pgar], [R_sgab[j2]])
                        act(sgb[j2][:], pgb[:, :], AF.Sigmoid, [pgbr], [R_sgab[j2]])
                        pya, pyar = nps()
                        for kc in range(KC):
                            pe_mm(pya[:, :], mhT[i][:, kc, :], wa[:, kc, hs], kc == 0, kc == KC - 1, [R_mhT[i], R_w], [pyar])
                        pyb, pybr = nps()
                        for kc in range(4):
                            pe_mm(pyb[:, :], aoT[i][:, kc, :], wb[:, kc, hs], kc == 0, kc == 3, [R_aoT[i], R_w], [pybr])
                        tt(sga[j2][:], sga[j2][:], pya[:, :], ALU.mult, [R_sgab[j2], pyar], [R_sgab[j2]])
                        tt(sgb[j2][:], sgb[j2][:], pyb[:, :], ALU.mult, [R_sgab[j2], pybr], [R_sgab[j2]])
                        tt(gat[i][:, hs], sga[j2][:], sgb[j2][:], ALU.add, [R_sgab[j2]], [R_gat[i]], eng="pool")
                    pB, pBr = npsb()
                    for kc in range(KC):
                        pe_tr(pB[:, kc * 128:(kc + 1) * 128], gat[i][:, kc * 128:(kc + 1) * 128], identb[:], [R_gat[i], R_c], [pBr])
                    cp(UT[:, :, rows], pB[:, :].rearrange("p (k t) -> p k t", t=128), [pBr], [R_UT[t]])
            P.barrier()
            if l > 0:
                compute_ada(l, do_cols=False)
            with Scope() as st:
                wout = sb("wout", [128, KC, D], BF16, st); R_w = Res()
                dma(wout[:], w_out_d[l].rearrange("(kc p) n -> p kc n", p=128), (), [R_w], q="pool")
                lnb = sb("lnb", [128, 2, D], F32, st); R_ln = Res()
                for i_, src in enumerate((ln1g_d, ln1b_d)):
                    dma(lnb[:, i_, :], src[l:l + 1, :].partition_broadcast(128), (), [R_ln])
                hin = [sb(f"hin{i}", [128, D], F32, st) for i in range(2)]; R_hin = [Res(), Res()]
                rt = [sb(f"rt{i}", [128, D], F32, st) for i in range(2)]; R_rt = [Res(), Res()]
                h1 = [sb(f"h1_{i}", [128, D], F32, st) for i in range(2)]; R_h1 = [Res(), Res()]
                st6 = [sb(f"st6_{i}", [128, 12], F32, st) for i in range(2)]
                mv = [sb(f"mv_{i}", [128, 4], F32, st) for i in range(2)]; R_st = [Res(), Res()]
                u2t = [sb(f"u2t{i}", [128, KC, 128], BF16, st) for i in range(2)]; R_u2t = [Res(), Res()]
                for t in range(NT):
                    i = t % 2
                    rows = slice(t * 128, (t + 1) * 128)
                    dma(hin[i][:], h_scr[rows, :], [R_h[t]], [R_hin[i]])
                    for half in range(2):
                        hs = slice(half * 512, (half + 1) * 512)
                        pt, pr = nps()
                        for kc in range(KC):
                            pe_mm(pt[:, :], UT[:, kc, rows], wout[:, kc, hs], kc == 0, kc == KC - 1, [R_UT[t], R_w], [pr])
                        tt(rt[i][:, hs], pt[:, :], gb[:, 0, hs], ALU.mult, [pr, R_ada], [R_rt[i]])
                    stt(rt[i][:], hin[i][:], ALU_ALPHA, rt[i][:], ALU.mult, ALU.add, [R_hin[i], R_rt[i]], [R_rt[i]])
                    ln_tile(rt[i], h1[i], st6[i], mv[i], R_rt[i], R_h1[i], R_st[i], gamma=lnb[:, 0, :], beta=lnb[:, 1, :], R_gb=R_ln)
                    dma(h_scr[rows, :], h1[i][:], [R_h1[i]], [R_h[t]], q="pool")
                    mod_to_T(h1[i], R_h1[i], u2t[i], 0, R_u2t[i], 1, l)
                    dma(u2_scr[:, :, rows], u2t[i][:], [R_u2t[i]], [R_u2[t]], q="pool")
            with Scope() as st:
                lnb = sb("lnb2", [128, 2, D], F32, st); R_ln = Res()
                for i_, src in enumerate((ln2g_d, ln2b_d)):
                    dma(lnb[:, i_, :], src[l:l + 1, :].partition_broadcast(128), (), [R_ln])
                hin = [sb(f"hinD{i}", [128, D], F32, st) for i in range(2)]; R_hin = [Res(), Res()]
                rt = [sb(f"rtD{i}", [128, D], F32, st) for i in range(2)]; R_rt = [Res(), Res()]
                h2 = [sb(f"h2_{i}", [128, D], F32, st) for i in range(2)]; R_h2 = [Res(), Res()]
                st6 = [sb(f"st6D{i}", [128, 12], F32, st) for i in range(2)]
                mv = [sb(f"mvD{i}", [128, 4], F32, st) for i in range(2)]; R_st = [Res(), Res()]
                u2g = [sb(f"u2g{i}", [128, KC, 256], BF16, st) for i in range(2)]; R_u2g = [Res(), Res()]
                hidT = sb("hidT", [128, 22, 256], BF16, st); R_hid = Res()
                w1s = [sb(f"w1s{i}", [128, KC, 256], BF16, st) for i in range(2)]
                w3s = [sb(f"w3s{i}", [128, KC, 256], BF16, st) for i in range(2)]
                R_ws = [Res(), Res()]
                w2h = sb("w2h", [128, 22, 512], BF16, st); R_w2 = Res()
                sa = [sb(f"sa{i}", [128, 256], F32, st) for i in range(2)]; R_sa = [Res(), Res()]
                w1v = w1b_d[l].rearrange("(kc p) n -> p kc n", p=128)
                w3v = w3b_d[l].rearrange("(kc p) n -> p kc n", p=128)
                w2v = w2b_d[l].rearrange("(f p) n -> p f n", p=128)
                kk = 0; ks = 0
                for G in range(16):
                    gi = G % 2
                    dma(u2g[gi][:], u2_scr[:, :, G * 256:(G + 1) * 256], [R_u2[2 * G], R_u2[2 * G + 1]], [R_u2g[gi]])
                    for b in range(11):
                        si = ks % 2; ks += 1
                        dma(w1s[si][:], w1v[:, :, b * 256:(b + 1) * 256], [R_ffw], [R_ws[si]])
                        dma(w3s[si][:], w3v[:, :, b * 256:(b + 1) * 256], [R_ffw], [R_ws[si]])
                        for j in range(2):
                            fb = b * 2 + j
                            pa, par = nps()
                            for kc in range(KC):
                                pe_mm(pa[:, 0:256], w1s[si][:, kc, j * 128:(j + 1) * 128], u2g[gi][:, kc, :], kc == 0, kc == KC - 1, [R_ws[si], R_u2g[gi]], [par])
                            pb, pbr = nps()
                            for kc in range(KC):
                                pe_mm(pb[:, 0:256], w3s[si][:, kc, j * 128:(j + 1) * 128], u2g[gi][:, kc, :], kc == 0, kc == KC - 1, [R_ws[si], R_u2g[gi]], [pbr])
                            ai = kk % 2; kk += 1
                            act(sa[ai][:], pa[:, 0:256], AF.Silu, [par], [R_sa[ai]])
                            tt(hidT[:, fb, :], sa[ai][:], pb[:, 0:256], ALU.mult, [R_sa[ai], pbr], [R_hid])
                    for half in range(2):
                        hs = slice(half * 512, (half + 1) * 512)
                        dma(w2h[:], w2v[:, :, hs], [R_ffw], [R_w2])
                        for ti in range(2):
                            pt, pr = nps()
                            for fb in range(22):
                                pe_mm(pt[:, :], hidT[:, fb, ti * 128:(ti + 1) * 128], w2h[:, fb, :], fb == 0, fb == 21, [R_hid, R_w2], [pr])
                            tt(rt[ti][:, hs], pt[:, :], gb[:, 1, hs], ALU.mult, [pr, R_ada], [R_rt[ti]])
                    for ti in range(2):
                        t = G * 2 + ti
                        rows = slice(t * 128, (t + 1) * 128)
                        dma(hin[ti][:], h_scr[rows, :], [R_h[t]], [R_hin[ti]])
                        stt(rt[ti][:], hin[ti][:], ALU_ALPHA, rt[ti][:], ALU.mult, ALU.add, [R_hin[ti], R_rt[ti]], [R_rt[ti]])
                        ln_tile(rt[ti], h2[ti], st6[ti], mv[ti], R_rt[ti], R_h2[ti], R_st[ti], gamma=lnb[:, 0, :], beta=lnb[:, 1, :], R_gb=R_ln)
                        if l + 1 < L:
                            dma(h_scr[rows, :], h2[ti][:], [R_h2[ti]], [R_h[t]], q="pool")
                            mod_to_T(h2[ti], R_h2[ti], UT, t * 128, R_UT[t], 0, l + 1)
                        else:
                            dma(out_d[rows, :], h2[ti][:], [R_h2[ti]], [R_h[t]], q="pool")

        if ut_dbg is not None:
            dma(ut_dbg[:, :, :], UT[:], R_UT, [Res()])

        sems = {}
        for key in P.keys():
            sems[key] = es.enter_context(nc.semaphore("s_" + "_".join(str(k) for k in key)))
        with nc.Block() as block:
            P.emit(block, sems)
    return nc


ALU_ALPHA = float(ALPHA)


def _consts():
    identf = np.eye(128, dtype=np.float32)
    identb = identf.astype(ml_dtypes.bfloat16)
    s = np.arange(128)[:, None]; t = np.arange(128)[None, :]
    tri = np.stack([(s <= t), (s >= t)], axis=1).astype(np.float32).astype(ml_dtypes.bfloat16)
    sel = np.zeros((128, 2, 128), np.float32)
    sel[127, 0, :] = 1.0
    sel[0, 1, :] = 1.0
    hh = np.arange(1, 25, dtype=np.float32)
    slopes = np.exp2(-8.0 * hh / 24).astype(np.float32).reshape(3, 8)
    wdec = np.zeros((128, 72, 128), np.float64)
    p = np.arange(128)[:, None].astype(np.float64); j = np.arange(128)[None, :].astype(np.float64)
    for g, dil in enumerate(DILS):
        for h in range(8):
            for ch in range(3):
                delta = j - p - 128.0 * (ch - 1)
                valid = np.abs(delta) <= 64
                w = np.where(valid, np.exp(-float(slopes[g, h]) * dil * np.abs(delta)), 0.0)
                wdec[:, (g * 8 + h) * 3 + ch, :] = w
    return dict(identf=identf, identb=identb, tri=tri, sel=sel, wdec=wdec.astype(np.float32).astype(ml_dtypes.bfloat16))


def make_in_maps(inputs, n_layers=DEPTH, l0=0, x_override=None):
    f = lambda a: np.ascontiguousarray(np.asarray(a, dtype=np.float32))
    Ls = slice(l0, l0 + n_layers)
    w_in = f(inputs["w_in"])[Ls]
    L = n_layers
    wg = np.zeros((L, D, 2, 36), np.float32)
    wg[:, :, 0, 0:4] = w_in[:, :, 4096:4100]; wg[:, :, 0, 32:36] = w_in[:, :, 4104:4108]
    wg[:, :, 1, 0:4] = w_in[:, :, 4100:4104]; wg[:, :, 1, 32:36] = w_in[:, :, 4108:4112]
    bgs = f(inputs["b_gates"])[Ls]
    bg = np.zeros((L, 36, 2), np.float32)
    bg[:, 0:4, 0] = bgs[:, 0:4]; bg[:, 32:36, 0] = bgs[:, 8:12]
    bg[:, 0:4, 1] = bgs[:, 4:8]; bg[:, 32:36, 1] = bgs[:, 12:16]
    cwv = f(inputs["conv_w"])[Ls]
    convw = np.ascontiguousarray(cwv.transpose(0, 2, 1).reshape(L, 16, 128, 5).transpose(0, 2, 1, 3))
    convb = np.ascontiguousarray(f(inputs["conv_b"])[Ls].reshape(L, 16, 128).transpose(0, 2, 1))
    common = dict(w_in=w_in, wg=wg, bg=bg, convw=convw, convb=convb, gn_w=f(inputs["gn_w"])[Ls],
                  w_a=f(inputs["w_a"])[Ls], w_b=f(inputs["w_b"])[Ls], w_out=f(inputs["w_out"])[Ls],
                  w_ada=f(inputs["w_ada"])[Ls], b_ada=f(inputs["b_ada"])[Ls],
                  ln1_g=f(inputs["ln1_g"])[Ls], ln1_b=f(inputs["ln1_b"])[Ls], ln2_g=f(inputs["ln2_g"])[Ls], ln2_b=f(inputs["ln2_b"])[Ls],
                  w1=f(inputs["w1"])[Ls], w3=f(inputs["w3"])[Ls], w2=f(inputs["w2"])[Ls])
    common.update(_consts())
    x = f(inputs["x"]) if x_override is None else x_override
    c = f(inputs["c"])
    maps = []
    for b in range(x.shape[0]):
        m = dict(common)
        m["x"] = np.ascontiguousarray(x[b])
        m["ccol"] = np.ascontiguousarray(c[b].reshape(KC, 128).T)
        maps.append(m)
    return maps


def kernel(**inputs):
    nc = build(DEPTH)
    maps = make_in_maps(inputs)
    res = run_bass_kernel_spmd(nc, maps, core_ids=list(range(len(maps))))
    return np.stack([np.asarray(r["out"], dtype=np.float32) for r in res.results], axis=0)
```

```python
import numpy as np
import ml_dtypes
from contextlib import ExitStack
import concourse.bass as bass
import concourse.mybir as mybir
from concourse.bass_utils import run_bass_kernel_spmd

F32, BF16 = mybir.dt.float32, mybir.dt.bfloat16
AF = mybir.ActivationFunctionType
ALU = mybir.AluOpType

S = 4096
D = 1024
NT = 32
KC = 8
DFF = 2816
NIN = 10768
DEPTH = 2
ALPHA = (2 * DEPTH) ** 0.25
EPS = 1e-5
DILS = (1, 4, 16)
NCORES = 4


class Res:
    __slots__ = ("w", "r", "name")

    def __init__(self, name=""):
        self.w = None
        self.r = {}
        self.name = name


class Prog:
    ENG = ("pe", "act", "dve", "pool", "sp")

    def __init__(self):
        self.ops = {e: [] for e in self.ENG}
        self.cnt = {e: 0 for e in self.ENG}
        self.seen = {e: {} for e in self.ENG}
        self.ndsem = {"sp": 20, "pool": 20}
        self.dcount = {q: 0 for q in self.ndsem}
        self.last = {}
        self.bar = {e: None for e in self.ENG}

    def _wait(self, eng, key, val, waits):
        if key == ("e", "pe") and eng == "pe":
            return
        if self.seen[eng].get(key, 0) >= val:
            return
        self.seen[eng][key] = val
        waits.append((key, val))

    def add(self, eng, fn, reads=(), writes=(), dma=False):
        waits = []
        if self.bar[eng] is not None:
            for key, val in self.bar[eng].items():
                self._wait(eng, key, val, waits)
            self.bar[eng] = None
        for r in reads:
            if r.w is not None:
                self._wait(eng, r.w[0], r.w[1], waits)
        for w in writes:
            if w.w is not None:
                self._wait(eng, w.w[0], w.w[1], waits)
            for key, val in w.r.items():
                self._wait(eng, key, val, waits)
        if dma:
            j = self.dcount[eng]
            self.dcount[eng] += 1
            n = self.ndsem[eng]
            idx, rnd = j % n, j // n
            key = ("d", eng, idx)
            if rnd > 0:
                self._wait(eng, key, 16 * rnd, waits)
            ev = (key, 16 * (rnd + 1))
        else:
            self.cnt[eng] += 1
            ev = (("e", eng), self.cnt[eng])
        self.last[ev[0]] = ev[1]
        self.ops[eng].append((waits, fn, ev))
        for r in reads:
            if r.r.get(ev[0], 0) < ev[1]:
                r.r[ev[0]] = ev[1]
        for w in writes:
            w.w = ev
            w.r = {}
        return ev

    def barrier(self):
        snap = dict(self.last)
        for e in self.ENG:
            self.bar[e] = dict(snap)

    def keys(self):
        ks = [("e", e) for e in self.ENG]
        for q, n in self.ndsem.items():
            ks += [("d", q, i) for i in range(n)]
        return ks

    def emit(self, block, sems):
        dec = {"pe": block.tensor, "act": block.scalar, "dve": block.vector,
               "pool": block.gpsimd, "sp": block.sync}
        final = dict(self.last)
        for eng in self.ENG:
            ops = self.ops[eng]

            def body(e, ops=ops, eng=eng):
                for waits, fn, ev in ops:
                    for key, val in waits:
                        e.wait_ge(sems[key], val)
                    ins = fn(e)
                    ins.then_inc(sems[ev[0]], 16 if ev[0][0] == "d" else 1)
                if eng == "sp":
                    for key, val in final.items():
                        e.wait_ge(sems[key], val)

            dec[eng](body)


def build(n_layers=DEPTH, debug=None):
    debug = debug or set()
    nc = bass.Bass("TRN2", target_bir_lowering=False)
    P = Prog()
    L = n_layers

    def din(name, shape, dt=F32):
        return nc.dram_tensor(name, list(shape), dt, kind="ExternalInput").ap()

    def dscr(name, shape, dt=F32):
        kind = "ExternalOutput" if name in debug else "Internal"
        return nc.dram_tensor(name, list(shape), dt, kind=kind).ap()

    x_d = din("x", [S, D])
    ccol_d = din("ccol", [128, KC])
    w_in_d = din("w_in", [L, D, NIN])
    wg_d = din("wg", [L, D, 2, 36])
    bg_d = din("bg", [L, 36, 2])
    convw_d = din("convw", [L, 128, 16, 5])
    convb_d = din("convb", [L, 128, 16])
    gnw_d = din("gn_w", [L, D])
    w_a_d = din("w_a", [L, D, D])
    w_b_d = din("w_b", [L, 512, D])
    w_out_d = din("w_out", [L, D, D])
    w_ada_d = din("w_ada", [L, D, 6 * D])
    b_ada_d = din("b_ada", [L, 6 * D])
    ln1g_d = din("ln1_g", [L, D]); ln1b_d = din("ln1_b", [L, D])
    ln2g_d = din("ln2_g", [L, D]); ln2b_d = din("ln2_b", [L, D])
    w1_d = din("w1", [L, D, DFF]); w3_d = din("w3", [L, D, DFF]); w2_d = din("w2", [L, DFF, D])
    identf_d = din("identf", [128, 128])
    identb_d = din("identb", [128, 128], BF16)
    tri_d = din("tri", [128, 2, 128], BF16)
    sel_d = din("sel", [128, 2, 128])
    wdec_d = din("wdec", [128, 72, 128], BF16)
    out_d = nc.dram_tensor("out", [S, D], F32, kind="ExternalOutput").ap()

    h_scr = dscr("h_scr", [S, D])
    mh_scr = dscr("mh_scr", [S, D], BF16)
    ao_scr = [dscr(f"ao_scr{g}", [S, 520]) for g in range(3)]
    w1b_d = dscr("w1b", [L, D, DFF], BF16)
    w3b_d = dscr("w3b", [L, D, DFF], BF16)
    w2b_d = dscr("w2b", [L, DFF, D], BF16)
    ut_dbg = dscr("ut_dbg", [128, KC, S], BF16) if "ut_dbg" in debug else None
    R_h = [Res() for _ in range(NT)]
    R_mh = [Res() for _ in range(NT)]
    R_ao = [[Res() for _ in range(NT)] for _ in range(3)]
    R_ffw = Res()
    u2_scr = dscr("u2_scr", [128, KC, S], BF16)
    R_u2 = [Res() for _ in range(NT)]

    es = ExitStack()
    with es:
        CAP = 196608
        big = es.enter_context(nc.sbuf_tensor("big", [128, CAP], mybir.dt.uint8))
        alloc = {"off": 0, "peak": 0}

        def sb(name, shape, dt=F32, stack=None):
            esz = 4 if dt == F32 else 2
            nel = 1
            for d_ in shape[1:]:
                nel *= d_
            nb = nel * esz
            off = alloc["off"]
            alloc["off"] = off + ((nb + 63) // 64) * 64
            alloc["peak"] = max(alloc["peak"], alloc["off"])
            assert alloc["off"] <= CAP, (name, alloc["off"])
            ap = big[0:shape[0], off:off + nb].bitcast(dt)
            if len(shape) == 3:
                ap = ap.rearrange("p (a b) -> p a b", b=shape[2])
            elif len(shape) == 4:
                ap = ap.rearrange("p (a b c) -> p a b c", b=shape[2], c=shape[3])
            return ap

        class Scope:
            def __enter__(self):
                self.m = alloc["off"]
                return self

            def enter_context(self, x):
                return x

            def __exit__(self, *a):
                alloc["off"] = self.m
                P.barrier()
                return False

        def pst(name, shape, dt=F32):
            return es.enter_context(nc.psum_tensor(name, list(shape), dt))

        ps = [pst(f"ps{i}", [128, 512]) for i in range(6)]
        psR = [Res() for _ in range(6)]
        psb = [pst(f"psb{i}", [128, 1024], BF16) for i in range(2)]
        psbR = [Res() for _ in range(2)]
        rr = {"ps": 0, "psb": 0}

        def nps():
            i = rr["ps"]; rr["ps"] = (i + 1) % 6
            return ps[i], psR[i]

        def npsb():
            i = rr["psb"]; rr["psb"] = (i + 1) % 2
            return psb[i], psbR[i]

        def pe_mm(out, lhsT, rhs, start, stop, reads, writes):
            P.add("pe", lambda e: e.matmul(out, lhsT=lhsT, rhs=rhs, start=start, stop=stop), reads, writes)

        def pe_tr(out, in_, ident, reads, writes):
            P.add("pe", lambda e: e.transpose(out, in_, ident), reads, writes)

        def act(out, in_, func, reads, writes, bias=None, scale=None):
            kw = {}
            if bias is not None: kw["bias"] = bias
            if scale is not None: kw["scale"] = scale
            P.add("act", lambda e: e.activation(out=out, in_=in_, func=func, **kw), reads, writes)

        def tt(out, in0, in1, op, reads, writes, eng="dve"):
            P.add(eng, lambda e: e.tensor_tensor(out=out, in0=in0, in1=in1, op=op), reads, writes)

        def ts(out, in0, s1, s2, op0, op1, reads, writes, eng="dve"):
            if op1 is None:
                P.add(eng, lambda e: e.tensor_scalar(out=out, in0=in0, scalar1=s1, scalar2=None, op0=op0), reads, writes)
            else:
                P.add(eng, lambda e: e.tensor_scalar(out=out, in0=in0, scalar1=s1, scalar2=s2, op0=op0, op1=op1), reads, writes)

        def stt(out, in0, scalar, in1, op0, op1, reads, writes):
            P.add("dve", lambda e: e.scalar_tensor_tensor(out=out, in0=in0, scalar=scalar, in1=in1, op0=op0, op1=op1), reads, writes)

        def cp(out, in_, reads, writes, eng="dve"):
            P.add(eng, lambda e: e.tensor_copy(out=out, in_=in_), reads, writes)

        def mset(ap, val, writes, eng="dve"):
            P.add(eng, lambda e: e.memset(ap, val), (), writes)

        def bnstats(out, in_, reads, writes):
            P.add("dve", lambda e: e.bn_stats(out=out, in_=in_), reads, writes)

        def bnaggr(out, in_, reads, writes):
            P.add("dve", lambda e: e.bn_aggr(out=out, in_=in_), reads, writes)

        def recip(out, in_, reads, writes):
            P.add("dve", lambda e: e.reciprocal(out=out, in_=in_), reads, writes)

        def scan(out, d0, d1, init, op0, op1, reads, writes):
            P.add("dve", lambda e: e.tensor_tensor_scan(out=out, data0=d0, data1=d1, initial=init, op0=op0, op1=op1), reads, writes)

        def dma(out, in_, reads, writes, q="sp"):
            P.add(q, lambda e: e.dma_start(out=out, in_=in_), reads, writes, dma=True)

        UT = sb("UT", [128, KC, S], BF16)
        R_UT = [Res() for _ in range(NT)]
        identf = sb("identf", [128, 128]); R_c = Res()
        identb = sb("identb", [128, 128], BF16)
        zer = sb("zer", [128, 128])
        ones1 = sb("ones1", [1, 128])
        adac = sb("adac", [128, L, 4, KC])
        gb = sb("gb", [128, 2, D])
        R_ada = Res()
        cact = sb("cact", [128, KC]); R_cact = Res()

        dma(identf[:], identf_d[:, :], (), [R_c])
        dma(identb[:], identb_d[:, :], (), [R_c])
        dma(cact[:], ccol_d[:, :], (), [R_cact])
        epsc = sb("epsc", [128, 1])
        mset(epsc[:], EPS, [R_c])
        mset(zer[:], 0.0, [R_c])
        mset(ones1[:], 1.0, [R_c])
        act(cact[:], cact[:], AF.Silu, [R_cact], [R_cact])

        for l in range(L):
            for (src, dst, rows, cols) in ((w1_d, w1b_d, D, DFF), (w3_d, w3b_d, D, DFF), (w2_d, w2b_d, DFF, D)):
                for r0 in range(0, rows, 128):
                    dma(dst[l, r0:r0 + 128, :], src[l, r0:r0 + 128, :], (), [R_ffw], q="pool")

        def compute_ada(l, do_cols=True, do_gb=True):
            with Scope() as st:
                Cb = sb("Cb", [128, KC, 128], F32, st); R_Cb = Res()
                wad = [sb(f"wad{i}", [128, KC, 512], F32, st) for i in range(2)]; R_wad = [Res(), Res()]
                brow = [sb(f"brow{i}", [1, 512], F32, st) for i in range(2)]
                tmp = sb("adatmp", [128, 512], F32, st); R_tmp = Res()
                for kc in range(KC):
                    act(Cb[:, kc, :], zer[:], AF.Identity, [R_cact, R_c], [R_Cb], bias=cact[:, kc:kc + 1], scale=0.0)
                wv = w_ada_d[l].rearrange("(kc p) n -> p kc n", p=128)
                for j in range(12):
                    i = j % 2
                    isgb = (j // 2) in (2, 5)
                    if (isgb and not do_gb) or ((not isgb) and not do_cols):
                        continue
                    dma(wad[i][:], wv[:, :, j * 512:(j + 1) * 512], (), [R_wad[i]])
                    dma(brow[i][:], b_ada_d[l:l + 1, j * 512:(j + 1) * 512], (), [R_wad[i]])
                    pt, pr = nps()
                    for kc in range(KC):
                        pe_mm(pt[:, :], Cb[:, kc, :], wad[i][:, kc, :], kc == 0, False, [R_Cb, R_wad[i]], [pr])
                    pe_mm(pt[:, :], ones1[:, :], brow[i][:, :], False, True, [R_wad[i], R_c], [pr])
                    sec, half = j // 2, j % 2
                    if sec in (2, 5):
                        cp(gb[:, 0 if sec == 2 else 1, half * 512:(half + 1) * 512], pt[:, :], [pr], [R_ada])
                    else:
                        cp(tmp[:], pt[:, :], [pr], [R_tmp])
                        si = {0: 0, 1: 1, 3: 2, 4: 3}[sec]
                        p2, p2r = nps()
                        for q in range(4):
                            pe_tr(p2[:, q * 128:(q + 1) * 128], tmp[:, q * 128:(q + 1) * 128], identf[:], [R_tmp, R_c], [p2r])
                        for q in range(4):
                            kc = half * 4 + q
                            if si in (1, 3):
                                ts(adac[:, l, si, kc:kc + 1], p2[:, q * 128:q * 128 + 1], 1.0, None, ALU.add, None, [p2r], [R_ada])
                            else:
                                cp(adac[:, l, si, kc:kc + 1], p2[:, q * 128:q * 128 + 1], [p2r], [R_ada])
            P.barrier()

        def ln_tile(src, dst, st6, mv, R_src, R_dst, R_st, gamma=None, beta=None, R_gb=None):
            bnstats(st6[:, 0:6], src[:, 0:512], [R_src], [R_st])
            bnstats(st6[:, 6:12], src[:, 512:1024], [R_src], [R_st])
            bnaggr(mv[:, 0:2], st6[:, 0:12], [R_st], [R_st])
            act(mv[:, 2:3], mv[:, 1:2], AF.Sqrt, [R_st], [R_st], bias=epsc[:, 0:1])
            recip(mv[:, 3:4], mv[:, 2:3], [R_st], [R_st])
            ts(dst[:, :], src[:, :], mv[:, 0:1], mv[:, 3:4], ALU.subtract, ALU.mult, [R_src, R_st], [R_dst])
            if gamma is not None:
                tt(dst[:, :], dst[:, :], gamma, ALU.mult, [R_dst, R_gb], [R_dst], eng="pool")
                tt(dst[:, :], dst[:, :], beta, ALU.add, [R_dst, R_gb], [R_dst], eng="pool")

        def mod_to_T(ht, R_ht, dstT, col0, R_dst, which, l):
            for half in range(2):
                pt, pr = nps()
                for q in range(4):
                    kc = half * 4 + q
                    pe_tr(pt[:, q * 128:(q + 1) * 128], ht[:, kc * 128:(kc + 1) * 128], identf[:], [R_ht, R_c], [pr])
                for q in range(4):
                    kc = half * 4 + q
                    act(dstT[:, kc, col0:col0 + 128], pt[:, q * 128:(q + 1) * 128], AF.Identity, [pr, R_ada], [R_dst],
                        bias=adac[:, l, 2 * which, kc:kc + 1], scale=adac[:, l, 2 * which + 1, kc:kc + 1])

        compute_ada(0)
        if L > 1:
            compute_ada(1, do_gb=False)
        with Scope() as st:
            xt = [sb(f"xt{i}", [128, D], F32, st) for i in range(2)]; R_xt = [Res(), Res()]
            ht = [sb(f"ht{i}", [128, D], F32, st) for i in range(2)]; R_ht = [Res(), Res()]
            st6 = [sb(f"st6{i}", [128, 12], F32, st) for i in range(2)]
            mv = [sb(f"mv{i}", [128, 4], F32, st) for i in range(2)]; R_st = [Res(), Res()]
            for t in range(NT):
                i = t % 2
                dma(xt[i][:], x_d[t * 128:(t + 1) * 128, :], (), [R_xt[i]])
                ln_tile(xt[i], ht[i], st6[i], mv[i], R_xt[i], R_ht[i], R_st[i])
                dma(h_scr[t * 128:(t + 1) * 128, :], ht[i][:], [R_ht[i]], [R_h[t]], q="pool")
                mod_to_T(ht[i], R_ht[i], UT, t * 128, R_UT[t], 0, 0)
        P.barrier()

        for l in range(L):
            if "stopA" in debug:
                break
            with Scope() as stB:
                TA = sb("TA", [128, NT, 8], F32, stB); TB = sb("TB", [128, NT, 8], F32, stB)
                TG = sb("TG", [128, NT, 8], F32, stB); GL = sb("GL", [128, NT, 8], F32, stB)
                MP = sb("MP", [128, NT, 8], F32, stB); DEC = sb("DEC", [128, NT, 8], F32, stB)
                DECn = sb("DECn", [128, NT, 8], F32, stB); WS = sb("WS", [128, NT, 8], F32, stB)
                WS2 = sb("WS2", [128, NT, 8], F32, stB); ET = sb("ET", [128, NT, 8], F32, stB)
                R_tab = Res()
                tri = sb("tri", [128, 2, 128], BF16, stB)
                gnwb = sb("gnwb", [128, D], F32, stB)
                cw = sb("cw", [128, 16, 5], F32, stB); cb = sb("cb", [128, 16], F32, stB)
                R_mc = Res()
                dma(tri[:], tri_d[:, :, :], (), [R_mc])
                dma(gnwb[:], gnw_d[l:l + 1, :].partition_broadcast(128), (), [R_mc])
                dma(cw[:], convw_d[l], (), [R_mc])
                dma(cb[:], convb_d[l], (), [R_mc])
                with Scope() as st:
                    T1 = sb("T1", [36, S], F32, st); T2 = sb("T2", [36, S], F32, st); T3 = sb("T3", [36, S], F32, st)
                    R_T1, R_T2, R_T3 = Res(), Res(), Res()
                    wgf = sb("wgf", [128, KC, 2, 36], F32, st); wgb = sb("wgb", [128, KC, 2, 36], BF16, st); R_wg = Res()
                    bgc = sb("bgc", [36, 2], F32, st)
                    sel = sb("sel", [128, 2, 128], F32, st)
                    dma(wgf[:], wg_d[l].rearrange("(kc p) a b -> p kc a b", p=128), (), [R_wg])
                    dma(bgc[:], bg_d[l], (), [R_wg])
                    dma(sel[:], sel_d[:, :, :], (), [R_wg])
                    cp(wgb[:], wgf[:], [R_wg], [R_wg])
                    mset(T3[:], 0.0, [R_T3])
                    for tg in range(8):
                        for gi, (T, RT) in enumerate(((T1, R_T1), (T2, R_T2))):
                            pt, pr = nps()
                            for kc in range(KC):
                                pe_mm(pt[0:36, :], wgb[:, kc, gi, :], UT[:, kc, tg * 512:(tg + 1) * 512], kc == 0, kc == KC - 1,
                                      [R_wg] + R_UT[tg * 4:tg * 4 + 4], [pr])
                            act(T[:, tg * 512:(tg + 1) * 512], pt[0:36, :], AF.Identity, [pr, R_wg], [RT], bias=bgc[:, gi:gi + 1])
                    act(T2[:], T2[:], AF.Exp, [R_T2], [R_T2], scale=-1.0)
                    act(T2[:], T2[:], AF.Ln, [R_T2], [R_T2], bias=1.0)
                    ts(T2[:], T2[:], -0.5, None, ALU.mult, None, [R_T2], [R_T2])
                    scan(T3[0:4, :], T2[0:4, :], T2[0:4, :], 0.0, ALU.add, ALU.add, [R_T2], [R_T3])
                    scan(T3[32:36, ::-1], T2[32:36, ::-1], T2[32:36, ::-1], 0.0, ALU.add, ALU.add, [R_T2], [R_T3])
                    tt(T1[:], T1[:], T3[:], ALU.subtract, [R_T1, R_T3], [R_T1])
                    scan(T2[0:4, :], T1[0:4, :], T1[0:4, :], -1e30, ALU.max, ALU.max, [R_T1], [R_T2])
                    scan(T2[32:36, ::-1], T1[32:36, ::-1], T1[32:36, ::-1], -1e30, ALU.max, ALU.max, [R_T1], [R_T2])
                    for (T, RT, TT) in ((T1, R_T1, TA), (T3, R_T3, TB), (T2, R_T2, TG)):
                        for c0 in range(0, NT, 8):
                            pt, pr = nps()
                            for cc in range(8):
                                c = c0 + cc
                                pe_tr(pt[:, cc * 36:(cc + 1) * 36], T[0:36, c * 128:(c + 1) * 128], identf[0:36, 0:36], [RT, R_c], [pr])
                            pv = pt[:, 0:288].rearrange("p (c k) -> p c k", k=36)
                            cp(TT[:, c0:c0 + 8, 0:4], pv[:, :, 0:4], [pr], [R_tab])
                            cp(TT[:, c0:c0 + 8, 4:8], pv[:, :, 32:36], [pr], [R_tab])
                    TGf = TG[:, :, :].rearrange("p c k -> p (c k)")
                    for d_ in range(2):
                        pt, pr = nps()
                        pe_mm(pt[:, 0:256], sel[:, d_, :], TGf, True, True, [R_tab, R_wg], [pr])
                        pv = pt[:, 0:256].rearrange("p (c k) -> p c k", k=8)
                        cp(GL[:, :, d_ * 4:(d_ + 1) * 4], pv[:, :, d_ * 4:(d_ + 1) * 4], [pr], [R_tab])
                    cp(MP[:], GL[:], [R_tab], [R_tab])
                    cp(MP[:, 1:NT, 0:4], GL[:, 0:NT - 1, 0:4], [R_tab], [R_tab])
                    cp(MP[:, 0:NT - 1, 4:8], GL[:, 1:NT, 4:8], [R_tab], [R_tab])
                    tt(DEC[:], MP[:], GL[:], ALU.subtract, [R_tab], [R_tab])
                    act(DEC[:], DEC[:], AF.Exp, [R_tab], [R_tab])
                    tt(WS[:], TA[:], GL[:], ALU.subtract, [R_tab], [R_tab])
                    act(WS[:], WS[:], AF.Exp, [R_tab], [R_tab])
                    tt(ET[:], TB[:], GL[:], ALU.add, [R_tab], [R_tab])
                    act(ET[:], ET[:], AF.Exp, [R_tab], [R_tab], scale=-1.0)
                    mset(DECn[:], 1.0, [R_tab])
                    cp(DECn[:, 0:NT - 1, 0:4], DEC[:, 1:NT, 0:4], [R_tab], [R_tab])
                    cp(DECn[:, 1:NT, 4:8], DEC[:, 0:NT - 1, 4:8], [R_tab], [R_tab])
                    tt(WS2[:], WS[:], DECn[:], ALU.mult, [R_tab], [R_tab])
                P.barrier()

                for hd in range(4):
                    if "skip_mlstm" in debug:
                        break
                    with Scope() as stH:
                        qT = sb("qT", [128, 2, S], BF16, stH); kT = sb("kT", [128, 2, S], BF16, stH)
                        V1 = sb("V1", [128, NT, 257], BF16, stH)
                        R_qT, R_kT = Res(), Res()
                        R_V1 = [Res() for _ in range(NT)]
                        wo = sb("wo", [128, KC, 256], BF16, stH)
                        R_w = Res()
                        wi = w_in_d[l].rearrange("(kc p) n -> p kc n", p=128)
                        dma(wo[:], wi[:, :, 3072 + hd * 256:3072 + hd * 256 + 256], (), [R_w], q="pool")
                        mset(V1[:, :, 256:257], 1.0, R_V1)
                        with Scope() as st:
                            wq = sb("wq", [128, KC, 256], BF16, st); wk = sb("wk", [128, KC, 256], BF16, st)
                            wv = sb("wv", [128, KC, 256], BF16, st)
                            for (wt, c0) in ((wq, hd * 256), (wk, 1024 + hd * 256), (wv, 2048 + hd * 256)):
                                dma(wt[:], wi[:, :, c0:c0 + 256], (), [R_w], q="pool")
                            X = sb("X", [128, S + 4], F32, st); ACC = sb("ACC", [128, S], F32, st)
                            R_Xq = [Res() for _ in range(4)]; R_Aq = [Res() for _ in range(4)]
                            mset(X[:, 0:2], 0.0, [R_Xq[0]]); mset(X[:, S + 2:S + 4], 0.0, [R_Xq[3]])
                            for isk, (wt, dstT, R_dst) in enumerate(((wq, qT, R_qT), (wk, kT, R_kT))):
                                for j in range(2):
                                    cc = isk * 8 + hd * 2 + j
                                    for tg in range(8):
                                        pt, pr = nps()
                                        for kc in range(KC):
                                            pe_mm(pt[:, :], wt[:, kc, j * 128:(j + 1) * 128], UT[:, kc, tg * 512:(tg + 1) * 512],
                                                  kc == 0, kc == KC - 1, [R_w] + R_UT[tg * 4:tg * 4 + 4], [pr])
                                        act(X[:, 2 + tg * 512:2 + (tg + 1) * 512], pt[:, :], AF.Copy, [pr], [R_Xq[tg // 2]])
                                    for qq in range(4):
                                        c0 = qq * 1024
                                        rx = [R_Xq[i_] for i_ in (qq - 1, qq, qq + 1) if 0 <= i_ < 4]
                                        acs = ACC[:, c0:c0 + 1024]
                                        act(acs, X[:, c0:c0 + 1024], AF.Identity, rx + [R_mc], [R_Aq[qq]],
                                            bias=cb[:, cc:cc + 1], scale=cw[:, cc, 0:1])
                                        for k in range(1, 5):
                                            stt(acs, X[:, c0 + k:c0 + k + 1024], cw[:, cc, k:k + 1], acs, ALU.mult, ALU.add,
                                                rx + [R_mc, R_Aq[qq]], [R_Aq[qq]])
                                        if isk == 0:
                                            act(acs, acs, AF.Silu, [R_Aq[qq]], [R_Aq[qq]])
                                            act(dstT[:, j, c0:c0 + 1024], acs, AF.Copy, [R_Aq[qq]], [R_dst], scale=1.0 / 16.0)
                                        else:
                                            act(dstT[:, j, c0:c0 + 1024], acs, AF.Silu, [R_Aq[qq]], [R_dst])
                            for t in range(NT):
                                pt, pr = nps()
                                for kc in range(KC):
                                    pe_mm(pt[:, 0:256], UT[:, kc, t * 128:(t + 1) * 128], wv[:, kc, :], kc == 0, kc == KC - 1,
                                          [R_w, R_UT[t]], [pr])
                                act(V1[:, t, 0:256], pt[:, 0:256], AF.Copy, [pr], [R_V1[t]])
                        P.barrier()
                        with Scope() as st:
                            H = sb("H", [128, NT, 256], F32, st); R_H = [Res() for _ in range(NT)]
                            hw = [False] * NT
                            C32 = [sb(f"C32_{d_}", [128, 2, 257], F32, st) for d_ in range(2)]
                            Cbf2 = [[sb(f"Cbf_{d_}_{p_}", [128, 2, 257], BF16, st) for p_ in range(2)] for d_ in range(2)]
                            R_C = [Res(), Res()]
                            R_Cb2 = [[Res(), Res()], [Res(), Res()]]
                            PT2 = [[sb(f"PT{d_}_{p_}", [128, 128], BF16, st) for p_ in range(2)] for d_ in range(2)]
                            R_PT2 = [[Res(), Res()], [Res(), Res()]]
                            ktok2 = [[sb(f"ktok{d_}_{p_}", [128, 256], BF16, st) for p_ in range(2)] for d_ in range(2)]
                            R_kt2 = [[Res(), Res()], [Res(), Res()]]
                            sm2 = [[sb(f"sm{d_}_{p_}", [128, 4], F32, st) for p_ in range(2)] for d_ in range(2)]
                            R_sm2 = [[Res(), Res()], [Res(), Res()]]
                            for d_ in range(2):
                                mset(C32[d_][:], 0.0, [R_C[d_]])
                                mset(Cbf2[d_][0][:], 0.0, [R_Cb2[d_][0]])
                            for step in range(NT):
                                par = step % 2
                                PT = [PT2[0][par], PT2[1][par]]; R_PT = [R_PT2[0][par], R_PT2[1][par]]
                                ktok = [ktok2[0][par], ktok2[1][par]]; R_kt = [R_kt2[0][par], R_kt2[1][par]]
                                sm = [sm2[0][par], sm2[1][par]]; R_sm = [R_sm2[0][par], R_sm2[1][par]]
                                Cbf = [Cbf2[0][par], Cbf2[1][par]]; R_Cb = [R_Cb2[0][par], R_Cb2[1][par]]
                                CbfN = [Cbf2[0][1 - par], Cbf2[1][1 - par]]; R_CbN = [R_Cb2[0][1 - par], R_Cb2[1][1 - par]]
                                for d_ in range(2):
                                    c = step if d_ == 0 else NT - 1 - step
                                    col = d_ * 4 + hd
                                    sl = slice(c * 128, (c + 1) * 128)
                                    last = step == NT - 1
                                    pS, pSr = nps()
                                    for dc in range(2):
                                        pe_mm(pS[:, 0:128], kT[:, dc, sl], qT[:, dc, sl], dc == 0, dc == 1, [R_kT, R_qT], [pSr])
                                    stt(PT[d_][:], pS[:, 0:128], WS[:, c, col:col + 1], tri[:, d_, :], ALU.mult, ALU.mult,
                                        [pSr, R_tab, R_mc], [R_PT[d_]])
                                    if not last:
                                        pK, pKr = npsb()
                                        for dc in range(2):
                                            pe_tr(pK[:, dc * 128:(dc + 1) * 128], kT[:, dc, sl], identb[:], [R_kT, R_c], [pKr])
                                        ts(ktok[d_][:], pK[:, 0:256], WS2[:, c, col:col + 1], None, ALU.mult, None, [pKr, R_tab], [R_kt[d_]])
                                    pO, pOr = nps()
                                    pe_mm(pO[:, 0:257], PT[d_][:], V1[:, c, :], True, False, [R_PT[d_], R_V1[c]], [pOr])
                                    for dc in range(2):
                                        pe_mm(pO[:, 0:257], qT[:, dc, sl], Cbf[d_][:, dc, :], False, dc == 1, [R_qT, R_Cb[d_]], [pOr])
                                    act(sm[d_][:, 0:1], pO[:, 256:257], AF.Abs, [pOr], [R_sm[d_]])
                                    tt(sm[d_][:, 1:2], sm[d_][:, 0:1], ET[:, c, col:col + 1], ALU.max, [R_sm[d_], R_tab], [R_sm[d_]])
                                    recip(sm[d_][:, 2:3], sm[d_][:, 1:2], [R_sm[d_]], [R_sm[d_]])
                                    if not hw[c]:
                                        ts(H[:, c, :], pO[:, 0:256], sm[d_][:, 2:3], None, ALU.mult, None, [pOr, R_sm[d_]], [R_H[c]])
                                        hw[c] = True
                                    else:
                                        stt(H[:, c, :], pO[:, 0:256], sm[d_][:, 2:3], H[:, c, :], ALU.mult, ALU.add, [pOr, R_sm[d_], R_H[c]], [R_H[c]])
                                    if not last:
                                        for dc in range(2):
                                            pU, pUr = nps()
                                            pe_mm(pU[:, 0:257], ktok[d_][:, dc * 128:(dc + 1) * 128], V1[:, c, :], True, True,
                                                  [R_kt[d_], R_V1[c]], [pUr])
                                            stt(C32[d_][:, dc, :], C32[d_][:, dc, :], DECn[:, c, col:col + 1], pU[:, 0:257], ALU.mult, ALU.add,
                                                [pUr, R_tab, R_C[d_]], [R_C[d_]])
                                        act(CbfN[d_][:], C32[d_][:], AF.Copy, [R_C[d_]], [R_CbN[d_]])
                            stg = sb("stg", [128, NT, 6], F32, st); mvg = sb("mvg", [128, NT, 2], F32, st)
                            rs = sb("rs", [128, NT], F32, st); R_g = Res()
                            for c in range(NT):
                                bnstats(stg[:, c, :], H[:, c, :], [R_H[c]], [R_g])
                            for c in range(NT):
                                bnaggr(mvg[:, c, :], stg[:, c, :], [R_g], [R_g])
                            act(rs[:], mvg[:, :, 1], AF.Sqrt, [R_g], [R_g], bias=epsc[:, 0:1])
                            recip(rs[:], rs[:], [R_g], [R_g])
                            sg = [sb(f"sg{i}", [128, 256], F32, st) for i in range(2)]; R_sg = [Res(), Res()]
                            mo = [sb(f"mo{i}", [128, 256], BF16, st) for i in range(2)]; R_mo = [Res(), Res()]
                            for t in range(NT):
                                i = t % 2
                                pt, pr = nps()
                                for kc in range(KC):
                                    pe_mm(pt[:, 0:256], UT[:, kc, t * 128:(t + 1) * 128], wo[:, kc, :], kc == 0, kc == KC - 1, [R_w, R_UT[t]], [pr])
                                act(sg[i][:], pt[:, 0:256], AF.Sigmoid, [pr], [R_sg[i]])
                                ts(H[:, t, :], H[:, t, :], mvg[:, t, 0:1], rs[:, t:t + 1], ALU.subtract, ALU.mult, [R_H[t], R_g], [R_H[t]])
                                tt(H[:, t, :], H[:, t, :], gnwb[:, hd * 256:(hd + 1) * 256], ALU.mult, [R_H[t], R_mc], [R_H[t]], eng="pool")
                                tt(mo[i][:], H[:, t, :], sg[i][:], ALU.mult, [R_H[t], R_sg[i]], [R_mo[i]])
                                dma(mh_scr[t * 128:(t + 1) * 128, hd * 256:(hd + 1) * 256], mo[i][:], [R_mo[i]], [R_mh[t]], q="pool")
                        P.barrier()
                P.barrier()

            with Scope() as stA:
                wdec = sb("wdec", [128, 72, 128], BF16, stA); R_wd = Res()
                dma(wdec[:], wdec_d[:, :, :], (), [R_wd])
                waq = sb("waq", [128, KC, 512], BF16, stA); wak = sb("wak", [128, KC, 512], BF16, stA)
                wav = sb("wav", [128, KC, 512], BF16, stA); R_wa = Res()
                kTa = sb("kTa", [128, 4, 768], BF16, stA); R_kTa = Res()
                qAB = sb("qAB", [128, 4, 2, 512], BF16, stA); R_q = Res()
                V1a = sb("V1a", [128, 6, 8, 65], BF16, stA); R_Va = Res()
                Eb = [sb(f"Eb{i}", [128, 4, 128], BF16, stA) for i in range(2)]; R_E = [Res(), Res()]
                PTa = [sb(f"PTa{i}", [128, 4, 128], BF16, stA) for i in range(3)]; R_PTa = [Res() for _ in range(3)]
                ost = [sb(f"ost{i}", [128, 520], F32, stA) for i in range(2)]; R_ost = [Res(), Res()]
                mset(qAB[:], 0.0, [R_q], eng="pool")
                mset(V1a[:, :, :, 64:65], 1.0, [R_Va])
                wi = w_in_d[l].rearrange("(kc p) n -> p kc n", p=128)
                ecount = 0
                ocount = 0
                for g, dil in enumerate(DILS):
                    if "skip_attn" in debug:
                        break
                    base = 4112 + g * 512
                    for (wt, c0) in ((waq, base), (wak, base + 1536), (wav, base + 3072)):
                        dma(wt[:], wi[:, :, c0:c0 + 512], (), [R_wa], q="pool")
                    n = S // dil
                    NJ = n // 128
                    for r in range(dil):
                        for J0 in range(0, NJ, 4):
                            J1 = min(J0 + 4, NJ)
                            a0 = max(J0 - 1, 0); a1 = min(J1, NJ - 1)
                            nkt = a1 - a0 + 1

                            def cols(a_start, ntile):
                                b0 = (128 * a_start) * dil + r
                                return slice(b0, b0 + (128 * ntile - 1) * dil + 1, dil) if dil > 1 else slice(b0, b0 + 128 * ntile)
                            for pair in range(4):
                                for t0 in range(0, nkt, 4):
                                    nt_ = min(4, nkt - t0)
                                    pt, pr = nps()
                                    for kc in range(KC):
                                        pe_mm(pt[:, 0:128 * nt_], wak[:, kc, pair * 128:(pair + 1) * 128], UT[:, kc, cols(a0 + t0, nt_)],
                                              kc == 0, kc == KC - 1, [R_wa] + R_UT, [pr])
                                    cp(kTa[:, pair, t0 * 128:(t0 + nt_) * 128], pt[:, 0:128 * nt_], [pr], [R_kTa])
                            nq = J1 - J0
                            for pair in range(4):
                                pt, pr = nps()
                                for kc in range(KC):
                                    pe_mm(pt[:, 0:128 * nq], waq[:, kc, pair * 128:(pair + 1) * 128], UT[:, kc, cols(J0, nq)],
                                          kc == 0, kc == KC - 1, [R_wa] + R_UT, [pr])
                                act(qAB[0:64, pair, 0, 0:128 * nq], pt[0:64, 0:128 * nq], AF.Copy, [pr], [R_q], scale=0.125)
                                act(qAB[64:128, pair, 1, 0:128 * nq], pt[64:128, 0:128 * nq], AF.Copy, [pr], [R_q], scale=0.125)
                            for ai in range(nkt):
                                pt, pr = nps()
                                for kc in range(KC):
                                    pe_mm(pt[:, :], UT[:, kc, cols(a0 + ai, 1)], wav[:, kc, :], kc == 0, kc == KC - 1, [R_wa] + R_UT, [pr])
                                cp(V1a[:, ai, :, 0:64], pt[:, :].rearrange("p (h d) -> p h d", d=64), [pr], [R_Va])
                            for J in range(J0, J1):
                                jl = J - J0
                                oi = ocount % 2; ocount += 1
                                for hg in range(2):
                                    chs = [ch for ch in range(3) if 0 <= J - 1 + ch < NJ]
                                    for ch in chs:
                                        ai = J - 1 + ch - a0
                                        pS, pSr = nps()
                                        for hh in range(4):
                                            h = hg * 4 + hh
                                            pe_mm(pS[:, hh * 128:(hh + 1) * 128], kTa[:, h // 2, ai * 128:(ai + 1) * 128],
                                                  qAB[:, h // 2, h % 2, jl * 128:(jl + 1) * 128], True, True, [R_kTa, R_q], [pSr])
                                        ei = ecount % 2; ecount += 1
                                        act(Eb[ei][:], pS[:, :].rearrange("p (h q) -> p h q", q=128), AF.Exp, [pSr], [R_E[ei]])
                                        wb0 = (g * 8 + hg * 4) * 3 + ch
                                        tt(PTa[ch][:], Eb[ei][:], wdec[:, wb0:wb0 + 10:3, :], ALU.mult, [R_E[ei], R_wd], [R_PTa[ch]], eng="pool")
                                    pO, pOr = nps()
                                    for hh in range(4):
                                        h = hg * 4 + hh
                                        for ci, ch in enumerate(chs):
                                            ai = J - 1 + ch - a0
                                            pe_mm(pO[:, hh * 65:(hh + 1) * 65], PTa[ch][:, hh, :], V1a[:, ai, h, :], ci == 0, ci == len(chs) - 1,
                                                  [R_PTa[ch], R_Va], [pOr])
                                    cp(ost[oi][:, hg * 260:(hg + 1) * 260], pO[:, 0:260], [pOr], [R_ost[oi]])
                                rb = (128 * J) * dil + r
                                rows = slice(rb, rb + 127 * dil + 1, dil) if dil > 1 else slice(rb, rb + 128)
                                tl0 = rb // 128; tl1 = (rb + 127 * dil) // 128
                                dma(ao_scr[g][rows, :], ost[oi][:], [R_ost[oi]], [R_ao[g][t] for t in range(tl0, tl1 + 1)], q="sp")
            P.barrier()
            if "stopB" in debug:
                break

            with Scope() as st:
                wa = sb("wa", [128, KC, D], BF16, st); wb = sb("wb", [128, 4, D], BF16, st)
                wgt = sb("wgt", [128, KC, 2048], BF16, st); R_w = Res()
                wi = w_in_d[l].rearrange("(kc p) n -> p kc n", p=128)
                dma(wa[:], w_a_d[l].rearrange("(kc p) n -> p kc n", p=128), (), [R_w], q="pool")
                dma(wb[:], w_b_d[l].rearrange("(kc p) n -> p kc n", p=128), (), [R_w], q="pool")
                for q4 in range(4):
                    dma(wgt[:, :, q4 * 512:(q4 + 1) * 512], wi[:, :, 8720 + q4 * 512:8720 + (q4 + 1) * 512], (), [R_w], q="pool")
                mht = [sb(f"mht{i}", [128, D], BF16, st) for i in range(2)]; R_mht = [Res(), Res()]
                aot = [[sb(f"aot{i}_{g}", [128, 520], F32, st) for g in range(3)] for i in range(2)]; R_aot = [Res(), Res()]
                rden = [sb(f"rden{i}", [128, 8], F32, st) for i in range(2)]
                aon = [sb(f"aon{i}", [128, 512], BF16, st) for i in range(2)]; R_aon = [Res(), Res()]
                mhT = [sb(f"mhT{i}", [128, KC, 128], BF16, st) for i in range(2)]; R_mhT = [Res(), Res()]
                aoT = [sb(f"aoT{i}", [128, 4, 128], BF16, st) for i in range(2)]; R_aoT = [Res(), Res()]
                sga = [sb(f"sga{i}", [128, 512], F32, st) for i in range(2)]; sgb = [sb(f"sgb{i}", [128, 512], F32, st) for i in range(2)]
                R_sgab = [Res(), Res()]
                gat = [sb(f"gat{i}", [128, D], BF16, st) for i in range(2)]; R_gat = [Res(), Res()]
                k2 = 0
                for t in range(NT):
                    i = t % 2
                    rows = slice(t * 128, (t + 1) * 128)
                    dma(mht[i][:], mh_scr[rows, :], [R_mh[t]], [R_mht[i]])
                    for g in range(3):
                        dma(aot[i][g][:], ao_scr[g][rows, :], [R_ao[g][t]], [R_aot[i]])
                    tt(aot[i][0][:], aot[i][0][:], aot[i][1][:], ALU.add, [R_aot[i]], [R_aot[i]], eng="pool")
                    tt(aot[i][0][:], aot[i][0][:], aot[i][2][:], ALU.add, [R_aot[i]], [R_aot[i]], eng="pool")
                    av = aot[i][0][:, :].rearrange("p (h d) -> p h d", d=65)
                    recip(rden[i][:, :], av[:, :, 64], [R_aot[i]], [R_aot[i]])
                    for h in range(8):
                        ts(aon[i][:, h * 64:(h + 1) * 64], av[:, h, 0:64], rden[i][:, h:h + 1], None, ALU.mult, None, [R_aot[i]], [R_aon[i]])
                    pB, pBr = npsb()
                    for kc in range(KC):
                        pe_tr(pB[:, kc * 128:(kc + 1) * 128], mht[i][:, kc * 128:(kc + 1) * 128], identb[:], [R_mht[i], R_c], [pBr])
                    cp(mhT[i][:], pB[:, :].rearrange("p (k t) -> p k t", t=128), [pBr], [R_mhT[i]])
                    pB, pBr = npsb()
                    for kc in range(4):
                        pe_tr(pB[:, kc * 128:(kc + 1) * 128], aon[i][:, kc * 128:(kc + 1) * 128], identb[:], [R_aon[i], R_c], [pBr])
                    cp(aoT[i][:], pB[:, 0:512].rearrange("p (k t) -> p k t", t=128), [pBr], [R_aoT[i]])
                    for half in range(2):
                        hs = slice(half * 512, (half + 1) * 512)
                        j2 = k2 % 2; k2 += 1
                        pga, pgar = nps()
                        for kc in range(KC):
                            pe_mm(pga[:, :], UT[:, kc, rows], wgt[:, kc, half * 512:(half + 1) * 512], kc == 0, kc == KC - 1, [R_UT[t], R_w], [pgar])
                        pgb, pgbr = nps()
                        for kc in range(KC):
                            pe_mm(pgb[:, :], UT[:, kc, rows], wgt[:, kc, 1024 + half * 512:1024 + (half + 1) * 512], kc == 0, kc == KC - 1, [R_UT[t], R_w], [pgbr])
                        act(sga[j2][:], pga[:, :], AF.Sigmoid, [pgar], [R_sgab[j2]])
                        act(sgb[j2][:], pgb[:, :], AF.Sigmoid, [pgbr], [R_sgab[j2]])
                        pya, pyar = nps()
                        for kc in range(KC):
                            pe_mm(pya[:, :], mhT[i][:, kc, :], wa[:, kc, hs], kc == 0, kc == KC - 1, [R_mhT[i], R_w], [pyar])
                        pyb, pybr = nps()
                        for kc in range(4):
                            pe_mm(pyb[:, :], aoT[i][:, kc, :], wb[:, kc, hs], kc == 0, kc == 3, [R_aoT[i], R_w], [pybr])
                        tt(sga[j2][:], sga[j2][:], pya[:, :], ALU.mult, [R_sgab[j2], pyar], [R_sgab[j2]])
                        tt(sgb[j2][:], sgb[j2][:], pyb[:, :], ALU.mult, [R_sgab[j2], pybr], [R_sgab[j2]])
                        tt(gat[i][:, hs], sga[j2][:], sgb[j2][:], ALU.add, [R_sgab[j2]], [R_gat[i]], eng="pool")
                    pB, pBr = npsb()
                    for kc in range(KC):
                        pe_tr(pB[:, kc * 128:(kc + 1) * 128], gat[i][:, kc * 128:(kc + 1) * 128], identb[:], [R_gat[i], R_c], [pBr])
                    cp(UT[:, :, rows], pB[:, :].rearrange("p (k t) -> p k t", t=128), [pBr], [R_UT[t]])
            P.barrier()
            if l > 0:
                compute_ada(l, do_cols=False)
            with Scope() as st:
                wout = sb("wout", [128, KC, D], BF16, st); R_w = Res()
                dma(wout[:], w_out_d[l].rearrange("(kc p) n -> p kc n", p=128), (), [R_w], q="pool")
                lnb = sb("lnb", [128, 2, D], F32, st); R_ln = Res()
                for i_, src in enumerate((ln1g_d, ln1b_d)):
                    dma(lnb[:, i_, :], src[l:l + 1, :].partition_broadcast(128), (), [R_ln])
                hin = [sb(f"hin{i}", [128, D], F32, st) for i in range(2)]; R_hin = [Res(), Res()]
                rt = [sb(f"rt{i}", [128, D], F32, st) for i in range(2)]; R_rt = [Res(), Res()]
                h1 = [sb(f"h1_{i}", [128, D], F32, st) for i in range(2)]; R_h1 = [Res(), Res()]
                st6 = [sb(f"st6_{i}", [128, 12], F32, st) for i in range(2)]
                mv = [sb(f"mv_{i}", [128, 4], F32, st) for i in range(2)]; R_st = [Res(), Res()]
                u2t = [sb(f"u2t{i}", [128, KC, 128], BF16, st) for i in range(2)]; R_u2t = [Res(), Res()]
                for t in range(NT):
                    i = t % 2
                    rows = slice(t * 128, (t + 1) * 128)
                    dma(hin[i][:], h_scr[rows, :], [R_h[t]], [R_hin[i]])
                    for half in range(2):
                        hs = slice(half * 512, (half + 1) * 512)
                        pt, pr = nps()
                        for kc in range(KC):
                            pe_mm(pt[:, :], UT[:, kc, rows], wout[:, kc, hs], kc == 0, kc == KC - 1, [R_UT[t], R_w], [pr])
                        tt(rt[i][:, hs], pt[:, :], gb[:, 0, hs], ALU.mult, [pr, R_ada], [R_rt[i]])
                    stt(rt[i][:], hin[i][:], ALU_ALPHA, rt[i][:], ALU.mult, ALU.add, [R_hin[i], R_rt[i]], [R_rt[i]])
                    ln_tile(rt[i], h1[i], st6[i], mv[i], R_rt[i], R_h1[i], R_st[i], gamma=lnb[:, 0, :], beta=lnb[:, 1, :], R_gb=R_ln)
                    dma(h_scr[rows, :], h1[i][:], [R_h1[i]], [R_h[t]], q="pool")
                    mod_to_T(h1[i], R_h1[i], u2t[i], 0, R_u2t[i], 1, l)
                    dma(u2_scr[:, :, rows], u2t[i][:], [R_u2t[i]], [R_u2[t]], q="pool")
            with Scope() as st:
                lnb = sb("lnb2", [128, 2, D], F32, st); R_ln = Res()
                for i_, src in enumerate((ln2g_d, ln2b_d)):
                    dma(lnb[:, i_, :], src[l:l + 1, :].partition_broadcast(128), (), [R_ln])
                hin = [sb(f"hinD{i}", [128, D], F32, st) for i in range(2)]; R_hin = [Res(), Res()]
                rt = [sb(f"rtD{i}", [128, D], F32, st) for i in range(4)]; R_rt = [Res() for _ in range(4)]
                st6 = [sb(f"st6D{i}", [128, 12], F32, st) for i in range(2)]
                mv = [sb(f"mvD{i}", [128, 4], F32, st) for i in range(2)]; R_st = [Res(), Res()]
                u2g = sb("u2g", [128, KC, 512], BF16, st); R_u2g = Res()
                hidT = sb("hidT", [128, 22, 512], BF16, st); R_hid = Res()
                w1s = [sb(f"w1s{i}", [128, KC, 256], BF16, st) for i in range(2)]
                w3s = [sb(f"w3s{i}", [128, KC, 256], BF16, st) for i in range(2)]
                R_ws = [Res(), Res()]
                w2h = sb("w2h", [128, 22, 512], BF16, st); R_w2 = Res()
                sa = [sb(f"sa{i}", [128, 512], F32, st) for i in range(2)]; R_sa = [Res(), Res()]
                w1v = w1b_d[l].rearrange("(kc p) n -> p kc n", p=128)
                w3v = w3b_d[l].rearrange("(kc p) n -> p kc n", p=128)
                w2v = w2b_d[l].rearrange("(f p) n -> p f n", p=128)
                kk = 0; ks = 0; kh = 0
                for G in range(8):
                    dma(u2g[:], u2_scr[:, :, G * 512:(G + 1) * 512], [R_u2[4 * G + i_] for i_ in range(4)], [R_u2g])
                    for b in range(11):
                        si = ks % 2; ks += 1
                        dma(w1s[si][:], w1v[:, :, b * 256:(b + 1) * 256], [R_ffw], [R_ws[si]])
                        dma(w3s[si][:], w3v[:, :, b * 256:(b + 1) * 256], [R_ffw], [R_ws[si]])
                        for j in range(2):
                            fb = b * 2 + j
                            pa, par = nps()
                            for kc in range(KC):
                                pe_mm(pa[:, :], w1s[si][:, kc, j * 128:(j + 1) * 128], u2g[:, kc, :], kc == 0, kc == KC - 1, [R_ws[si], R_u2g], [par])
                            pb, pbr = nps()
                            for kc in range(KC):
                                pe_mm(pb[:, :], w3s[si][:, kc, j * 128:(j + 1) * 128], u2g[:, kc, :], kc == 0, kc == KC - 1, [R_ws[si], R_u2g], [pbr])
                            ai = kk % 2; kk += 1
                            act(sa[ai][:], pa[:, :], AF.Silu, [par], [R_sa[ai]])
                            tt(hidT[:, fb, :], sa[ai][:], pb[:, :], ALU.mult, [R_sa[ai], pbr], [R_hid])
                    for half in range(2):
                        hs = slice(half * 512, (half + 1) * 512)
                        dma(w2h[:], w2v[:, :, hs], [R_ffw], [R_w2])
                        for ti in range(4):
                            pt, pr = nps()
                            for fb in range(22):
                                pe_mm(pt[:, :], hidT[:, fb, ti * 128:(ti + 1) * 128], w2h[:, fb, :], fb == 0, fb == 21, [R_hid, R_w2], [pr])
                            tt(rt[ti][:, hs], pt[:, :], gb[:, 1, hs], ALU.mult, [pr, R_ada], [R_rt[ti]])
                    for ti in range(4):
                        t = G * 4 + ti
                        rows = slice(t * 128, (t + 1) * 128)
                        hi = kh % 2; kh += 1
                        dma(hin[hi][:], h_scr[rows, :], [R_h[t]], [R_hin[hi]])
                        stt(rt[ti][:], hin[hi][:], ALU_ALPHA, rt[ti][:], ALU.mult, ALU.add, [R_hin[hi], R_rt[ti]], [R_rt[ti]])
                        ln_tile(rt[ti], rt[ti], st6[hi], mv[hi], R_rt[ti], R_rt[ti], R_st[hi], gamma=lnb[:, 0, :], beta=lnb[:, 1, :], R_gb=R_ln)
                        if l + 1 < L:
                            dma(h_scr[rows, :], rt[ti][:], [R_rt[ti]], [R_h[t]], q="pool")
                            mod_to_T(rt[ti], R_rt[ti], UT, t * 128, R_UT[t], 0, l + 1)
                        else:
                            dma(out_d[rows, :], rt[ti][:], [R_rt[ti]], [R_h[t]], q="pool")

        if ut_dbg is not None:
            dma(ut_dbg[:, :, :], UT[:], R_UT, [Res()])

        sems = {}
        for key in P.keys():
            sems[key] = es.enter_context(nc.semaphore("s_" + "_".join(str(k) for k in key)))
        with nc.Block() as block:
            P.emit(block, sems)
    return nc


ALU_ALPHA = float(ALPHA)


def _consts():
    identf = np.eye(128, dtype=np.float32)
    identb = identf.astype(ml_dtypes.bfloat16)
    s = np.arange(128)[:, None]; t = np.arange(128)[None, :]
    tri = np.stack([(s <= t), (s >= t)], axis=1).astype(np.float32).astype(ml_dtypes.bfloat16)
    sel = np.zeros((128, 2, 128), np.float32)
    sel[127, 0, :] = 1.0
    sel[0, 1, :] = 1.0
    hh = np.arange(1, 25, dtype=np.float32)
    slopes = np.exp2(-8.0 * hh / 24).astype(np.float32).reshape(3, 8)
    wdec = np.zeros((128, 72, 128), np.float64)
    p = np.arange(128)[:, None].astype(np.float64); j = np.arange(128)[None, :].astype(np.float64)
    for g, dil in enumerate(DILS):
        for h in range(8):
            for ch in range(3):
                delta = j - p - 128.0 * (ch - 1)
                valid = np.abs(delta) <= 64
                w = np.where(valid, np.exp(-float(slopes[g, h]) * dil * np.abs(delta)), 0.0)
                wdec[:, (g * 8 + h) * 3 + ch, :] = w
    return dict(identf=identf, identb=identb, tri=tri, sel=sel, wdec=wdec.astype(np.float32).astype(ml_dtypes.bfloat16))


def make_in_maps(inputs, n_layers=DEPTH, l0=0, x_override=None):
    f = lambda a: np.ascontiguousarray(np.asarray(a, dtype=np.float32))
    Ls = slice(l0, l0 + n_layers)
    w_in = f(inputs["w_in"])[Ls]
    L = n_layers
    wg = np.zeros((L, D, 2, 36), np.float32)
    wg[:, :, 0, 0:4] = w_in[:, :, 4096:4100]; wg[:, :, 0, 32:36] = w_in[:, :, 4104:4108]
    wg[:, :, 1, 0:4] = w_in[:, :, 4100:4104]; wg[:, :, 1, 32:36] = w_in[:, :, 4108:4112]
    bgs = f(inputs["b_gates"])[Ls]
    bg = np.zeros((L, 36, 2), np.float32)
    bg[:, 0:4, 0] = bgs[:, 0:4]; bg[:, 32:36, 0] = bgs[:, 8:12]
    bg[:, 0:4, 1] = bgs[:, 4:8]; bg[:, 32:36, 1] = bgs[:, 12:16]
    cwv = f(inputs["conv_w"])[Ls]
    convw = np.ascontiguousarray(cwv.transpose(0, 2, 1).reshape(L, 16, 128, 5).transpose(0, 2, 1, 3))
    convb = np.ascontiguousarray(f(inputs["conv_b"])[Ls].reshape(L, 16, 128).transpose(0, 2, 1))
    common = dict(w_in=w_in, wg=wg, bg=bg, convw=convw, convb=convb, gn_w=f(inputs["gn_w"])[Ls],
                  w_a=f(inputs["w_a"])[Ls], w_b=f(inputs["w_b"])[Ls], w_out=f(inputs["w_out"])[Ls],
                  w_ada=f(inputs["w_ada"])[Ls], b_ada=f(inputs["b_ada"])[Ls],
                  ln1_g=f(inputs["ln1_g"])[Ls], ln1_b=f(inputs["ln1_b"])[Ls], ln2_g=f(inputs["ln2_g"])[Ls], ln2_b=f(inputs["ln2_b"])[Ls],
                  w1=f(inputs["w1"])[Ls], w3=f(inputs["w3"])[Ls], w2=f(inputs["w2"])[Ls])
    common.update(_consts())
    x = f(inputs["x"]) if x_override is None else x_override
    c = f(inputs["c"])
    maps = []
    for b in range(x.shape[0]):
        m = dict(common)
        m["x"] = np.ascontiguousarray(x[b])
        m["ccol"] = np.ascontiguousarray(c[b].reshape(KC, 128).T)
        maps.append(m)
    return maps


def kernel(**inputs):
    nc = build(DEPTH)
    maps = make_in_maps(inputs)
    res = run_bass_kernel_spmd(nc, maps, core_ids=list(range(len(maps))))
    return np.stack([np.asarray(r["out"], dtype=np.float32) for r in res.results], axis=0)
```

```python
import numpy as np
import ml_dtypes
from contextlib import ExitStack
import concourse.bass as bass
import concourse.mybir as mybir
from concourse.bass_utils import run_bass_kernel_spmd

F32, BF16 = mybir.dt.float32, mybir.dt.bfloat16
AF = mybir.ActivationFunctionType
ALU = mybir.AluOpType

S = 4096
D = 1024
NT = 32
KC = 8
DFF = 2816
NIN = 10768
DEPTH = 2
ALPHA = (2 * DEPTH) ** 0.25
EPS = 1e-5
DILS = (1, 4, 16)
NCORES = 4


class Res:
    __slots__ = ("w", "r", "name")

    def __init__(self, name=""):
        self.w = None
        self.r = {}
        self.name = name


class Prog:
    ENG = ("pe", "act", "dve", "pool", "sp")

    def __init__(self):
        self.ops = {e: [] for e in self.ENG}
        self.cnt = {e: 0 for e in self.ENG}
        self.seen = {e: {} for e in self.ENG}
        self.ndsem = {"sp": 20, "pool": 20}
        self.dcount = {q: 0 for q in self.ndsem}
        self.last = {}
        self.bar = {e: None for e in self.ENG}

    def _wait(self, eng, key, val, waits):
        if key == ("e", "pe") and eng == "pe":
            return
        if self.seen[eng].get(key, 0) >= val:
            return
        self.seen[eng][key] = val
        waits.append((key, val))

    def add(self, eng, fn, reads=(), writes=(), dma=False):
        waits = []
        if self.bar[eng] is not None:
            for key, val in self.bar[eng].items():
                self._wait(eng, key, val, waits)
            self.bar[eng] = None
        for r in reads:
            if r.w is not None:
                self._wait(eng, r.w[0], r.w[1], waits)
        for w in writes:
            if w.w is not None:
                self._wait(eng, w.w[0], w.w[1], waits)
            for key, val in w.r.items():
                self._wait(eng, key, val, waits)
        if dma:
            j = self.dcount[eng]
            self.dcount[eng] += 1
            n = self.ndsem[eng]
            idx, rnd = j % n, j // n
            key = ("d", eng, idx)
            if rnd > 0:
                self._wait(eng, key, 16 * rnd, waits)
            ev = (key, 16 * (rnd + 1))
        else:
            self.cnt[eng] += 1
            ev = (("e", eng), self.cnt[eng])
        self.last[ev[0]] = ev[1]
        self.ops[eng].append((waits, fn, ev))
        for r in reads:
            if r.r.get(ev[0], 0) < ev[1]:
                r.r[ev[0]] = ev[1]
        for w in writes:
            w.w = ev
            w.r = {}
        return ev

    def barrier(self):
        snap = dict(self.last)
        for e in self.ENG:
            self.bar[e] = dict(snap)

    def keys(self):
        ks = [("e", e) for e in self.ENG]
        for q, n in self.ndsem.items():
            ks += [("d", q, i) for i in range(n)]
        return ks

    def emit(self, block, sems):
        dec = {"pe": block.tensor, "act": block.scalar, "dve": block.vector,
               "pool": block.gpsimd, "sp": block.sync}
        final = dict(self.last)
        for eng in self.ENG:
            ops = self.ops[eng]

            def body(e, ops=ops, eng=eng):
                for waits, fn, ev in ops:
                    for key, val in waits:
                        e.wait_ge(sems[key], val)
                    ins = fn(e)
                    ins.then_inc(sems[ev[0]], 16 if ev[0][0] == "d" else 1)
                if eng == "sp":
                    for key, val in final.items():
                        e.wait_ge(sems[key], val)

            dec[eng](body)


def build(n_layers=DEPTH, debug=None):
    debug = debug or set()
    nc = bass.Bass("TRN2", target_bir_lowering=False)
    P = Prog()
    L = n_layers

    def din(name, shape, dt=F32):
        return nc.dram_tensor(name, list(shape), dt, kind="ExternalInput").ap()

    def dscr(name, shape, dt=F32):
        kind = "ExternalOutput" if name in debug else "Internal"
        return nc.dram_tensor(name, list(shape), dt, kind=kind).ap()

    x_d = din("x", [S, D])
    ccol_d = din("ccol", [128, KC])
    w_in_d = din("w_in", [L, D, NIN])
    wg_d = din("wg", [L, D, 2, 36])
    bg_d = din("bg", [L, 36, 2])
    convw_d = din("convw", [L, 128, 16, 5])
    convb_d = din("convb", [L, 128, 16])
    gnw_d = din("gn_w", [L, D])
    w_a_d = din("w_a", [L, D, D])
    w_b_d = din("w_b", [L, 512, D])
    w_out_d = din("w_out", [L, D, D])
    w_ada_d = din("w_ada", [L, D, 6 * D])
    b_ada_d = din("b_ada", [L, 6 * D])
    ln1g_d = din("ln1_g", [L, D]); ln1b_d = din("ln1_b", [L, D])
    ln2g_d = din("ln2_g", [L, D]); ln2b_d = din("ln2_b", [L, D])
    w1_d = din("w1", [L, D, DFF]); w3_d = din("w3", [L, D, DFF]); w2_d = din("w2", [L, DFF, D])
    identf_d = din("identf", [128, 128])
    identb_d = din("identb", [128, 128], BF16)
    tri_d = din("tri", [128, 2, 128], BF16)
    sel_d = din("sel", [128, 2, 128])
    wdec_d = din("wdec", [128, 72, 128], BF16)
    out_d = nc.dram_tensor("out", [S, D], F32, kind="ExternalOutput").ap()

    h_scr = dscr("h_scr", [S, D])
    mh_scr = dscr("mh_scr", [S, D], BF16)
    ao_scr = [dscr(f"ao_scr{g}", [S, 520]) for g in range(3)]
    w1b_d = dscr("w1b", [L, D, DFF], BF16)
    w3b_d = dscr("w3b", [L, D, DFF], BF16)
    w2b_d = dscr("w2b", [L, DFF, D], BF16)
    ut_dbg = dscr("ut_dbg", [128, KC, S], BF16) if "ut_dbg" in debug else None
    R_h = [Res() for _ in range(NT)]
    R_mh = [Res() for _ in range(NT)]
    R_ao = [[Res() for _ in range(NT)] for _ in range(3)]
    R_ffw = Res()
    u2_scr = dscr("u2_scr", [128, KC, S], BF16)
    R_u2 = [Res() for _ in range(NT)]

    es = ExitStack()
    with es:
        CAP = 196608
        big = es.enter_context(nc.sbuf_tensor("big", [128, CAP], mybir.dt.uint8))
        alloc = {"off": 0, "peak": 0}

        def sb(name, shape, dt=F32, stack=None):
            esz = 4 if dt == F32 else 2
            nel = 1
            for d_ in shape[1:]:
                nel *= d_
            nb = nel * esz
            off = alloc["off"]
            alloc["off"] = off + ((nb + 63) // 64) * 64
            alloc["peak"] = max(alloc["peak"], alloc["off"])
            assert alloc["off"] <= CAP, (name, alloc["off"])
            ap = big[0:shape[0], off:off + nb].bitcast(dt)
            if len(shape) == 3:
                ap = ap.rearrange("p (a b) -> p a b", b=shape[2])
            elif len(shape) == 4:
                ap = ap.rearrange("p (a b c) -> p a b c", b=shape[2], c=shape[3])
            return ap

        class Scope:
            def __enter__(self):
                self.m = alloc["off"]
                return self

            def enter_context(self, x):
                return x

            def __exit__(self, *a):
                alloc["off"] = self.m
                P.barrier()
                return False

        def pst(name, shape, dt=F32):
            return es.enter_context(nc.psum_tensor(name, list(shape), dt))

        ps = [pst(f"ps{i}", [128, 512]) for i in range(6)]
        psR = [Res() for _ in range(6)]
        psb = [pst(f"psb{i}", [128, 1024], BF16) for i in range(2)]
        psbR = [Res() for _ in range(2)]
        rr = {"ps": 0, "psb": 0}

        def nps():
            i = rr["ps"]; rr["ps"] = (i + 1) % 6
            return ps[i], psR[i]

        def npsb():
            i = rr["psb"]; rr["psb"] = (i + 1) % 2
            return psb[i], psbR[i]

        def pe_mm(out, lhsT, rhs, start, stop, reads, writes):
            P.add("pe", lambda e: e.matmul(out, lhsT=lhsT, rhs=rhs, start=start, stop=stop), reads, writes)

        def pe_tr(out, in_, ident, reads, writes):
            P.add("pe", lambda e: e.transpose(out, in_, ident), reads, writes)

        def act(out, in_, func, reads, writes, bias=None, scale=None):
            kw = {}
            if bias is not None: kw["bias"] = bias
            if scale is not None: kw["scale"] = scale
            P.add("act", lambda e: e.activation(out=out, in_=in_, func=func, **kw), reads, writes)

        def tt(out, in0, in1, op, reads, writes, eng="dve"):
            P.add(eng, lambda e: e.tensor_tensor(out=out, in0=in0, in1=in1, op=op), reads, writes)

        def ts(out, in0, s1, s2, op0, op1, reads, writes, eng="dve"):
            if op1 is None:
                P.add(eng, lambda e: e.tensor_scalar(out=out, in0=in0, scalar1=s1, scalar2=None, op0=op0), reads, writes)
            else:
                P.add(eng, lambda e: e.tensor_scalar(out=out, in0=in0, scalar1=s1, scalar2=s2, op0=op0, op1=op1), reads, writes)

        def stt(out, in0, scalar, in1, op0, op1, reads, writes):
            P.add("dve", lambda e: e.scalar_tensor_tensor(out=out, in0=in0, scalar=scalar, in1=in1, op0=op0, op1=op1), reads, writes)

        def cp(out, in_, reads, writes, eng="dve"):
            P.add(eng, lambda e: e.tensor_copy(out=out, in_=in_), reads, writes)

        def mset(ap, val, writes, eng="dve"):
            P.add(eng, lambda e: e.memset(ap, val), (), writes)

        def bnstats(out, in_, reads, writes):
            P.add("dve", lambda e: e.bn_stats(out=out, in_=in_), reads, writes)

        def bnaggr(out, in_, reads, writes):
            P.add("dve", lambda e: e.bn_aggr(out=out, in_=in_), reads, writes)

        def recip(out, in_, reads, writes):
            P.add("dve", lambda e: e.reciprocal(out=out, in_=in_), reads, writes)

        def scan(out, d0, d1, init, op0, op1, reads, writes):
            P.add("dve", lambda e: e.tensor_tensor_scan(out=out, data0=d0, data1=d1, initial=init, op0=op0, op1=op1), reads, writes)

        def dma(out, in_, reads, writes, q="sp"):
            P.add(q, lambda e: e.dma_start(out=out, in_=in_), reads, writes, dma=True)

        UT = sb("UT", [128, KC, S], BF16)
        R_UT = [Res() for _ in range(NT)]
        identf = sb("identf", [128, 128]); R_c = Res()
        identb = sb("identb", [128, 128], BF16)
        zer = sb("zer", [128, 128])
        ones1 = sb("ones1", [1, 128])
        adac = sb("adac", [128, L, 4, KC])
        gb = sb("gb", [128, 2, D])
        R_ada = Res()
        cact = sb("cact", [128, KC]); R_cact = Res()

        dma(identf[:], identf_d[:, :], (), [R_c])
        dma(identb[:], identb_d[:, :], (), [R_c])
        dma(cact[:], ccol_d[:, :], (), [R_cact])
        epsc = sb("epsc", [128, 1])
        mset(epsc[:], EPS, [R_c])
        mset(zer[:], 0.0, [R_c])
        mset(ones1[:], 1.0, [R_c])
        act(cact[:], cact[:], AF.Silu, [R_cact], [R_cact])

        for l in range(L):
            for (src, dst, rows, cols) in ((w1_d, w1b_d, D, DFF), (w3_d, w3b_d, D, DFF), (w2_d, w2b_d, DFF, D)):
                for r0 in range(0, rows, 128):
                    dma(dst[l, r0:r0 + 128, :], src[l, r0:r0 + 128, :], (), [R_ffw], q="pool")

        def compute_ada(l, do_cols=True, do_gb=True):
            with Scope() as st:
                Cb = sb("Cb", [128, KC, 128], F32, st); R_Cb = Res()
                wad = [sb(f"wad{i}", [128, KC, 512], F32, st) for i in range(2)]; R_wad = [Res(), Res()]
                brow = [sb(f"brow{i}", [1, 512], F32, st) for i in range(2)]
                tmp = sb("adatmp", [128, 512], F32, st); R_tmp = Res()
                for kc in range(KC):
                    act(Cb[:, kc, :], zer[:], AF.Identity, [R_cact, R_c], [R_Cb], bias=cact[:, kc:kc + 1], scale=0.0)
                wv = w_ada_d[l].rearrange("(kc p) n -> p kc n", p=128)
                for j in range(12):
                    i = j % 2
                    isgb = (j // 2) in (2, 5)
                    if (isgb and not do_gb) or ((not isgb) and not do_cols):
                        continue
                    dma(wad[i][:], wv[:, :, j * 512:(j + 1) * 512], (), [R_wad[i]])
                    dma(brow[i][:], b_ada_d[l:l + 1, j * 512:(j + 1) * 512], (), [R_wad[i]])
                    pt, pr = nps()
                    for kc in range(KC):
                        pe_mm(pt[:, :], Cb[:, kc, :], wad[i][:, kc, :], kc == 0, False, [R_Cb, R_wad[i]], [pr])
                    pe_mm(pt[:, :], ones1[:, :], brow[i][:, :], False, True, [R_wad[i], R_c], [pr])
                    sec, half = j // 2, j % 2
                    if sec in (2, 5):
                        cp(gb[:, 0 if sec == 2 else 1, half * 512:(half + 1) * 512], pt[:, :], [pr], [R_ada])
                    else:
                        cp(tmp[:], pt[:, :], [pr], [R_tmp])
                        si = {0: 0, 1: 1, 3: 2, 4: 3}[sec]
                        p2, p2r = nps()
                        for q in range(4):
                            pe_tr(p2[:, q * 128:(q + 1) * 128], tmp[:, q * 128:(q + 1) * 128], identf[:], [R_tmp, R_c], [p2r])
                        for q in range(4):
                            kc = half * 4 + q
                            if si in (1, 3):
                                ts(adac[:, l, si, kc:kc + 1], p2[:, q * 128:q * 128 + 1], 1.0, None, ALU.add, None, [p2r], [R_ada])
                            else:
                                cp(adac[:, l, si, kc:kc + 1], p2[:, q * 128:q * 128 + 1], [p2r], [R_ada])
            P.barrier()

        def ln_tile(src, dst, st6, mv, R_src, R_dst, R_st, gamma=None, beta=None, R_gb=None):
            bnstats(st6[:, 0:6], src[:, 0:512], [R_src], [R_st])
            bnstats(st6[:, 6:12], src[:, 512:1024], [R_src], [R_st])
            bnaggr(mv[:, 0:2], st6[:, 0:12], [R_st], [R_st])
            act(mv[:, 2:3], mv[:, 1:2], AF.Sqrt, [R_st], [R_st], bias=epsc[:, 0:1])
            recip(mv[:, 3:4], mv[:, 2:3], [R_st], [R_st])
            ts(dst[:, :], src[:, :], mv[:, 0:1], mv[:, 3:4], ALU.subtract, ALU.mult, [R_src, R_st], [R_dst])
            if gamma is not None:
                tt(dst[:, :], dst[:, :], gamma, ALU.mult, [R_dst, R_gb], [R_dst], eng="pool")
                tt(dst[:, :], dst[:, :], beta, ALU.add, [R_dst, R_gb], [R_dst], eng="pool")

        def mod_to_T(ht, R_ht, dstT, col0, R_dst, which, l):
            for half in range(2):
                pt, pr = nps()
                for q in range(4):
                    kc = half * 4 + q
                    pe_tr(pt[:, q * 128:(q + 1) * 128], ht[:, kc * 128:(kc + 1) * 128], identf[:], [R_ht, R_c], [pr])
                for q in range(4):
                    kc = half * 4 + q
                    act(dstT[:, kc, col0:col0 + 128], pt[:, q * 128:(q + 1) * 128], AF.Identity, [pr, R_ada], [R_dst],
                        bias=adac[:, l, 2 * which, kc:kc + 1], scale=adac[:, l, 2 * which + 1, kc:kc + 1])

        compute_ada(0)
        if L > 1:
            compute_ada(1, do_gb=False)
        with Scope() as st:
            xt = [sb(f"xt{i}", [128, D], F32, st) for i in range(2)]; R_xt = [Res(), Res()]
            ht = [sb(f"ht{i}", [128, D], F32, st) for i in range(2)]; R_ht = [Res(), Res()]
            st6 = [sb(f"st6{i}", [128, 12], F32, st) for i in range(2)]
            mv = [sb(f"mv{i}", [128, 4], F32, st) for i in range(2)]; R_st = [Res(), Res()]
            for t in range(NT):
                i = t % 2
                dma(xt[i][:], x_d[t * 128:(t + 1) * 128, :], (), [R_xt[i]])
                ln_tile(xt[i], ht[i], st6[i], mv[i], R_xt[i], R_ht[i], R_st[i])
                dma(h_scr[t * 128:(t + 1) * 128, :], ht[i][:], [R_ht[i]], [R_h[t]], q="sp")
                mod_to_T(ht[i], R_ht[i], UT, t * 128, R_UT[t], 0, 0)
        P.barrier()

        for l in range(L):
            if "stopA" in debug:
                break
            with Scope() as stB:
                TA = sb("TA", [128, NT, 8], F32, stB); TB = sb("TB", [128, NT, 8], F32, stB)
                TG = sb("TG", [128, NT, 8], F32, stB); GL = sb("GL", [128, NT, 8], F32, stB)
                MP = sb("MP", [128, NT, 8], F32, stB); DEC = sb("DEC", [128, NT, 8], F32, stB)
                DECn = sb("DECn", [128, NT, 8], F32, stB); WS = sb("WS", [128, NT, 8], F32, stB)
                WS2 = sb("WS2", [128, NT, 8], F32, stB); ET = sb("ET", [128, NT, 8], F32, stB)
                R_tab = Res()
                tri = sb("tri", [128, 2, 128], BF16, stB)
                gnwb = sb("gnwb", [128, D], F32, stB)
                cw = sb("cw", [128, 16, 5], F32, stB); cb = sb("cb", [128, 16], F32, stB)
                R_mc = Res()
                dma(tri[:], tri_d[:, :, :], (), [R_mc])
                dma(gnwb[:], gnw_d[l:l + 1, :].partition_broadcast(128), (), [R_mc])
                dma(cw[:], convw_d[l], (), [R_mc])
                dma(cb[:], convb_d[l], (), [R_mc])
                with Scope() as st:
                    T1 = sb("T1", [36, S], F32, st); T2 = sb("T2", [36, S], F32, st); T3 = sb("T3", [36, S], F32, st)
                    R_T1, R_T2, R_T3 = Res(), Res(), Res()
                    wgf = sb("wgf", [128, KC, 2, 36], F32, st); wgb = sb("wgb", [128, KC, 2, 36], BF16, st); R_wg = Res()
                    bgc = sb("bgc", [36, 2], F32, st)
                    sel = sb("sel", [128, 2, 128], F32, st)
                    dma(wgf[:], wg_d[l].rearrange("(kc p) a b -> p kc a b", p=128), (), [R_wg])
                    dma(bgc[:], bg_d[l], (), [R_wg])
                    dma(sel[:], sel_d[:, :, :], (), [R_wg])
                    cp(wgb[:], wgf[:], [R_wg], [R_wg])
                    mset(T3[:], 0.0, [R_T3])
                    for tg in range(8):
                        for gi, (T, RT) in enumerate(((T1, R_T1), (T2, R_T2))):
                            pt, pr = nps()
                            for kc in range(KC):
                                pe_mm(pt[0:36, :], wgb[:, kc, gi, :], UT[:, kc, tg * 512:(tg + 1) * 512], kc == 0, kc == KC - 1,
                                      [R_wg] + R_UT[tg * 4:tg * 4 + 4], [pr])
                            act(T[:, tg * 512:(tg + 1) * 512], pt[0:36, :], AF.Identity, [pr, R_wg], [RT], bias=bgc[:, gi:gi + 1])
                    act(T2[:], T2[:], AF.Exp, [R_T2], [R_T2], scale=-1.0)
                    act(T2[:], T2[:], AF.Ln, [R_T2], [R_T2], bias=1.0)
                    ts(T2[:], T2[:], -0.5, None, ALU.mult, None, [R_T2], [R_T2])
                    scan(T3[0:4, :], T2[0:4, :], T2[0:4, :], 0.0, ALU.add, ALU.add, [R_T2], [R_T3])
                    scan(T3[32:36, ::-1], T2[32:36, ::-1], T2[32:36, ::-1], 0.0, ALU.add, ALU.add, [R_T2], [R_T3])
                    tt(T1[:], T1[:], T3[:], ALU.subtract, [R_T1, R_T3], [R_T1])
                    scan(T2[0:4, :], T1[0:4, :], T1[0:4, :], -1e30, ALU.max, ALU.max, [R_T1], [R_T2])
                    scan(T2[32:36, ::-1], T1[32:36, ::-1], T1[32:36, ::-1], -1e30, ALU.max, ALU.max, [R_T1], [R_T2])
                    for (T, RT, TT) in ((T1, R_T1, TA), (T3, R_T3, TB), (T2, R_T2, TG)):
                        for c0 in range(0, NT, 8):
                            pt, pr = nps()
                            for cc in range(8):
                                c = c0 + cc
                                pe_tr(pt[:, cc * 36:(cc + 1) * 36], T[0:36, c * 128:(c + 1) * 128], identf[0:36, 0:36], [RT, R_c], [pr])
                            pv = pt[:, 0:288].rearrange("p (c k) -> p c k", k=36)
                            cp(TT[:, c0:c0 + 8, 0:4], pv[:, :, 0:4], [pr], [R_tab])
                            cp(TT[:, c0:c0 + 8, 4:8], pv[:, :, 32:36], [pr], [R_tab])
                    TGf = TG[:, :, :].rearrange("p c k -> p (c k)")
                    for d_ in range(2):
                        pt, pr = nps()
                        pe_mm(pt[:, 0:256], sel[:, d_, :], TGf, True, True, [R_tab, R_wg], [pr])
                        pv = pt[:, 0:256].rearrange("p (c k) -> p c k", k=8)
                        cp(GL[:, :, d_ * 4:(d_ + 1) * 4], pv[:, :, d_ * 4:(d_ + 1) * 4], [pr], [R_tab])
                    cp(MP[:], GL[:], [R_tab], [R_tab])
                    cp(MP[:, 1:NT, 0:4], GL[:, 0:NT - 1, 0:4], [R_tab], [R_tab])
                    cp(MP[:, 0:NT - 1, 4:8], GL[:, 1:NT, 4:8], [R_tab], [R_tab])
                    tt(DEC[:], MP[:], GL[:], ALU.subtract, [R_tab], [R_tab])
                    act(DEC[:], DEC[:], AF.Exp, [R_tab], [R_tab])
                    tt(WS[:], TA[:], GL[:], ALU.subtract, [R_tab], [R_tab])
                    act(WS[:], WS[:], AF.Exp, [R_tab], [R_tab])
                    tt(ET[:], TB[:], GL[:], ALU.add, [R_tab], [R_tab])
                    act(ET[:], ET[:], AF.Exp, [R_tab], [R_tab], scale=-1.0)
                    mset(DECn[:], 1.0, [R_tab])
                    cp(DECn[:, 0:NT - 1, 0:4], DEC[:, 1:NT, 0:4], [R_tab], [R_tab])
                    cp(DECn[:, 1:NT, 4:8], DEC[:, 0:NT - 1, 4:8], [R_tab], [R_tab])
                    tt(WS2[:], WS[:], DECn[:], ALU.mult, [R_tab], [R_tab])
                P.barrier()

                for hd in range(4):
                    if "skip_mlstm" in debug:
                        break
                    with Scope() as stH:
                        qT = sb("qT", [128, 2, S], BF16, stH); kT = sb("kT", [128, 2, S], BF16, stH)
                        V1 = sb("V1", [128, NT, 257], BF16, stH)
                        R_qT, R_kT = Res(), Res()
                        R_V1 = [Res() for _ in range(NT)]
                        wo = sb("wo", [128, KC, 256], BF16, stH)
                        R_w = Res()
                        wi = w_in_d[l].rearrange("(kc p) n -> p kc n", p=128)
                        dma(wo[:], wi[:, :, 3072 + hd * 256:3072 + hd * 256 + 256], (), [R_w], q="pool")
                        mset(V1[:, :, 256:257], 1.0, R_V1)
                        with Scope() as st:
                            wq = sb("wq", [128, KC, 256], BF16, st); wk = sb("wk", [128, KC, 256], BF16, st)
                            wv = sb("wv", [128, KC, 256], BF16, st)
                            for (wt, c0) in ((wq, hd * 256), (wk, 1024 + hd * 256), (wv, 2048 + hd * 256)):
                                dma(wt[:], wi[:, :, c0:c0 + 256], (), [R_w], q="pool")
                            X = sb("X", [128, S + 4], F32, st); ACC = sb("ACC", [128, S], F32, st)
                            R_Xq = [Res() for _ in range(4)]; R_Aq = [Res() for _ in range(4)]
                            mset(X[:, 0:2], 0.0, [R_Xq[0]]); mset(X[:, S + 2:S + 4], 0.0, [R_Xq[3]])
                            for isk, (wt, dstT, R_dst) in enumerate(((wq, qT, R_qT), (wk, kT, R_kT))):
                                for j in range(2):
                                    cc = isk * 8 + hd * 2 + j
                                    for tg in range(8):
                                        pt, pr = nps()
                                        for kc in range(KC):
                                            pe_mm(pt[:, :], wt[:, kc, j * 128:(j + 1) * 128], UT[:, kc, tg * 512:(tg + 1) * 512],
                                                  kc == 0, kc == KC - 1, [R_w] + R_UT[tg * 4:tg * 4 + 4], [pr])
                                        act(X[:, 2 + tg * 512:2 + (tg + 1) * 512], pt[:, :], AF.Copy, [pr], [R_Xq[tg // 2]])
                                    for qq in range(4):
                                        c0 = qq * 1024
                                        rx = [R_Xq[i_] for i_ in (qq - 1, qq, qq + 1) if 0 <= i_ < 4]
                                        acs = ACC[:, c0:c0 + 1024]
                                        act(acs, X[:, c0:c0 + 1024], AF.Identity, rx + [R_mc], [R_Aq[qq]],
                                            bias=cb[:, cc:cc + 1], scale=cw[:, cc, 0:1])
                                        for k in range(1, 5):
                                            stt(acs, X[:, c0 + k:c0 + k + 1024], cw[:, cc, k:k + 1], acs, ALU.mult, ALU.add,
                                                rx + [R_mc, R_Aq[qq]], [R_Aq[qq]])
                                        if isk == 0:
                                            act(acs, acs, AF.Silu, [R_Aq[qq]], [R_Aq[qq]])
                                            act(dstT[:, j, c0:c0 + 1024], acs, AF.Copy, [R_Aq[qq]], [R_dst], scale=1.0 / 16.0)
                                        else:
                                            act(dstT[:, j, c0:c0 + 1024], acs, AF.Silu, [R_Aq[qq]], [R_dst])
                            for t in range(NT):
                                pt, pr = nps()
                                for kc in range(KC):
                                    pe_mm(pt[:, 0:256], UT[:, kc, t * 128:(t + 1) * 128], wv[:, kc, :], kc == 0, kc == KC - 1,
                                          [R_w, R_UT[t]], [pr])
                                act(V1[:, t, 0:256], pt[:, 0:256], AF.Copy, [pr], [R_V1[t]])
                        P.barrier()
                        with Scope() as st:
                            H = sb("H", [128, NT, 256], F32, st); R_H = [Res() for _ in range(NT)]
                            hw = [False] * NT
                            C32 = [sb(f"C32_{d_}", [128, 2, 257], F32, st) for d_ in range(2)]
                            Cbf2 = [[sb(f"Cbf_{d_}_{p_}", [128, 2, 257], BF16, st) for p_ in range(2)] for d_ in range(2)]
                            R_C = [Res(), Res()]
                            R_Cb2 = [[Res(), Res()], [Res(), Res()]]
                            PT2 = [[sb(f"PT{d_}_{p_}", [128, 128], BF16, st) for p_ in range(2)] for d_ in range(2)]
                            R_PT2 = [[Res(), Res()], [Res(), Res()]]
                            ktok2 = [[sb(f"ktok{d_}_{p_}", [128, 256], BF16, st) for p_ in range(2)] for d_ in range(2)]
                            R_kt2 = [[Res(), Res()], [Res(), Res()]]
                            sm2 = [[sb(f"sm{d_}_{p_}", [128, 4], F32, st) for p_ in range(2)] for d_ in range(2)]
                            R_sm2 = [[Res(), Res()], [Res(), Res()]]
                            for d_ in range(2):
                                mset(C32[d_][:], 0.0, [R_C[d_]])
                                mset(Cbf2[d_][0][:], 0.0, [R_Cb2[d_][0]])
                            def bufs(step):
                                par = step % 2
                                return dict(
                                    PT=[PT2[0][par], PT2[1][par]], R_PT=[R_PT2[0][par], R_PT2[1][par]],
                                    ktok=[ktok2[0][par], ktok2[1][par]], R_kt=[R_kt2[0][par], R_kt2[1][par]],
                                    sm=[sm2[0][par], sm2[1][par]], R_sm=[R_sm2[0][par], R_sm2[1][par]],
                                    Cbf=[Cbf2[0][par], Cbf2[1][par]], R_Cb=[R_Cb2[0][par], R_Cb2[1][par]],
                                    CbfN=[Cbf2[0][1 - par], Cbf2[1][1 - par]], R_CbN=[R_Cb2[0][1 - par], R_Cb2[1][1 - par]])

                            def stage_a(step):
                                B_ = bufs(step)
                                last = step == NT - 1
                                for d_ in range(2):
                                    c = step if d_ == 0 else NT - 1 - step
                                    col = d_ * 4 + hd
                                    sl = slice(c * 128, (c + 1) * 128)
                                    pS, pSr = nps()
                                    for dc in range(2):
                                        pe_mm(pS[:, 0:128], kT[:, dc, sl], qT[:, dc, sl], dc == 0, dc == 1, [R_kT, R_qT], [pSr])
                                    stt(B_["PT"][d_][:], pS[:, 0:128], WS[:, c, col:col + 1], tri[:, d_, :], ALU.mult, ALU.mult,
                                        [pSr, R_tab, R_mc], [B_["R_PT"][d_]])
                                    if not last:
                                        pK, pKr = npsb()
                                        for dc in range(2):
                                            pe_tr(pK[:, dc * 128:(dc + 1) * 128], kT[:, dc, sl], identb[:], [R_kT, R_c], [pKr])
                                        ts(B_["ktok"][d_][:], pK[:, 0:256], WS2[:, c, col:col + 1], None, ALU.mult, None, [pKr, R_tab], [B_["R_kt"][d_]])

                            def stage_b(step):
                                B_ = bufs(step)
                                last = step == NT - 1
                                PT, R_PT, ktok, R_kt, sm, R_sm = B_["PT"], B_["R_PT"], B_["ktok"], B_["R_kt"], B_["sm"], B_["R_sm"]
                                Cbf, R_Cb, CbfN, R_CbN = B_["Cbf"], B_["R_Cb"], B_["CbfN"], B_["R_CbN"]
                                for d_ in range(2):
                                    c = step if d_ == 0 else NT - 1 - step
                                    col = d_ * 4 + hd
                                    sl = slice(c * 128, (c + 1) * 128)
                                    pO, pOr = nps()
                                    pe_mm(pO[:, 0:257], PT[d_][:], V1[:, c, :], True, False, [R_PT[d_], R_V1[c]], [pOr])
                                    for dc in range(2):
                                        pe_mm(pO[:, 0:257], qT[:, dc, sl], Cbf[d_][:, dc, :], False, dc == 1, [R_qT, R_Cb[d_]], [pOr])
                                    act(sm[d_][:, 0:1], pO[:, 256:257], AF.Abs, [pOr], [R_sm[d_]])
                                    tt(sm[d_][:, 1:2], sm[d_][:, 0:1], ET[:, c, col:col + 1], ALU.max, [R_sm[d_], R_tab], [R_sm[d_]])
                                    recip(sm[d_][:, 2:3], sm[d_][:, 1:2], [R_sm[d_]], [R_sm[d_]])
                                    if not hw[c]:
                                        ts(H[:, c, :], pO[:, 0:256], sm[d_][:, 2:3], None, ALU.mult, None, [pOr, R_sm[d_]], [R_H[c]])
                                        hw[c] = True
                                    else:
                                        stt(H[:, c, :], pO[:, 0:256], sm[d_][:, 2:3], H[:, c, :], ALU.mult, ALU.add, [pOr, R_sm[d_], R_H[c]], [R_H[c]])
                                    if not last:
                                        for dc in range(2):
                                            pU, pUr = nps()
                                            pe_mm(pU[:, 0:257], ktok[d_][:, dc * 128:(dc + 1) * 128], V1[:, c, :], True, True,
                                                  [R_kt[d_], R_V1[c]], [pUr])
                                            stt(C32[d_][:, dc, :], C32[d_][:, dc, :], DECn[:, c, col:col + 1], pU[:, 0:257], ALU.mult, ALU.add,
                                                [pUr, R_tab, R_C[d_]], [R_C[d_]])
                                        act(CbfN[d_][:], C32[d_][:], AF.Copy, [R_C[d_]], [R_CbN[d_]])

                            stage_a(0)
                            for step in range(NT):
                                if step + 1 < NT:
                                    stage_a(step + 1)
                                stage_b(step)
                            stg = sb("stg", [128, NT, 6], F32, st); mvg = sb("mvg", [128, NT, 2], F32, st)
                            rs = sb("rs", [128, NT], F32, st); R_g = Res()
                            for c in range(NT):
                                bnstats(stg[:, c, :], H[:, c, :], [R_H[c]], [R_g])
                            for c in range(NT):
                                bnaggr(mvg[:, c, :], stg[:, c, :], [R_g], [R_g])
                            act(rs[:], mvg[:, :, 1], AF.Sqrt, [R_g], [R_g], bias=epsc[:, 0:1])
                            recip(rs[:], rs[:], [R_g], [R_g])
                            sg = [sb(f"sg{i}", [128, 256], F32, st) for i in range(2)]; R_sg = [Res(), Res()]
                            mo = [sb(f"mo{i}", [128, 256], BF16, st) for i in range(2)]; R_mo = [Res(), Res()]
                            for t in range(NT):
                                i = t % 2
                                pt, pr = nps()
                                for kc in range(KC):
                                    pe_mm(pt[:, 0:256], UT[:, kc, t * 128:(t + 1) * 128], wo[:, kc, :], kc == 0, kc == KC - 1, [R_w, R_UT[t]], [pr])
                                act(sg[i][:], pt[:, 0:256], AF.Sigmoid, [pr], [R_sg[i]])
                                ts(H[:, t, :], H[:, t, :], mvg[:, t, 0:1], rs[:, t:t + 1], ALU.subtract, ALU.mult, [R_H[t], R_g], [R_H[t]])
                                tt(H[:, t, :], H[:, t, :], gnwb[:, hd * 256:(hd + 1) * 256], ALU.mult, [R_H[t], R_mc], [R_H[t]], eng="pool")
                                tt(mo[i][:], H[:, t, :], sg[i][:], ALU.mult, [R_H[t], R_sg[i]], [R_mo[i]])
                                dma(mh_scr[t * 128:(t + 1) * 128, hd * 256:(hd + 1) * 256], mo[i][:], [R_mo[i]], [R_mh[t]], q="pool")
                        P.barrier()
                P.barrier()

            with Scope() as stA:
                wdec = sb("wdec", [128, 72, 128], BF16, stA); R_wd = Res()
                dma(wdec[:], wdec_d[:, :, :], (), [R_wd])
                waq = sb("waq", [128, KC, 512], BF16, stA); wak = sb("wak", [128, KC, 512], BF16, stA)
                wav = sb("wav", [128, KC, 512], BF16, stA); R_wa = Res()
                kTa = sb("kTa", [128, 4, 768], BF16, stA); R_kTa = Res()
                qAB = sb("qAB", [128, 4, 2, 512], BF16, stA); R_q = Res()
                V1a = sb("V1a", [128, 6, 8, 65], BF16, stA); R_Va = Res()
                Eb = [sb(f"Eb{i}", [128, 4, 128], BF16, stA) for i in range(4)]; R_E = [Res() for _ in range(4)]
                PTa2 = [[sb(f"PTa{p_}_{i}", [128, 4, 128], BF16, stA) for i in range(3)] for p_ in range(2)]
                R_PTa2 = [[Res() for _ in range(3)] for p_ in range(2)]
                hgcount = 0
                ost = [sb(f"ost{i}", [128, 520], F32, stA) for i in range(2)]; R_ost = [Res(), Res()]
                mset(qAB[:], 0.0, [R_q], eng="pool")
                mset(V1a[:, :, :, 64:65], 1.0, [R_Va])
                wi = w_in_d[l].rearrange("(kc p) n -> p kc n", p=128)
                ecount = 0
                ocount = 0
                for g, dil in enumerate(DILS):
                    if "skip_attn" in debug:
                        break
                    base = 4112 + g * 512
                    for (wt, c0) in ((waq, base), (wak, base + 1536), (wav, base + 3072)):
                        dma(wt[:], wi[:, :, c0:c0 + 512], (), [R_wa], q="pool")
                    n = S // dil
                    NJ = n // 128
                    for r in range(dil):
                        for J0 in range(0, NJ, 4):
                            J1 = min(J0 + 4, NJ)
                            a0 = max(J0 - 1, 0); a1 = min(J1, NJ - 1)
                            nkt = a1 - a0 + 1

                            def cols(a_start, ntile):
                                b0 = (128 * a_start) * dil + r
                                return slice(b0, b0 + (128 * ntile - 1) * dil + 1, dil) if dil > 1 else slice(b0, b0 + 128 * ntile)
                            for pair in range(4):
                                for t0 in range(0, nkt, 4):
                                    nt_ = min(4, nkt - t0)
                                    pt, pr = nps()
                                    for kc in range(KC):
                                        pe_mm(pt[:, 0:128 * nt_], wak[:, kc, pair * 128:(pair + 1) * 128], UT[:, kc, cols(a0 + t0, nt_)],
                                              kc == 0, kc == KC - 1, [R_wa] + R_UT, [pr])
                                    cp(kTa[:, pair, t0 * 128:(t0 + nt_) * 128], pt[:, 0:128 * nt_], [pr], [R_kTa])
                            nq = J1 - J0
                            for pair in range(4):
                                pt, pr = nps()
                                for kc in range(KC):
                                    pe_mm(pt[:, 0:128 * nq], waq[:, kc, pair * 128:(pair + 1) * 128], UT[:, kc, cols(J0, nq)],
                                          kc == 0, kc == KC - 1, [R_wa] + R_UT, [pr])
                                act(qAB[0:64, pair, 0, 0:128 * nq], pt[0:64, 0:128 * nq], AF.Copy, [pr], [R_q], scale=0.125)
                                act(qAB[64:128, pair, 1, 0:128 * nq], pt[64:128, 0:128 * nq], AF.Copy, [pr], [R_q], scale=0.125)
                            for ai in range(nkt):
                                pt, pr = nps()
                                for kc in range(KC):
                                    pe_mm(pt[:, :], UT[:, kc, cols(a0 + ai, 1)], wav[:, kc, :], kc == 0, kc == KC - 1, [R_wa] + R_UT, [pr])
                                cp(V1a[:, ai, :, 0:64], pt[:, :].rearrange("p (h d) -> p h d", d=64), [pr], [R_Va])
                            for J in range(J0, J1):
                                jl = J - J0
                                oi = ocount % 2; ocount += 1
                                for hg in range(2):
                                    PTa = PTa2[hgcount % 2]; R_PTa = R_PTa2[hgcount % 2]; hgcount += 1
                                    chs = [ch for ch in range(3) if 0 <= J - 1 + ch < NJ]
                                    for ch in chs:
                                        ai = J - 1 + ch - a0
                                        pS, pSr = nps()
                                        for hh in range(4):
                                            h = hg * 4 + hh
                                            pe_mm(pS[:, hh * 128:(hh + 1) * 128], kTa[:, h // 2, ai * 128:(ai + 1) * 128],
                                                  qAB[:, h // 2, h % 2, jl * 128:(jl + 1) * 128], True, True, [R_kTa, R_q], [pSr])
                                        ei = ecount % 4; ecount += 1
                                        act(Eb[ei][:], pS[:, :].rearrange("p (h q) -> p h q", q=128), AF.Exp, [pSr], [R_E[ei]])
                                        wb0 = (g * 8 + hg * 4) * 3 + ch
                                        tt(PTa[ch][:], Eb[ei][:], wdec[:, wb0:wb0 + 10:3, :], ALU.mult, [R_E[ei], R_wd], [R_PTa[ch]])
                                    pO, pOr = nps()
                                    for hh in range(4):
                                        h = hg * 4 + hh
                                        for ci, ch in enumerate(chs):
                                            ai = J - 1 + ch - a0
                                            pe_mm(pO[:, hh * 65:(hh + 1) * 65], PTa[ch][:, hh, :], V1a[:, ai, h, :], ci == 0, ci == len(chs) - 1,
                                                  [R_PTa[ch], R_Va], [pOr])
                                    cp(ost[oi][:, hg * 260:(hg + 1) * 260], pO[:, 0:260], [pOr], [R_ost[oi]])
                                rb = (128 * J) * dil + r
                                rows = slice(rb, rb + 127 * dil + 1, dil) if dil > 1 else slice(rb, rb + 128)
                                tl0 = rb // 128; tl1 = (rb + 127 * dil) // 128
                                dma(ao_scr[g][rows, :], ost[oi][:], [R_ost[oi]], [R_ao[g][t] for t in range(tl0, tl1 + 1)], q="sp")
            P.barrier()
            if "stopB" in debug:
                break

            with Scope() as st:
                wa = sb("wa", [128, KC, D], BF16, st); wb = sb("wb", [128, 4, D], BF16, st)
                wgt = sb("wgt", [128, KC, 2048], BF16, st); R_w = Res()
                wi = w_in_d[l].rearrange("(kc p) n -> p kc n", p=128)
                dma(wa[:], w_a_d[l].rearrange("(kc p) n -> p kc n", p=128), (), [R_w], q="pool")
                dma(wb[:], w_b_d[l].rearrange("(kc p) n -> p kc n", p=128), (), [R_w], q="pool")
                for q4 in range(4):
                    dma(wgt[:, :, q4 * 512:(q4 + 1) * 512], wi[:, :, 8720 + q4 * 512:8720 + (q4 + 1) * 512], (), [R_w], q="pool")
                mht = [sb(f"mht{i}", [128, D], BF16, st) for i in range(2)]; R_mht = [Res(), Res()]
                aot = [[sb(f"aot{i}_{g}", [128, 520], F32, st) for g in range(3)] for i in range(2)]; R_aot = [Res(), Res()]
                rden = [sb(f"rden{i}", [128, 8], F32, st) for i in range(2)]
                aon = [sb(f"aon{i}", [128, 512], BF16, st) for i in range(2)]; R_aon = [Res(), Res()]
                mhT = [sb(f"mhT{i}", [128, KC, 128], BF16, st) for i in range(2)]; R_mhT = [Res(), Res()]
                aoT = [sb(f"aoT{i}", [128, 4, 128], BF16, st) for i in range(2)]; R_aoT = [Res(), Res()]
                sga = [sb(f"sga{i}", [128, 512], F32, st) for i in range(2)]; sgb = [sb(f"sgb{i}", [128, 512], F32, st) for i in range(2)]
                R_sgab = [Res(), Res()]
                gat = [sb(f"gat{i}", [128, D], BF16, st) for i in range(2)]; R_gat = [Res(), Res()]
                k2 = 0
                for t in range(NT):
                    i = t % 2
                    rows = slice(t * 128, (t + 1) * 128)
                    dma(mht[i][:], mh_scr[rows, :], [R_mh[t]], [R_mht[i]])
                    for g in range(3):
                        dma(aot[i][g][:], ao_scr[g][rows, :], [R_ao[g][t]], [R_aot[i]])
                    tt(aot[i][0][:], aot[i][0][:], aot[i][1][:], ALU.add, [R_aot[i]], [R_aot[i]])
                    tt(aot[i][0][:], aot[i][0][:], aot[i][2][:], ALU.add, [R_aot[i]], [R_aot[i]])
                    av = aot[i][0][:, :].rearrange("p (h d) -> p h d", d=65)
                    recip(rden[i][:, :], av[:, :, 64], [R_aot[i]], [R_aot[i]])
                    for h in range(8):
                        ts(aon[i][:, h * 64:(h + 1) * 64], av[:, h, 0:64], rden[i][:, h:h + 1], None, ALU.mult, None, [R_aot[i]], [R_aon[i]])
                    pB, pBr = npsb()
                    for kc in range(KC):
                        pe_tr(pB[:, kc * 128:(kc + 1) * 128], mht[i][:, kc * 128:(kc + 1) * 128], identb[:], [R_mht[i], R_c], [pBr])
                    cp(mhT[i][:], pB[:, :].rearrange("p (k t) -> p k t", t=128), [pBr], [R_mhT[i]])
                    pB, pBr = npsb()
                    for kc in range(4):
                        pe_tr(pB[:, kc * 128:(kc + 1) * 128], aon[i][:, kc * 128:(kc + 1) * 128], identb[:], [R_aon[i], R_c], [pBr])
                    cp(aoT[i][:], pB[:, 0:512].rearrange("p (k t) -> p k t", t=128), [pBr], [R_aoT[i]])
                    for half in range(2):
                        hs = slice(half * 512, (half + 1) * 512)
                        j2 = k2 % 2; k2 += 1
                        pga, pgar = nps()
                        for kc in range(KC):
                            pe_mm(pga[:, :], UT[:, kc, rows], wgt[:, kc, half * 512:(half + 1) * 512], kc == 0, kc == KC - 1, [R_UT[t], R_w], [pgar])
                        pgb, pgbr = nps()
                        for kc in range(KC):
                            pe_mm(pgb[:, :], UT[:, kc, rows], wgt[:, kc, 1024 + half * 512:1024 + (half + 1) * 512], kc == 0, kc == KC - 1, [R_UT[t], R_w], [pgbr])
                        act(sga[j2][:], pga[:, :], AF.Sigmoid, [pgar], [R_sgab[j2]])
                        act(sgb[j2][:], pgb[:, :], AF.Sigmoid, [pgbr], [R_sgab[j2]])
                        pya, pyar = nps()
                        for kc in range(KC):
                            pe_mm(pya[:, :], mhT[i][:, kc, :], wa[:, kc, hs], kc == 0, kc == KC - 1, [R_mhT[i], R_w], [pyar])
                        pyb, pybr = nps()
                        for kc in range(4):
                            pe_mm(pyb[:, :], aoT[i][:, kc, :], wb[:, kc, hs], kc == 0, kc == 3, [R_aoT[i], R_w], [pybr])
                        tt(sga[j2][:], sga[j2][:], pya[:, :], ALU.mult, [R_sgab[j2], pyar], [R_sgab[j2]])
                        tt(sgb[j2][:], sgb[j2][:], pyb[:, :], ALU.mult, [R_sgab[j2], pybr], [R_sgab[j2]])
                        tt(gat[i][:, hs], sga[j2][:], sgb[j2][:], ALU.add, [R_sgab[j2]], [R_gat[i]], eng="pool")
                    pB, pBr = npsb()
                    for kc in range(KC):
                        pe_tr(pB[:, kc * 128:(kc + 1) * 128], gat[i][:, kc * 128:(kc + 1) * 128], identb[:], [R_gat[i], R_c], [pBr])
                    cp(UT[:, :, rows], pB[:, :].rearrange("p (k t) -> p k t", t=128), [pBr], [R_UT[t]])
            P.barrier()
            if l > 0:
                compute_ada(l, do_cols=False)
            with Scope() as st:
                wout = sb("wout", [128, KC, D], BF16, st); R_w = Res()
                dma(wout[:], w_out_d[l].rearrange("(kc p) n -> p kc n", p=128), (), [R_w], q="pool")
                lnb = sb("lnb", [128, 2, D], F32, st); R_ln = Res()
                for i_, src in enumerate((ln1g_d, ln1b_d)):
                    dma(lnb[:, i_, :], src[l:l + 1, :].partition_broadcast(128), (), [R_ln])
                hin = [sb(f"hin{i}", [128, D], F32, st) for i in range(2)]; R_hin = [Res(), Res()]
                rt = [sb(f"rt{i}", [128, D], F32, st) for i in range(2)]; R_rt = [Res(), Res()]
                h1 = [sb(f"h1_{i}", [128, D], F32, st) for i in range(2)]; R_h1 = [Res(), Res()]
                st6 = [sb(f"st6_{i}", [128, 12], F32, st) for i in range(2)]
                mv = [sb(f"mv_{i}", [128, 4], F32, st) for i in range(2)]; R_st = [Res(), Res()]
                u2t = [sb(f"u2t{i}", [128, KC, 128], BF16, st) for i in range(2)]; R_u2t = [Res(), Res()]
                for t in range(NT):
                    i = t % 2
                    rows = slice(t * 128, (t + 1) * 128)
                    dma(hin[i][:], h_scr[rows, :], [R_h[t]], [R_hin[i]])
                    for half in range(2):
                        hs = slice(half * 512, (half + 1) * 512)
                        pt, pr = nps()
                        for kc in range(KC):
                            pe_mm(pt[:, :], UT[:, kc, rows], wout[:, kc, hs], kc == 0, kc == KC - 1, [R_UT[t], R_w], [pr])
                        tt(rt[i][:, hs], pt[:, :], gb[:, 0, hs], ALU.mult, [pr, R_ada], [R_rt[i]])
                    stt(rt[i][:], hin[i][:], ALU_ALPHA, rt[i][:], ALU.mult, ALU.add, [R_hin[i], R_rt[i]], [R_rt[i]])
                    ln_tile(rt[i], h1[i], st6[i], mv[i], R_rt[i], R_h1[i], R_st[i], gamma=lnb[:, 0, :], beta=lnb[:, 1, :], R_gb=R_ln)
                    dma(h_scr[rows, :], h1[i][:], [R_h1[i]], [R_h[t]], q="pool")
                    mod_to_T(h1[i], R_h1[i], u2t[i], 0, R_u2t[i], 1, l)
                    dma(u2_scr[:, :, rows], u2t[i][:], [R_u2t[i]], [R_u2[t]], q="pool")
            with Scope() as st:
                lnb = sb("lnb2", [128, 2, D], F32, st); R_ln = Res()
                for i_, src in enumerate((ln2g_d, ln2b_d)):
                    dma(lnb[:, i_, :], src[l:l + 1, :].partition_broadcast(128), (), [R_ln])
                hin = [sb(f"hinD{i}", [128, D], F32, st) for i in range(2)]; R_hin = [Res(), Res()]
                rt = [sb(f"rtD{i}", [128, D], F32, st) for i in range(4)]; R_rt = [Res() for _ in range(4)]
                st6 = [sb(f"st6D{i}", [128, 12], F32, st) for i in range(2)]
                mv = [sb(f"mvD{i}", [128, 4], F32, st) for i in range(2)]; R_st = [Res(), Res()]
                u2g = sb("u2g", [128, KC, 512], BF16, st); R_u2g = Res()
                hidT = sb("hidT", [128, 22, 512], BF16, st); R_hid = Res()
                w1s = [sb(f"w1s{i}", [128, KC, 256], BF16, st) for i in range(2)]
                w3s = [sb(f"w3s{i}", [128, KC, 256], BF16, st) for i in range(2)]
                R_ws = [Res(), Res()]
                w2h = sb("w2h", [128, 22, 512], BF16, st); R_w2 = Res()
                sa = [sb(f"sa{i}", [128, 512], F32, st) for i in range(2)]; R_sa = [Res(), Res()]
                w1v = w1b_d[l].rearrange("(kc p) n -> p kc n", p=128)
                w3v = w3b_d[l].rearrange("(kc p) n -> p kc n", p=128)
                w2v = w2b_d[l].rearrange("(f p) n -> p f n", p=128)
                kk = 0; ks = 0; kh = 0
                for G in range(8):
                    dma(u2g[:], u2_scr[:, :, G * 512:(G + 1) * 512], [R_u2[4 * G + i_] for i_ in range(4)], [R_u2g])
                    for b in range(11):
                        si = ks % 2; ks += 1
                        dma(w1s[si][:], w1v[:, :, b * 256:(b + 1) * 256], [R_ffw], [R_ws[si]])
                        dma(w3s[si][:], w3v[:, :, b * 256:(b + 1) * 256], [R_ffw], [R_ws[si]])
                        for j in range(2):
                            fb = b * 2 + j
                            pa, par = nps()
                            for kc in range(KC):
                                pe_mm(pa[:, :], w1s[si][:, kc, j * 128:(j + 1) * 128], u2g[:, kc, :], kc == 0, kc == KC - 1, [R_ws[si], R_u2g], [par])
                            pb, pbr = nps()
                            for kc in range(KC):
                                pe_mm(pb[:, :], w3s[si][:, kc, j * 128:(j + 1) * 128], u2g[:, kc, :], kc == 0, kc == KC - 1, [R_ws[si], R_u2g], [pbr])
                            ai = kk % 2; kk += 1
                            act(sa[ai][:], pa[:, :], AF.Silu, [par], [R_sa[ai]])
                            tt(hidT[:, fb, :], sa[ai][:], pb[:, :], ALU.mult, [R_sa[ai], pbr], [R_hid])
                    for half in range(2):
                        hs = slice(half * 512, (half + 1) * 512)
                        dma(w2h[:], w2v[:, :, hs], [R_ffw], [R_w2])
                        for ti in range(4):
                            pt, pr = nps()
                            for fb in range(22):
                                pe_mm(pt[:, :], hidT[:, fb, ti * 128:(ti + 1) * 128], w2h[:, fb, :], fb == 0, fb == 21, [R_hid, R_w2], [pr])
                            tt(rt[ti][:, hs], pt[:, :], gb[:, 1, hs], ALU.mult, [pr, R_ada], [R_rt[ti]])
                    for ti in range(4):
                        t = G * 4 + ti
                        rows = slice(t * 128, (t + 1) * 128)
                        hi = kh % 2; kh += 1
                        dma(hin[hi][:], h_scr[rows, :], [R_h[t]], [R_hin[hi]])
                        stt(rt[ti][:], hin[hi][:], ALU_ALPHA, rt[ti][:], ALU.mult, ALU.add, [R_hin[hi], R_rt[ti]], [R_rt[ti]])
                        ln_tile(rt[ti], rt[ti], st6[hi], mv[hi], R_rt[ti], R_rt[ti], R_st[hi], gamma=lnb[:, 0, :], beta=lnb[:, 1, :], R_gb=R_ln)
                        if l + 1 < L:
                            dma(h_scr[rows, :], rt[ti][:], [R_rt[ti]], [R_h[t]], q="pool")
                            mod_to_T(rt[ti], R_rt[ti], UT, t * 128, R_UT[t], 0, l + 1)
                        else:
                            dma(out_d[rows, :], rt[ti][:], [R_rt[ti]], [R_h[t]], q="pool")

        if ut_dbg is not None:
            dma(ut_dbg[:, :, :], UT[:], R_UT, [Res()])

        sems = {}
        for key in P.keys():
            sems[key] = es.enter_context(nc.semaphore("s_" + "_".join(str(k) for k in key)))
        with nc.Block() as block:
            P.emit(block, sems)
    return nc


ALU_ALPHA = float(ALPHA)


def _consts():
    identf = np.eye(128, dtype=np.float32)
    identb = identf.astype(ml_dtypes.bfloat16)
    s = np.arange(128)[:, None]; t = np.arange(128)[None, :]
    tri = np.stack([(s <= t), (s >= t)], axis=1).astype(np.float32).astype(ml_dtypes.bfloat16)
    sel = np.zeros((128, 2, 128), np.float32)
    sel[127, 0, :] = 1.0
    sel[0, 1, :] = 1.0
    hh = np.arange(1, 25, dtype=np.float32)
    slopes = np.exp2(-8.0 * hh / 24).astype(np.float32).reshape(3, 8)
    wdec = np.zeros((128, 72, 128), np.float64)
    p = np.arange(128)[:, None].astype(np.float64); j = np.arange(128)[None, :].astype(np.float64)
    for g, dil in enumerate(DILS):
        for h in range(8):
            for ch in range(3):
                delta = j - p - 128.0 * (ch - 1)
                valid = np.abs(delta) <= 64
                w = np.where(valid, np.exp(-float(slopes[g, h]) * dil * np.abs(delta)), 0.0)
                wdec[:, (g * 8 + h) * 3 + ch, :] = w
    return dict(identf=identf, identb=identb, tri=tri, sel=sel, wdec=wdec.astype(np.float32).astype(ml_dtypes.bfloat16))


def make_in_maps(inputs, n_layers=DEPTH, l0=0, x_override=None):
    f = lambda a: np.ascontiguousarray(np.asarray(a, dtype=np.float32))
    Ls = slice(l0, l0 + n_layers)
    w_in = f(inputs["w_in"])[Ls]
    L = n_layers
    wg = np.zeros((L, D, 2, 36), np.float32)
    wg[:, :, 0, 0:4] = w_in[:, :, 4096:4100]; wg[:, :, 0, 32:36] = w_in[:, :, 4104:4108]
    wg[:, :, 1, 0:4] = w_in[:, :, 4100:4104]; wg[:, :, 1, 32:36] = w_in[:, :, 4108:4112]
    bgs = f(inputs["b_gates"])[Ls]
    bg = np.zeros((L, 36, 2), np.float32)
    bg[:, 0:4, 0] = bgs[:, 0:4]; bg[:, 32:36, 0] = bgs[:, 8:12]
    bg[:, 0:4, 1] = bgs[:, 4:8]; bg[:, 32:36, 1] = bgs[:, 12:16]
    cwv = f(inputs["conv_w"])[Ls]
    convw = np.ascontiguousarray(cwv.transpose(0, 2, 1).reshape(L, 16, 128, 5).transpose(0, 2, 1, 3))
    convb = np.ascontiguousarray(f(inputs["conv_b"])[Ls].reshape(L, 16, 128).transpose(0, 2, 1))
    common = dict(w_in=w_in, wg=wg, bg=bg, convw=convw, convb=convb, gn_w=f(inputs["gn_w"])[Ls],
                  w_a=f(inputs["w_a"])[Ls], w_b=f(inputs["w_b"])[Ls], w_out=f(inputs["w_out"])[Ls],
                  w_ada=f(inputs["w_ada"])[Ls], b_ada=f(inputs["b_ada"])[Ls],
                  ln1_g=f(inputs["ln1_g"])[Ls], ln1_b=f(inputs["ln1_b"])[Ls], ln2_g=f(inputs["ln2_g"])[Ls], ln2_b=f(inputs["ln2_b"])[Ls],
                  w1=f(inputs["w1"])[Ls], w3=f(inputs["w3"])[Ls], w2=f(inputs["w2"])[Ls])
    common.update(_consts())
    x = f(inputs["x"]) if x_override is None else x_override
    c = f(inputs["c"])
    maps = []
    for b in range(x.shape[0]):
        m = dict(common)
        m["x"] = np.ascontiguousarray(x[b])
        m["ccol"] = np.ascontiguousarray(c[b].reshape(KC, 128).T)
        maps.append(m)
    return maps


def kernel(**inputs):
    nc = build(DEPTH)
    maps = make_in_maps(inputs)
    res = run_bass_kernel_spmd(nc, maps, core_ids=list(range(len(maps))))
    return np.stack([np.asarray(r["out"], dtype=np.float32) for r in res.results], axis=0)
```

```python
import numpy as np
import ml_dtypes
from contextlib import ExitStack
import concourse.bass as bass
import concourse.mybir as mybir
from concourse.bass_utils import run_bass_kernel_spmd

F32, BF16 = mybir.dt.float32, mybir.dt.bfloat16
AF = mybir.ActivationFunctionType
ALU = mybir.AluOpType

S = 4096
D = 1024
NT = 32
KC = 8
DFF = 2816
NIN = 10768
DEPTH = 2
ALPHA = (2 * DEPTH) ** 0.25
EPS = 1e-5
DILS = (1, 4, 16)
NCORES = 4


class Res:
    __slots__ = ("w", "r", "name")

    def __init__(self, name=""):
        self.w = None
        self.r = {}
        self.name = name


class Prog:
    ENG = ("pe", "act", "dve", "pool", "sp")

    def __init__(self):
        self.ops = {e: [] for e in self.ENG}
        self.cnt = {e: 0 for e in self.ENG}
        self.seen = {e: {} for e in self.ENG}
        self.ndsem = {"sp": 20, "pool": 20}
        self.dcount = {q: 0 for q in self.ndsem}
        self.last = {}
        self.bar = {e: None for e in self.ENG}

    def _wait(self, eng, key, val, waits):
        if key == ("e", "pe") and eng == "pe":
            return
        if self.seen[eng].get(key, 0) >= val:
            return
        self.seen[eng][key] = val
        waits.append((key, val))

    def add(self, eng, fn, reads=(), writes=(), dma=False):
        waits = []
        if self.bar[eng] is not None:
            for key, val in self.bar[eng].items():
                self._wait(eng, key, val, waits)
            self.bar[eng] = None
        for r in reads:
            if r.w is not None:
                self._wait(eng, r.w[0], r.w[1], waits)
        for w in writes:
            if w.w is not None:
                self._wait(eng, w.w[0], w.w[1], waits)
            for key, val in w.r.items():
                self._wait(eng, key, val, waits)
        if dma:
            j = self.dcount[eng]
            self.dcount[eng] += 1
            n = self.ndsem[eng]
            idx, rnd = j % n, j // n
            key = ("d", eng, idx)
            if rnd > 0:
                self._wait(eng, key, 16 * rnd, waits)
            ev = (key, 16 * (rnd + 1))
        else:
            self.cnt[eng] += 1
            ev = (("e", eng), self.cnt[eng])
        self.last[ev[0]] = ev[1]
        self.ops[eng].append((waits, fn, ev))
        for r in reads:
            if r.r.get(ev[0], 0) < ev[1]:
                r.r[ev[0]] = ev[1]
        for w in writes:
            w.w = ev
            w.r = {}
        return ev

    def barrier(self):
        snap = dict(self.last)
        for e in self.ENG:
            self.bar[e] = dict(snap)

    def keys(self):
        ks = [("e", e) for e in self.ENG]
        for q, n in self.ndsem.items():
            ks += [("d", q, i) for i in range(n)]
        return ks

    def emit(self, block, sems):
        dec = {"pe": block.tensor, "act": block.scalar, "dve": block.vector,
               "pool": block.gpsimd, "sp": block.sync}
        final = dict(self.last)
        for eng in self.ENG:
            ops = self.ops[eng]

            def body(e, ops=ops, eng=eng):
                for waits, fn, ev in ops:
                    for key, val in waits:
                        e.wait_ge(sems[key], val)
                    ins = fn(e)
                    ins.then_inc(sems[ev[0]], 16 if ev[0][0] == "d" else 1)
                if eng == "sp":
                    for key, val in final.items():
                        e.wait_ge(sems[key], val)

            dec[eng](body)


def build(n_layers=DEPTH, debug=None):
    debug = debug or set()
    nc = bass.Bass("TRN2", target_bir_lowering=False)
    P = Prog()
    L = n_layers

    def din(name, shape, dt=F32):
        return nc.dram_tensor(name, list(shape), dt, kind="ExternalInput").ap()

    def dscr(name, shape, dt=F32):
        kind = "ExternalOutput" if name in debug else "Internal"
        return nc.dram_tensor(name, list(shape), dt, kind=kind).ap()

    x_d = din("x", [S, D])
    ccol_d = din("ccol", [128, KC])
    w_in_d = din("w_in", [L, D, NIN])
    wg_d = din("wg", [L, D, 2, 36])
    bg_d = din("bg", [L, 36, 2])
    convw_d = din("convw", [L, 128, 16, 5])
    convb_d = din("convb", [L, 128, 16])
    gnw_d = din("gn_w", [L, D])
    w_a_d = din("w_a", [L, D, D])
    w_b_d = din("w_b", [L, 512, D])
    w_out_d = din("w_out", [L, D, D])
    w_ada_d = din("w_ada", [L, D, 6 * D])
    b_ada_d = din("b_ada", [L, 6 * D])
    ln1g_d = din("ln1_g", [L, D]); ln1b_d = din("ln1_b", [L, D])
    ln2g_d = din("ln2_g", [L, D]); ln2b_d = din("ln2_b", [L, D])
    w1_d = din("w1", [L, D, DFF]); w3_d = din("w3", [L, D, DFF]); w2_d = din("w2", [L, DFF, D])
    identf_d = din("identf", [128, 128])
    identb_d = din("identb", [128, 128], BF16)
    tri_d = din("tri", [128, 2, 128], BF16)
    sel_d = din("sel", [128, 2, 128])
    wdec_d = din("wdec", [128, 72, 128], BF16)
    out_d = nc.dram_tensor("out", [S, D], F32, kind="ExternalOutput").ap()

    h_scr = dscr("h_scr", [S, D])
    mh_scr = dscr("mh_scr", [S, D], BF16)
    ao_scr = [dscr(f"ao_scr{g}", [S, 520]) for g in range(3)]
    w1b_d = dscr("w1b", [L, D, DFF], BF16)
    w3b_d = dscr("w3b", [L, D, DFF], BF16)
    w2b_d = dscr("w2b", [L, DFF, D], BF16)
    ut_dbg = dscr("ut_dbg", [128, KC, S], BF16) if "ut_dbg" in debug else None
    R_h = [Res() for _ in range(NT)]
    R_mh = [Res() for _ in range(NT)]
    R_ao = [[Res() for _ in range(NT)] for _ in range(3)]
    R_ffw = Res()
    u2_scr = dscr("u2_scr", [128, KC, S], BF16)
    R_u2 = [Res() for _ in range(NT)]

    es = ExitStack()
    with es:
        CAP = 196608
        big = es.enter_context(nc.sbuf_tensor("big", [128, CAP], mybir.dt.uint8))
        alloc = {"off": 0, "peak": 0}

        def sb(name, shape, dt=F32, stack=None):
            esz = 4 if dt == F32 else 2
            nel = 1
            for d_ in shape[1:]:
                nel *= d_
            nb = nel * esz
            off = alloc["off"]
            alloc["off"] = off + ((nb + 63) // 64) * 64
            alloc["peak"] = max(alloc["peak"], alloc["off"])
            assert alloc["off"] <= CAP, (name, alloc["off"])
            ap = big[0:shape[0], off:off + nb].bitcast(dt)
            if len(shape) == 3:
                ap = ap.rearrange("p (a b) -> p a b", b=shape[2])
            elif len(shape) == 4:
                ap = ap.rearrange("p (a b c) -> p a b c", b=shape[2], c=shape[3])
            return ap

        class Scope:
            def __enter__(self):
                self.m = alloc["off"]
                return self

            def enter_context(self, x):
                return x

            def __exit__(self, *a):
                alloc["off"] = self.m
                P.barrier()
                return False

        def pst(name, shape, dt=F32):
            return es.enter_context(nc.psum_tensor(name, list(shape), dt))

        ps = [pst(f"ps{i}", [128, 512]) for i in range(6)]
        psR = [Res() for _ in range(6)]
        psb = [pst(f"psb{i}", [128, 1024], BF16) for i in range(2)]
        psbR = [Res() for _ in range(2)]
        rr = {"ps": 0, "psb": 0}

        def nps():
            i = rr["ps"]; rr["ps"] = (i + 1) % 6
            return ps[i], psR[i]

        def npsb():
            i = rr["psb"]; rr["psb"] = (i + 1) % 2
            return psb[i], psbR[i]

        def pe_mm(out, lhsT, rhs, start, stop, reads, writes):
            P.add("pe", lambda e: e.matmul(out, lhsT=lhsT, rhs=rhs, start=start, stop=stop), reads, writes)

        def pe_tr(out, in_, ident, reads, writes):
            P.add("pe", lambda e: e.transpose(out, in_, ident), reads, writes)

        def act(out, in_, func, reads, writes, bias=None, scale=None):
            kw = {}
            if bias is not None: kw["bias"] = bias
            if scale is not None: kw["scale"] = scale
            P.add("act", lambda e: e.activation(out=out, in_=in_, func=func, **kw), reads, writes)

        def tt(out, in0, in1, op, reads, writes, eng="dve"):
            P.add(eng, lambda e: e.tensor_tensor(out=out, in0=in0, in1=in1, op=op), reads, writes)

        def ts(out, in0, s1, s2, op0, op1, reads, writes, eng="dve"):
            if op1 is None:
                P.add(eng, lambda e: e.tensor_scalar(out=out, in0=in0, scalar1=s1, scalar2=None, op0=op0), reads, writes)
            else:
                P.add(eng, lambda e: e.tensor_scalar(out=out, in0=in0, scalar1=s1, scalar2=s2, op0=op0, op1=op1), reads, writes)

        def stt(out, in0, scalar, in1, op0, op1, reads, writes):
            P.add("dve", lambda e: e.scalar_tensor_tensor(out=out, in0=in0, scalar=scalar, in1=in1, op0=op0, op1=op1), reads, writes)

        def cp(out, in_, reads, writes, eng="dve"):
            P.add(eng, lambda e: e.tensor_copy(out=out, in_=in_), reads, writes)

        def mset(ap, val, writes, eng="dve"):
            P.add(eng, lambda e: e.memset(ap, val), (), writes)

        def bnstats(out, in_, reads, writes):
            P.add("dve", lambda e: e.bn_stats(out=out, in_=in_), reads, writes)

        def bnaggr(out, in_, reads, writes):
            P.add("dve", lambda e: e.bn_aggr(out=out, in_=in_), reads, writes)

        def recip(out, in_, reads, writes):
            P.add("dve", lambda e: e.reciprocal(out=out, in_=in_), reads, writes)

        def scan(out, d0, d1, init, op0, op1, reads, writes):
            P.add("dve", lambda e: e.tensor_tensor_scan(out=out, data0=d0, data1=d1, initial=init, op0=op0, op1=op1), reads, writes)

        def dma(out, in_, reads, writes, q="sp"):
            P.add(q, lambda e: e.dma_start(out=out, in_=in_), reads, writes, dma=True)

        UT = sb("UT", [128, KC, S], BF16)
        R_UT = [Res() for _ in range(NT)]
        identf = sb("identf", [128, 128]); R_c = Res()
        identb = sb("identb", [128, 128], BF16)
        zer = sb("zer", [128, 128])
        ones1 = sb("ones1", [1, 128])
        adac = sb("adac", [128, L, 4, KC])
        gb = sb("gb", [128, 2, D])
        R_ada = Res()
        cact = sb("cact", [128, KC]); R_cact = Res()

        dma(identf[:], identf_d[:, :], (), [R_c])
        dma(identb[:], identb_d[:, :], (), [R_c])
        dma(cact[:], ccol_d[:, :], (), [R_cact])
        epsc = sb("epsc", [128, 1])
        mset(epsc[:], EPS, [R_c])
        mset(zer[:], 0.0, [R_c])
        mset(ones1[:], 1.0, [R_c])
        act(cact[:], cact[:], AF.Silu, [R_cact], [R_cact])

        for l in range(L):
            for (src, dst, rows, cols) in ((w1_d, w1b_d, D, DFF), (w3_d, w3b_d, D, DFF), (w2_d, w2b_d, DFF, D)):
                for r0 in range(0, rows, 128):
                    dma(dst[l, r0:r0 + 128, :], src[l, r0:r0 + 128, :], (), [R_ffw], q="pool")

        def compute_ada(l, do_cols=True, do_gb=True):
            with Scope() as st:
                Cb = sb("Cb", [128, KC, 128], F32, st); R_Cb = Res()
                wad = [sb(f"wad{i}", [128, KC, 512], F32, st) for i in range(2)]; R_wad = [Res(), Res()]
                brow = [sb(f"brow{i}", [1, 512], F32, st) for i in range(2)]
                tmp = sb("adatmp", [128, 512], F32, st); R_tmp = Res()
                for kc in range(KC):
                    act(Cb[:, kc, :], zer[:], AF.Identity, [R_cact, R_c], [R_Cb], bias=cact[:, kc:kc + 1], scale=0.0)
                wv = w_ada_d[l].rearrange("(kc p) n -> p kc n", p=128)
                for j in range(12):
                    i = j % 2
                    isgb = (j // 2) in (2, 5)
                    if (isgb and not do_gb) or ((not isgb) and not do_cols):
                        continue
                    dma(wad[i][:], wv[:, :, j * 512:(j + 1) * 512], (), [R_wad[i]])
                    dma(brow[i][:], b_ada_d[l:l + 1, j * 512:(j + 1) * 512], (), [R_wad[i]])
                    pt, pr = nps()
                    for kc in range(KC):
                        pe_mm(pt[:, :], Cb[:, kc, :], wad[i][:, kc, :], kc == 0, False, [R_Cb, R_wad[i]], [pr])
                    pe_mm(pt[:, :], ones1[:, :], brow[i][:, :], False, True, [R_wad[i], R_c], [pr])
                    sec, half = j // 2, j % 2
                    if sec in (2, 5):
                        cp(gb[:, 0 if sec == 2 else 1, half * 512:(half + 1) * 512], pt[:, :], [pr], [R_ada])
                    else:
                        cp(tmp[:], pt[:, :], [pr], [R_tmp])
                        si = {0: 0, 1: 1, 3: 2, 4: 3}[sec]
                        p2, p2r = nps()
                        for q in range(4):
                            pe_tr(p2[:, q * 128:(q + 1) * 128], tmp[:, q * 128:(q + 1) * 128], identf[:], [R_tmp, R_c], [p2r])
                        for q in range(4):
                            kc = half * 4 + q
                            if si in (1, 3):
                                ts(adac[:, l, si, kc:kc + 1], p2[:, q * 128:q * 128 + 1], 1.0, None, ALU.add, None, [p2r], [R_ada])
                            else:
                                cp(adac[:, l, si, kc:kc + 1], p2[:, q * 128:q * 128 + 1], [p2r], [R_ada])
            P.barrier()

        def ln_tile(src, dst, st6, mv, R_src, R_dst, R_st, gamma=None, beta=None, R_gb=None):
            bnstats(st6[:, 0:6], src[:, 0:512], [R_src], [R_st])
            bnstats(st6[:, 6:12], src[:, 512:1024], [R_src], [R_st])
            bnaggr(mv[:, 0:2], st6[:, 0:12], [R_st], [R_st])
            act(mv[:, 2:3], mv[:, 1:2], AF.Sqrt, [R_st], [R_st], bias=epsc[:, 0:1])
            recip(mv[:, 3:4], mv[:, 2:3], [R_st], [R_st])
            ts(dst[:, :], src[:, :], mv[:, 0:1], mv[:, 3:4], ALU.subtract, ALU.mult, [R_src, R_st], [R_dst])
            if gamma is not None:
                tt(dst[:, :], dst[:, :], gamma, ALU.mult, [R_dst, R_gb], [R_dst])
                tt(dst[:, :], dst[:, :], beta, ALU.add, [R_dst, R_gb], [R_dst])

        def mod_to_T(ht, R_ht, dstT, col0, R_dst, which, l):
            for half in range(2):
                pt, pr = nps()
                for q in range(4):
                    kc = half * 4 + q
                    pe_tr(pt[:, q * 128:(q + 1) * 128], ht[:, kc * 128:(kc + 1) * 128], identf[:], [R_ht, R_c], [pr])
                for q in range(4):
                    kc = half * 4 + q
                    act(dstT[:, kc, col0:col0 + 128], pt[:, q * 128:(q + 1) * 128], AF.Identity, [pr, R_ada], [R_dst],
                        bias=adac[:, l, 2 * which, kc:kc + 1], scale=adac[:, l, 2 * which + 1, kc:kc + 1])

        compute_ada(0)
        if L > 1:
            compute_ada(1, do_gb=False)
        with Scope() as st:
            xt = [sb(f"xt{i}", [128, D], F32, st) for i in range(2)]; R_xt = [Res(), Res()]
            ht = [sb(f"ht{i}", [128, D], F32, st) for i in range(2)]; R_ht = [Res(), Res()]
            st6 = [sb(f"st6{i}", [128, 12], F32, st) for i in range(2)]
            mv = [sb(f"mv{i}", [128, 4], F32, st) for i in range(2)]; R_st = [Res(), Res()]
            for t in range(NT):
                i = t % 2
                dma(xt[i][:], x_d[t * 128:(t + 1) * 128, :], (), [R_xt[i]])
                ln_tile(xt[i], ht[i], st6[i], mv[i], R_xt[i], R_ht[i], R_st[i])
                dma(h_scr[t * 128:(t + 1) * 128, :], ht[i][:], [R_ht[i]], [R_h[t]], q="sp")
                mod_to_T(ht[i], R_ht[i], UT, t * 128, R_UT[t], 0, 0)
        P.barrier()

        for l in range(L):
            if "stopA" in debug:
                break
            with Scope() as stB:
                TA = sb("TA", [128, NT, 8], F32, stB); TB = sb("TB", [128, NT, 8], F32, stB)
                TG = sb("TG", [128, NT, 8], F32, stB); GL = sb("GL", [128, NT, 8], F32, stB)
                MP = sb("MP", [128, NT, 8], F32, stB); DEC = sb("DEC", [128, NT, 8], F32, stB)
                DECn = sb("DECn", [128, NT, 8], F32, stB); WS = sb("WS", [128, NT, 8], F32, stB)
                WS2 = sb("WS2", [128, NT, 8], F32, stB); ET = sb("ET", [128, NT, 8], F32, stB)
                R_tab = Res()
                tri = sb("tri", [128, 2, 128], BF16, stB)
                gnwb = sb("gnwb", [128, D], F32, stB)
                cw = sb("cw", [128, 16, 5], F32, stB); cb = sb("cb", [128, 16], F32, stB)
                R_mc = Res()
                dma(tri[:], tri_d[:, :, :], (), [R_mc])
                dma(gnwb[:], gnw_d[l:l + 1, :].partition_broadcast(128), (), [R_mc])
                dma(cw[:], convw_d[l], (), [R_mc])
                dma(cb[:], convb_d[l], (), [R_mc])
                with Scope() as st:
                    T1 = sb("T1", [36, S], F32, st); T2 = sb("T2", [36, S], F32, st); T3 = sb("T3", [36, S], F32, st)
                    R_T1, R_T2, R_T3 = Res(), Res(), Res()
                    wgf = sb("wgf", [128, KC, 2, 36], F32, st); wgb = sb("wgb", [128, KC, 2, 36], BF16, st); R_wg = Res()
                    bgc = sb("bgc", [36, 2], F32, st)
                    sel = sb("sel", [128, 2, 128], F32, st)
                    dma(wgf[:], wg_d[l].rearrange("(kc p) a b -> p kc a b", p=128), (), [R_wg])
                    dma(bgc[:], bg_d[l], (), [R_wg])
                    dma(sel[:], sel_d[:, :, :], (), [R_wg])
                    cp(wgb[:], wgf[:], [R_wg], [R_wg])
                    mset(T3[:], 0.0, [R_T3])
                    for tg in range(8):
                        for gi, (T, RT) in enumerate(((T1, R_T1), (T2, R_T2))):
                            pt, pr = nps()
                            for kc in range(KC):
                                pe_mm(pt[0:36, :], wgb[:, kc, gi, :], UT[:, kc, tg * 512:(tg + 1) * 512], kc == 0, kc == KC - 1,
                                      [R_wg] + R_UT[tg * 4:tg * 4 + 4], [pr])
                            act(T[:, tg * 512:(tg + 1) * 512], pt[0:36, :], AF.Identity, [pr, R_wg], [RT], bias=bgc[:, gi:gi + 1])
                    act(T2[:], T2[:], AF.Exp, [R_T2], [R_T2], scale=-1.0)
                    act(T2[:], T2[:], AF.Ln, [R_T2], [R_T2], bias=1.0)
                    ts(T2[:], T2[:], -0.5, None, ALU.mult, None, [R_T2], [R_T2])
                    scan(T3[0:4, :], T2[0:4, :], T2[0:4, :], 0.0, ALU.add, ALU.add, [R_T2], [R_T3])
                    scan(T3[32:36, ::-1], T2[32:36, ::-1], T2[32:36, ::-1], 0.0, ALU.add, ALU.add, [R_T2], [R_T3])
                    tt(T1[:], T1[:], T3[:], ALU.subtract, [R_T1, R_T3], [R_T1])
                    scan(T2[0:4, :], T1[0:4, :], T1[0:4, :], -1e30, ALU.max, ALU.max, [R_T1], [R_T2])
                    scan(T2[32:36, ::-1], T1[32:36, ::-1], T1[32:36, ::-1], -1e30, ALU.max, ALU.max, [R_T1], [R_T2])
                    for (T, RT, TT) in ((T1, R_T1, TA), (T3, R_T3, TB), (T2, R_T2, TG)):
                        for c0 in range(0, NT, 8):
                            pt, pr = nps()
                            for cc in range(8):
                                c = c0 + cc
                                pe_tr(pt[:, cc * 36:(cc + 1) * 36], T[0:36, c * 128:(c + 1) * 128], identf[0:36, 0:36], [RT, R_c], [pr])
                            pv = pt[:, 0:288].rearrange("p (c k) -> p c k", k=36)
                            cp(TT[:, c0:c0 + 8, 0:4], pv[:, :, 0:4], [pr], [R_tab])
                            cp(TT[:, c0:c0 + 8, 4:8], pv[:, :, 32:36], [pr], [R_tab])
                    TGf = TG[:, :, :].rearrange("p c k -> p (c k)")
                    for d_ in range(2):
                        pt, pr = nps()
                        pe_mm(pt[:, 0:256], sel[:, d_, :], TGf, True, True, [R_tab, R_wg], [pr])
                        pv = pt[:, 0:256].rearrange("p (c k) -> p c k", k=8)
                        cp(GL[:, :, d_ * 4:(d_ + 1) * 4], pv[:, :, d_ * 4:(d_ + 1) * 4], [pr], [R_tab])
                    cp(MP[:], GL[:], [R_tab], [R_tab])
                    cp(MP[:, 1:NT, 0:4], GL[:, 0:NT - 1, 0:4], [R_tab], [R_tab])
                    cp(MP[:, 0:NT - 1, 4:8], GL[:, 1:NT, 4:8], [R_tab], [R_tab])
                    tt(DEC[:], MP[:], GL[:], ALU.subtract, [R_tab], [R_tab])
                    act(DEC[:], DEC[:], AF.Exp, [R_tab], [R_tab])
                    tt(WS[:], TA[:], GL[:], ALU.subtract, [R_tab], [R_tab])
                    act(WS[:], WS[:], AF.Exp, [R_tab], [R_tab])
                    tt(ET[:], TB[:], GL[:], ALU.add, [R_tab], [R_tab])
                    act(ET[:], ET[:], AF.Exp, [R_tab], [R_tab], scale=-1.0)
                    mset(DECn[:], 1.0, [R_tab])
                    cp(DECn[:, 0:NT - 1, 0:4], DEC[:, 1:NT, 0:4], [R_tab], [R_tab])
                    cp(DECn[:, 1:NT, 4:8], DEC[:, 0:NT - 1, 4:8], [R_tab], [R_tab])
                    tt(WS2[:], WS[:], DECn[:], ALU.mult, [R_tab], [R_tab])
                P.barrier()

                for hd in range(4):
                    if "skip_mlstm" in debug:
                        break
                    with Scope() as stH:
                        qT = sb("qT", [128, 2, S], BF16, stH); kT = sb("kT", [128, 2, S], BF16, stH)
                        V1 = sb("V1", [128, NT, 257], BF16, stH)
                        R_qT, R_kT = Res(), Res()
                        R_V1 = [Res() for _ in range(NT)]
                        wo = sb("wo", [128, KC, 256], BF16, stH)
                        R_w = Res()
                        wi = w_in_d[l].rearrange("(kc p) n -> p kc n", p=128)
                        dma(wo[:], wi[:, :, 3072 + hd * 256:3072 + hd * 256 + 256], (), [R_w], q="pool")
                        mset(V1[:, :, 256:257], 1.0, R_V1)
                        with Scope() as st:
                            wq = sb("wq", [128, KC, 256], BF16, st); wk = sb("wk", [128, KC, 256], BF16, st)
                            wv = sb("wv", [128, KC, 256], BF16, st)
                            for (wt, c0) in ((wq, hd * 256), (wk, 1024 + hd * 256), (wv, 2048 + hd * 256)):
                                dma(wt[:], wi[:, :, c0:c0 + 256], (), [R_w], q="pool")
                            X = sb("X", [128, S + 4], F32, st); ACC = sb("ACC", [128, S], F32, st)
                            R_Xq = [Res() for _ in range(4)]; R_Aq = [Res() for _ in range(4)]
                            mset(X[:, 0:2], 0.0, [R_Xq[0]]); mset(X[:, S + 2:S + 4], 0.0, [R_Xq[3]])
                            for isk, (wt, dstT, R_dst) in enumerate(((wq, qT, R_qT), (wk, kT, R_kT))):
                                for j in range(2):
                                    cc = isk * 8 + hd * 2 + j
                                    for tg in range(8):
                                        pt, pr = nps()
                                        for kc in range(KC):
                                            pe_mm(pt[:, :], wt[:, kc, j * 128:(j + 1) * 128], UT[:, kc, tg * 512:(tg + 1) * 512],
                                                  kc == 0, kc == KC - 1, [R_w] + R_UT[tg * 4:tg * 4 + 4], [pr])
                                        act(X[:, 2 + tg * 512:2 + (tg + 1) * 512], pt[:, :], AF.Copy, [pr], [R_Xq[tg // 2]])
                                    for qq in range(4):
                                        c0 = qq * 1024
                                        rx = [R_Xq[i_] for i_ in (qq - 1, qq, qq + 1) if 0 <= i_ < 4]
                                        acs = ACC[:, c0:c0 + 1024]
                                        act(acs, X[:, c0:c0 + 1024], AF.Identity, rx + [R_mc], [R_Aq[qq]],
                                            bias=cb[:, cc:cc + 1], scale=cw[:, cc, 0:1])
                                        for k in range(1, 5):
                                            stt(acs, X[:, c0 + k:c0 + k + 1024], cw[:, cc, k:k + 1], acs, ALU.mult, ALU.add,
                                                rx + [R_mc, R_Aq[qq]], [R_Aq[qq]])
                                        if isk == 0:
                                            act(acs, acs, AF.Silu, [R_Aq[qq]], [R_Aq[qq]])
                                            act(dstT[:, j, c0:c0 + 1024], acs, AF.Copy, [R_Aq[qq]], [R_dst], scale=1.0 / 16.0)
                                        else:
                                            act(dstT[:, j, c0:c0 + 1024], acs, AF.Silu, [R_Aq[qq]], [R_dst])
                            for t in range(NT):
                                pt, pr = nps()
                                for kc in range(KC):
                                    pe_mm(pt[:, 0:256], UT[:, kc, t * 128:(t + 1) * 128], wv[:, kc, :], kc == 0, kc == KC - 1,
                                          [R_w, R_UT[t]], [pr])
                                act(V1[:, t, 0:256], pt[:, 0:256], AF.Copy, [pr], [R_V1[t]])
                        P.barrier()
                        with Scope() as st:
                            H = sb("H", [128, NT, 256], F32, st); R_H = [Res() for _ in range(NT)]
                            hw = [False] * NT
                            C32 = [sb(f"C32_{d_}", [128, 2, 257], F32, st) for d_ in range(2)]
                            Cbf2 = [[sb(f"Cbf_{d_}_{p_}", [128, 2, 257], BF16, st) for p_ in range(2)] for d_ in range(2)]
                            R_C = [Res(), Res()]
                            R_Cb2 = [[Res(), Res()], [Res(), Res()]]
                            PT2 = [[sb(f"PT{d_}_{p_}", [128, 128], BF16, st) for p_ in range(2)] for d_ in range(2)]
                            R_PT2 = [[Res(), Res()], [Res(), Res()]]
                            ktok2 = [[sb(f"ktok{d_}_{p_}", [128, 256], BF16, st) for p_ in range(2)] for d_ in range(2)]
                            R_kt2 = [[Res(), Res()], [Res(), Res()]]
                            sm2 = [[sb(f"sm{d_}_{p_}", [128, 4], F32, st) for p_ in range(2)] for d_ in range(2)]
                            R_sm2 = [[Res(), Res()], [Res(), Res()]]
                            for d_ in range(2):
                                mset(C32[d_][:], 0.0, [R_C[d_]])
                                mset(Cbf2[d_][0][:], 0.0, [R_Cb2[d_][0]])
                            def bufs(step):
                                par = step % 2
                                return dict(
                                    PT=[PT2[0][par], PT2[1][par]], R_PT=[R_PT2[0][par], R_PT2[1][par]],
                                    ktok=[ktok2[0][par], ktok2[1][par]], R_kt=[R_kt2[0][par], R_kt2[1][par]],
                                    sm=[sm2[0][par], sm2[1][par]], R_sm=[R_sm2[0][par], R_sm2[1][par]],
                                    Cbf=[Cbf2[0][par], Cbf2[1][par]], R_Cb=[R_Cb2[0][par], R_Cb2[1][par]],
                                    CbfN=[Cbf2[0][1 - par], Cbf2[1][1 - par]], R_CbN=[R_Cb2[0][1 - par], R_Cb2[1][1 - par]])

                            def stage_a(step):
                                B_ = bufs(step)
                                last = step == NT - 1
                                for d_ in range(2):
                                    c = step if d_ == 0 else NT - 1 - step
                                    col = d_ * 4 + hd
                                    sl = slice(c * 128, (c + 1) * 128)
                                    pS, pSr = nps()
                                    for dc in range(2):
                                        pe_mm(pS[:, 0:128], kT[:, dc, sl], qT[:, dc, sl], dc == 0, dc == 1, [R_kT, R_qT], [pSr])
                                    stt(B_["PT"][d_][:], pS[:, 0:128], WS[:, c, col:col + 1], tri[:, d_, :], ALU.mult, ALU.mult,
                                        [pSr, R_tab, R_mc], [B_["R_PT"][d_]])
                                    if not last:
                                        pK, pKr = npsb()
                                        for dc in range(2):
                                            pe_tr(pK[:, dc * 128:(dc + 1) * 128], kT[:, dc, sl], identb[:], [R_kT, R_c], [pKr])
                                        ts(B_["ktok"][d_][:], pK[:, 0:256], WS2[:, c, col:col + 1], None, ALU.mult, None, [pKr, R_tab], [B_["R_kt"][d_]])

                            def stage_b(step):
                                B_ = bufs(step)
                                last = step == NT - 1
                                PT, R_PT, ktok, R_kt, sm, R_sm = B_["PT"], B_["R_PT"], B_["ktok"], B_["R_kt"], B_["sm"], B_["R_sm"]
                                Cbf, R_Cb, CbfN, R_CbN = B_["Cbf"], B_["R_Cb"], B_["CbfN"], B_["R_CbN"]
                                for d_ in range(2):
                                    c = step if d_ == 0 else NT - 1 - step
                                    col = d_ * 4 + hd
                                    sl = slice(c * 128, (c + 1) * 128)
                                    pO, pOr = nps()
                                    pe_mm(pO[:, 0:257], PT[d_][:], V1[:, c, :], True, False, [R_PT[d_], R_V1[c]], [pOr])
                                    for dc in range(2):
                                        pe_mm(pO[:, 0:257], qT[:, dc, sl], Cbf[d_][:, dc, :], False, dc == 1, [R_qT, R_Cb[d_]], [pOr])
                                    act(sm[d_][:, 0:1], pO[:, 256:257], AF.Abs, [pOr], [R_sm[d_]])
                                    tt(sm[d_][:, 1:2], sm[d_][:, 0:1], ET[:, c, col:col + 1], ALU.max, [R_sm[d_], R_tab], [R_sm[d_]])
                                    recip(sm[d_][:, 2:3], sm[d_][:, 1:2], [R_sm[d_]], [R_sm[d_]])
                                    if not hw[c]:
                                        ts(H[:, c, :], pO[:, 0:256], sm[d_][:, 2:3], None, ALU.mult, None, [pOr, R_sm[d_]], [R_H[c]])
                                        hw[c] = True
                                    else:
                                        stt(H[:, c, :], pO[:, 0:256], sm[d_][:, 2:3], H[:, c, :], ALU.mult, ALU.add, [pOr, R_sm[d_], R_H[c]], [R_H[c]])
                                    if not last:
                                        for dc in range(2):
                                            pU, pUr = nps()
                                            pe_mm(pU[:, 0:257], ktok[d_][:, dc * 128:(dc + 1) * 128], V1[:, c, :], True, True,
                                                  [R_kt[d_], R_V1[c]], [pUr])
                                            stt(C32[d_][:, dc, :], C32[d_][:, dc, :], DECn[:, c, col:col + 1], pU[:, 0:257], ALU.mult, ALU.add,
                                                [pUr, R_tab, R_C[d_]], [R_C[d_]])
                                        act(CbfN[d_][:], C32[d_][:], AF.Copy, [R_C[d_]], [R_CbN[d_]])

                            stage_a(0)
                            for step in range(NT):
                                if step + 1 < NT:
                                    stage_a(step + 1)
                                stage_b(step)
                            stg = sb("stg", [128, NT, 6], F32, st); mvg = sb("mvg", [128, NT, 2], F32, st)
                            rs = sb("rs", [128, NT], F32, st); R_g = Res()
                            for c in range(NT):
                                bnstats(stg[:, c, :], H[:, c, :], [R_H[c]], [R_g])
                            for c in range(NT):
                                bnaggr(mvg[:, c, :], stg[:, c, :], [R_g], [R_g])
                            act(rs[:], mvg[:, :, 1], AF.Sqrt, [R_g], [R_g], bias=epsc[:, 0:1])
                            recip(rs[:], rs[:], [R_g], [R_g])
                            sg = [sb(f"sg{i}", [128, 256], F32, st) for i in range(2)]; R_sg = [Res(), Res()]
                            mo = [sb(f"mo{i}", [128, 256], BF16, st) for i in range(2)]; R_mo = [Res(), Res()]
                            for t in range(NT):
                                i = t % 2
                                pt, pr = nps()
                                for kc in range(KC):
                                    pe_mm(pt[:, 0:256], UT[:, kc, t * 128:(t + 1) * 128], wo[:, kc, :], kc == 0, kc == KC - 1, [R_w, R_UT[t]], [pr])
                                act(sg[i][:], pt[:, 0:256], AF.Sigmoid, [pr], [R_sg[i]])
                                ts(H[:, t, :], H[:, t, :], mvg[:, t, 0:1], rs[:, t:t + 1], ALU.subtract, ALU.mult, [R_H[t], R_g], [R_H[t]])
                                tt(H[:, t, :], H[:, t, :], gnwb[:, hd * 256:(hd + 1) * 256], ALU.mult, [R_H[t], R_mc], [R_H[t]], eng="pool")
                                tt(mo[i][:], H[:, t, :], sg[i][:], ALU.mult, [R_H[t], R_sg[i]], [R_mo[i]])
                                dma(mh_scr[t * 128:(t + 1) * 128, hd * 256:(hd + 1) * 256], mo[i][:], [R_mo[i]], [R_mh[t]], q="pool")
                        P.barrier()
                P.barrier()

            with Scope() as stA:
                wdec = sb("wdec", [128, 72, 128], BF16, stA); R_wd = Res()
                dma(wdec[:], wdec_d[:, :, :], (), [R_wd])
                waq = sb("waq", [128, KC, 512], BF16, stA); wak = sb("wak", [128, KC, 512], BF16, stA)
                wav = sb("wav", [128, KC, 512], BF16, stA); R_wa = Res()
                kTa = sb("kTa", [128, 4, 768], BF16, stA); R_kTa = Res()
                qAB = sb("qAB", [128, 4, 2, 512], BF16, stA); R_q = Res()
                V1a = sb("V1a", [128, 6, 8, 65], BF16, stA); R_Va = Res()
                Eb = [sb(f"Eb{i}", [128, 4, 128], BF16, stA) for i in range(4)]; R_E = [Res() for _ in range(4)]
                PTa2 = [[sb(f"PTa{p_}_{i}", [128, 4, 128], BF16, stA) for i in range(3)] for p_ in range(2)]
                R_PTa2 = [[Res() for _ in range(3)] for p_ in range(2)]
                hgcount = 0
                ost = [sb(f"ost{i}", [128, 520], F32, stA) for i in range(2)]; R_ost = [Res(), Res()]
                mset(qAB[:], 0.0, [R_q], eng="pool")
                mset(V1a[:, :, :, 64:65], 1.0, [R_Va])
                wi = w_in_d[l].rearrange("(kc p) n -> p kc n", p=128)
                ecount = 0
                ocount = 0
                for g, dil in enumerate(DILS):
                    if "skip_attn" in debug:
                        break
                    base = 4112 + g * 512
                    for (wt, c0) in ((waq, base), (wak, base + 1536), (wav, base + 3072)):
                        dma(wt[:], wi[:, :, c0:c0 + 512], (), [R_wa], q="pool")
                    n = S // dil
                    NJ = n // 128
                    for r in range(dil):
                        for J0 in range(0, NJ, 4):
                            J1 = min(J0 + 4, NJ)
                            a0 = max(J0 - 1, 0); a1 = min(J1, NJ - 1)
                            nkt = a1 - a0 + 1

                            def cols(a_start, ntile):
                                b0 = (128 * a_start) * dil + r
                                return slice(b0, b0 + (128 * ntile - 1) * dil + 1, dil) if dil > 1 else slice(b0, b0 + 128 * ntile)
                            for pair in range(4):
                                for t0 in range(0, nkt, 4):
                                    nt_ = min(4, nkt - t0)
                                    pt, pr = nps()
                                    for kc in range(KC):
                                        pe_mm(pt[:, 0:128 * nt_], wak[:, kc, pair * 128:(pair + 1) * 128], UT[:, kc, cols(a0 + t0, nt_)],
                                              kc == 0, kc == KC - 1, [R_wa] + R_UT, [pr])
                                    cp(kTa[:, pair, t0 * 128:(t0 + nt_) * 128], pt[:, 0:128 * nt_], [pr], [R_kTa])
                            nq = J1 - J0
                            for pair in range(4):
                                pt, pr = nps()
                                for kc in range(KC):
                                    pe_mm(pt[:, 0:128 * nq], waq[:, kc, pair * 128:(pair + 1) * 128], UT[:, kc, cols(J0, nq)],
                                          kc == 0, kc == KC - 1, [R_wa] + R_UT, [pr])
                                act(qAB[0:64, pair, 0, 0:128 * nq], pt[0:64, 0:128 * nq], AF.Copy, [pr], [R_q], scale=0.125)
                                act(qAB[64:128, pair, 1, 0:128 * nq], pt[64:128, 0:128 * nq], AF.Copy, [pr], [R_q], scale=0.125)
                            for ai in range(nkt):
                                pt, pr = nps()
                                for kc in range(KC):
                                    pe_mm(pt[:, :], UT[:, kc, cols(a0 + ai, 1)], wav[:, kc, :], kc == 0, kc == KC - 1, [R_wa] + R_UT, [pr])
                                cp(V1a[:, ai, :, 0:64], pt[:, :].rearrange("p (h d) -> p h d", d=64), [pr], [R_Va])
                            for J in range(J0, J1):
                                jl = J - J0
                                oi = ocount % 2; ocount += 1
                                for hg in range(2):
                                    PTa = PTa2[hgcount % 2]; R_PTa = R_PTa2[hgcount % 2]; hgcount += 1
                                    chs = [ch for ch in range(3) if 0 <= J - 1 + ch < NJ]
                                    for ch in chs:
                                        ai = J - 1 + ch - a0
                                        pS, pSr = nps()
                                        for hh in range(4):
                                            h = hg * 4 + hh
                                            pe_mm(pS[:, hh * 128:(hh + 1) * 128], kTa[:, h // 2, ai * 128:(ai + 1) * 128],
                                                  qAB[:, h // 2, h % 2, jl * 128:(jl + 1) * 128], True, True, [R_kTa, R_q], [pSr])
                                        ei = ecount % 4; ecount += 1
                                        act(Eb[ei][:], pS[:, :].rearrange("p (h q) -> p h q", q=128), AF.Exp, [pSr], [R_E[ei]])
                                        wb0 = (g * 8 + hg * 4) * 3 + ch
                                        tt(PTa[ch][:], Eb[ei][:], wdec[:, wb0:wb0 + 10:3, :], ALU.mult, [R_E[ei], R_wd], [R_PTa[ch]])
                                    pO, pOr = nps()
                                    for hh in range(4):
                                        h = hg * 4 + hh
                                        for ci, ch in enumerate(chs):
                                            ai = J - 1 + ch - a0
                                            pe_mm(pO[:, hh * 65:(hh + 1) * 65], PTa[ch][:, hh, :], V1a[:, ai, h, :], ci == 0, ci == len(chs) - 1,
                                                  [R_PTa[ch], R_Va], [pOr])
                                    cp(ost[oi][:, hg * 260:(hg + 1) * 260], pO[:, 0:260], [pOr], [R_ost[oi]])
                                rb = (128 * J) * dil + r
                                rows = slice(rb, rb + 127 * dil + 1, dil) if dil > 1 else slice(rb, rb + 128)
                                tl0 = rb // 128; tl1 = (rb + 127 * dil) // 128
                                dma(ao_scr[g][rows, :], ost[oi][:], [R_ost[oi]], [R_ao[g][t] for t in range(tl0, tl1 + 1)], q="sp")
            P.barrier()
            if "stopB" in debug:
                break

            with Scope() as st:
                wa = sb("wa", [128, KC, D], BF16, st); wb = sb("wb", [128, 4, D], BF16, st)
                wgt = sb("wgt", [128, KC, 2048], BF16, st); R_w = Res()
                wi = w_in_d[l].rearrange("(kc p) n -> p kc n", p=128)
                dma(wa[:], w_a_d[l].rearrange("(kc p) n -> p kc n", p=128), (), [R_w], q="pool")
                dma(wb[:], w_b_d[l].rearrange("(kc p) n -> p kc n", p=128), (), [R_w], q="pool")
                for q4 in range(4):
                    dma(wgt[:, :, q4 * 512:(q4 + 1) * 512], wi[:, :, 8720 + q4 * 512:8720 + (q4 + 1) * 512], (), [R_w], q="pool")
                mht = [sb(f"mht{i}", [128, D], BF16, st) for i in range(2)]; R_mht = [Res(), Res()]
                aot = [[sb(f"aot{i}_{g}", [128, 520], F32, st) for g in range(3)] for i in range(2)]; R_aot = [Res(), Res()]
                rden = [sb(f"rden{i}", [128, 8], F32, st) for i in range(2)]
                aon = [sb(f"aon{i}", [128, 512], BF16, st) for i in range(2)]; R_aon = [Res(), Res()]
                mhT = [sb(f"mhT{i}", [128, KC, 128], BF16, st) for i in range(2)]; R_mhT = [Res(), Res()]
                aoT = [sb(f"aoT{i}", [128, 4, 128], BF16, st) for i in range(2)]; R_aoT = [Res(), Res()]
                sga = [sb(f"sga{i}", [128, 512], F32, st) for i in range(2)]; sgb = [sb(f"sgb{i}", [128, 512], F32, st) for i in range(2)]
                R_sgab = [Res(), Res()]
                gat = [sb(f"gat{i}", [128, D], BF16, st) for i in range(2)]; R_gat = [Res(), Res()]
                k2 = 0
                for t in range(NT):
                    i = t % 2
                    rows = slice(t * 128, (t + 1) * 128)
                    dma(mht[i][:], mh_scr[rows, :], [R_mh[t]], [R_mht[i]])
                    for g in range(3):
                        dma(aot[i][g][:], ao_scr[g][rows, :], [R_ao[g][t]], [R_aot[i]])
                    tt(aot[i][0][:], aot[i][0][:], aot[i][1][:], ALU.add, [R_aot[i]], [R_aot[i]])
                    tt(aot[i][0][:], aot[i][0][:], aot[i][2][:], ALU.add, [R_aot[i]], [R_aot[i]])
                    av = aot[i][0][:, :].rearrange("p (h d) -> p h d", d=65)
                    recip(rden[i][:, :], av[:, :, 64], [R_aot[i]], [R_aot[i]])
                    for h in range(8):
                        ts(aon[i][:, h * 64:(h + 1) * 64], av[:, h, 0:64], rden[i][:, h:h + 1], None, ALU.mult, None, [R_aot[i]], [R_aon[i]])
                    for half in range(2):
                        pga, pgar = nps()
                        for kc in range(KC):
                            pe_mm(pga[:, :], UT[:, kc, rows], wgt[:, kc, half * 512:(half + 1) * 512], kc == 0, kc == KC - 1, [R_UT[t], R_w], [pgar])
                        pgb, pgbr = nps()
                        for kc in range(KC):
                            pe_mm(pgb[:, :], UT[:, kc, rows], wgt[:, kc, 1024 + half * 512:1024 + (half + 1) * 512], kc == 0, kc == KC - 1, [R_UT[t], R_w], [pgbr])
                        act(sga[half][:], pga[:, :], AF.Sigmoid, [pgar], [R_sgab[half]])
                        act(sgb[half][:], pgb[:, :], AF.Sigmoid, [pgbr], [R_sgab[half]])
                    pB, pBr = npsb()
                    for kc in range(KC):
                        pe_tr(pB[:, kc * 128:(kc + 1) * 128], mht[i][:, kc * 128:(kc + 1) * 128], identb[:], [R_mht[i], R_c], [pBr])
                    cp(mhT[i][:], pB[:, :].rearrange("p (k t) -> p k t", t=128), [pBr], [R_mhT[i]])
                    pB, pBr = npsb()
                    for kc in range(4):
                        pe_tr(pB[:, kc * 128:(kc + 1) * 128], aon[i][:, kc * 128:(kc + 1) * 128], identb[:], [R_aon[i], R_c], [pBr])
                    cp(aoT[i][:], pB[:, 0:512].rearrange("p (k t) -> p k t", t=128), [pBr], [R_aoT[i]])
                    for half in range(2):
                        hs = slice(half * 512, (half + 1) * 512)
                        j2 = half
                        pya, pyar = nps()
                        for kc in range(KC):
                            pe_mm(pya[:, :], mhT[i][:, kc, :], wa[:, kc, hs], kc == 0, kc == KC - 1, [R_mhT[i], R_w], [pyar])
                        pyb, pybr = nps()
                        for kc in range(4):
                            pe_mm(pyb[:, :], aoT[i][:, kc, :], wb[:, kc, hs], kc == 0, kc == 3, [R_aoT[i], R_w], [pybr])
                        tt(sga[j2][:], sga[j2][:], pya[:, :], ALU.mult, [R_sgab[j2], pyar], [R_sgab[j2]])
                        tt(sgb[j2][:], sgb[j2][:], pyb[:, :], ALU.mult, [R_sgab[j2], pybr], [R_sgab[j2]])
                        tt(gat[i][:, hs], sga[j2][:], sgb[j2][:], ALU.add, [R_sgab[j2]], [R_gat[i]])
                    pB, pBr = npsb()
                    for kc in range(KC):
                        pe_tr(pB[:, kc * 128:(kc + 1) * 128], gat[i][:, kc * 128:(kc + 1) * 128], identb[:], [R_gat[i], R_c], [pBr])
                    cp(UT[:, :, rows], pB[:, :].rearrange("p (k t) -> p k t", t=128), [pBr], [R_UT[t]])
            P.barrier()
            if l > 0:
                compute_ada(l, do_cols=False)
            with Scope() as st:
                wout = sb("wout", [128, KC, D], BF16, st); R_w = Res()
                dma(wout[:], w_out_d[l].rearrange("(kc p) n -> p kc n", p=128), (), [R_w], q="pool")
                lnb = sb("lnb", [128, 2, D], F32, st); R_ln = Res()
                for i_, src in enumerate((ln1g_d, ln1b_d)):
                    dma(lnb[:, i_, :], src[l:l + 1, :].partition_broadcast(128), (), [R_ln])
                hin = [sb(f"hin{i}", [128, D], F32, st) for i in range(2)]; R_hin = [Res(), Res()]
                rt = [sb(f"rt{i}", [128, D], F32, st) for i in range(2)]; R_rt = [Res(), Res()]
                h1 = [sb(f"h1_{i}", [128, D], F32, st) for i in range(2)]; R_h1 = [Res(), Res()]
                st6 = [sb(f"st6_{i}", [128, 12], F32, st) for i in range(2)]
                mv = [sb(f"mv_{i}", [128, 4], F32, st) for i in range(2)]; R_st = [Res(), Res()]
                u2t = [sb(f"u2t{i}", [128, KC, 128], BF16, st) for i in range(2)]; R_u2t = [Res(), Res()]
                for t in range(NT):
                    i = t % 2
                    rows = slice(t * 128, (t + 1) * 128)
                    dma(hin[i][:], h_scr[rows, :], [R_h[t]], [R_hin[i]])
                    for half in range(2):
                        hs = slice(half * 512, (half + 1) * 512)
                        pt, pr = nps()
                        for kc in range(KC):
                            pe_mm(pt[:, :], UT[:, kc, rows], wout[:, kc, hs], kc == 0, kc == KC - 1, [R_UT[t], R_w], [pr])
                        tt(rt[i][:, hs], pt[:, :], gb[:, 0, hs], ALU.mult, [pr, R_ada], [R_rt[i]])
                    stt(rt[i][:], hin[i][:], ALU_ALPHA, rt[i][:], ALU.mult, ALU.add, [R_hin[i], R_rt[i]], [R_rt[i]])
                    ln_tile(rt[i], h1[i], st6[i], mv[i], R_rt[i], R_h1[i], R_st[i], gamma=lnb[:, 0, :], beta=lnb[:, 1, :], R_gb=R_ln)
                    dma(h_scr[rows, :], h1[i][:], [R_h1[i]], [R_h[t]], q="pool")
                    mod_to_T(h1[i], R_h1[i], u2t[i], 0, R_u2t[i], 1, l)
                    dma(u2_scr[:, :, rows], u2t[i][:], [R_u2t[i]], [R_u2[t]], q="pool")
            with Scope() as st:
                lnb = sb("lnb2", [128, 2, D], F32, st); R_ln = Res()
                for i_, src in enumerate((ln2g_d, ln2b_d)):
                    dma(lnb[:, i_, :], src[l:l + 1, :].partition_broadcast(128), (), [R_ln])
                hin = [sb(f"hinD{i}", [128, D], F32, st) for i in range(2)]; R_hin = [Res(), Res()]
                rt = [sb(f"rtD{i}", [128, D], F32, st) for i in range(4)]; R_rt = [Res() for _ in range(4)]
                st6 = [sb(f"st6D{i}", [128, 12], F32, st) for i in range(2)]
                mv = [sb(f"mvD{i}", [128, 4], F32, st) for i in range(2)]; R_st = [Res(), Res()]
                u2g = sb("u2g", [128, KC, 512], BF16, st); R_u2g = Res()
                hidT = sb("hidT", [128, 22, 512], BF16, st); R_hid = Res()
                w1s = [sb(f"w1s{i}", [128, KC, 256], BF16, st) for i in range(2)]
                w3s = [sb(f"w3s{i}", [128, KC, 256], BF16, st) for i in range(2)]
                R_ws = [Res(), Res()]
                w2h = sb("w2h", [128, 22, 512], BF16, st); R_w2 = Res()
                sa = [sb(f"sa{i}", [128, 512], F32, st) for i in range(2)]; R_sa = [Res(), Res()]
                w1v = w1b_d[l].rearrange("(kc p) n -> p kc n", p=128)
                w3v = w3b_d[l].rearrange("(kc p) n -> p kc n", p=128)
                w2v = w2b_d[l].rearrange("(f p) n -> p f n", p=128)
                kk = 0; ks = 0; kh = 0
                for G in range(8):
                    dma(u2g[:], u2_scr[:, :, G * 512:(G + 1) * 512], [R_u2[4 * G + i_] for i_ in range(4)], [R_u2g])
                    for b in range(11):
                        si = ks % 2; ks += 1
                        dma(w1s[si][:], w1v[:, :, b * 256:(b + 1) * 256], [R_ffw], [R_ws[si]])
                        dma(w3s[si][:], w3v[:, :, b * 256:(b + 1) * 256], [R_ffw], [R_ws[si]])
                        for j in range(2):
                            fb = b * 2 + j
                            pa, par = nps()
                            for kc in range(KC):
                                pe_mm(pa[:, :], w1s[si][:, kc, j * 128:(j + 1) * 128], u2g[:, kc, :], kc == 0, kc == KC - 1, [R_ws[si], R_u2g], [par])
                            pb, pbr = nps()
                            for kc in range(KC):
                                pe_mm(pb[:, :], w3s[si][:, kc, j * 128:(j + 1) * 128], u2g[:, kc, :], kc == 0, kc == KC - 1, [R_ws[si], R_u2g], [pbr])
                            ai = kk % 2; kk += 1
                            act(sa[ai][:], pa[:, :], AF.Silu, [par], [R_sa[ai]])
                            tt(hidT[:, fb, :], sa[ai][:], pb[:, :], ALU.mult, [R_sa[ai], pbr], [R_hid])
                    for half in range(2):
                        hs = slice(half * 512, (half + 1) * 512)
                        dma(w2h[:], w2v[:, :, hs], [R_ffw], [R_w2])
                        for ti in range(4):
                            pt, pr = nps()
                            for fb in range(22):
                                pe_mm(pt[:, :], hidT[:, fb, ti * 128:(ti + 1) * 128], w2h[:, fb, :], fb == 0, fb == 21, [R_hid, R_w2], [pr])
                            tt(rt[ti][:, hs], pt[:, :], gb[:, 1, hs], ALU.mult, [pr, R_ada], [R_rt[ti]])
                    for ti in range(4):
                        t = G * 4 + ti
                        rows = slice(t * 128, (t + 1) * 128)
                        hi = kh % 2; kh += 1
                        dma(hin[hi][:], h_scr[rows, :], [R_h[t]], [R_hin[hi]])
                        stt(rt[ti][:], hin[hi][:], ALU_ALPHA, rt[ti][:], ALU.mult, ALU.add, [R_hin[hi], R_rt[ti]], [R_rt[ti]])
                        ln_tile(rt[ti], rt[ti], st6[hi], mv[hi], R_rt[ti], R_rt[ti], R_st[hi], gamma=lnb[:, 0, :], beta=lnb[:, 1, :], R_gb=R_ln)
                        if l + 1 < L:
                            dma(h_scr[rows, :], rt[ti][:], [R_rt[ti]], [R_h[t]], q="pool")
                            mod_to_T(rt[ti], R_rt[ti], UT, t * 128, R_UT[t], 0, l + 1)
                        else:
                            dma(out_d[rows, :], rt[ti][:], [R_rt[ti]], [R_h[t]], q="pool")

        if ut_dbg is not None:
            dma(ut_dbg[:, :, :], UT[:], R_UT, [Res()])

        sems = {}
        for key in P.keys():
            sems[key] = es.enter_context(nc.semaphore("s_" + "_".join(str(k) for k in key)))
        with nc.Block() as block:
            P.emit(block, sems)
    return nc


ALU_ALPHA = float(ALPHA)


def _consts():
    identf = np.eye(128, dtype=np.float32)
    identb = identf.astype(ml_dtypes.bfloat16)
    s = np.arange(128)[:, None]; t = np.arange(128)[None, :]
    tri = np.stack([(s <= t), (s >= t)], axis=1).astype(np.float32).astype(ml_dtypes.bfloat16)
    sel = np.zeros((128, 2, 128), np.float32)
    sel[127, 0, :] = 1.0
    sel[0, 1, :] = 1.0
    hh = np.arange(1, 25, dtype=np.float32)
    slopes = np.exp2(-8.0 * hh / 24).astype(np.float32).reshape(3, 8)
    wdec = np.zeros((128, 72, 128), np.float64)
    p = np.arange(128)[:, None].astype(np.float64); j = np.arange(128)[None, :].astype(np.float64)
    for g, dil in enumerate(DILS):
        for h in range(8):
            for ch in range(3):
                delta = j - p - 128.0 * (ch - 1)
                valid = np.abs(delta) <= 64
                w = np.where(valid, np.exp(-float(slopes[g, h]) * dil * np.abs(delta)), 0.0)
                wdec[:, (g * 8 + h) * 3 + ch, :] = w
    return dict(identf=identf, identb=identb, tri=tri, sel=sel, wdec=wdec.astype(np.float32).astype(ml_dtypes.bfloat16))


def make_in_maps(inputs, n_layers=DEPTH, l0=0, x_override=None):
    f = lambda a: np.ascontiguousarray(np.asarray(a, dtype=np.float32))
    Ls = slice(l0, l0 + n_layers)
    w_in = f(inputs["w_in"])[Ls]
    L = n_layers
    wg = np.zeros((L, D, 2, 36), np.float32)
    wg[:, :, 0, 0:4] = w_in[:, :, 4096:4100]; wg[:, :, 0, 32:36] = w_in[:, :, 4104:4108]
    wg[:, :, 1, 0:4] = w_in[:, :, 4100:4104]; wg[:, :, 1, 32:36] = w_in[:, :, 4108:4112]
    bgs = f(inputs["b_gates"])[Ls]
    bg = np.zeros((L, 36, 2), np.float32)
    bg[:, 0:4, 0] = bgs[:, 0:4]; bg[:, 32:36, 0] = bgs[:, 8:12]
    bg[:, 0:4, 1] = bgs[:, 4:8]; bg[:, 32:36, 1] = bgs[:, 12:16]
    cwv = f(inputs["conv_w"])[Ls]
    convw = np.ascontiguousarray(cwv.transpose(0, 2, 1).reshape(L, 16, 128, 5).transpose(0, 2, 1, 3))
    convb = np.ascontiguousarray(f(inputs["conv_b"])[Ls].reshape(L, 16, 128).transpose(0, 2, 1))
    common = dict(w_in=w_in, wg=wg, bg=bg, convw=convw, convb=convb, gn_w=f(inputs["gn_w"])[Ls],
                  w_a=f(inputs["w_a"])[Ls], w_b=f(inputs["w_b"])[Ls], w_out=f(inputs["w_out"])[Ls],
                  w_ada=f(inputs["w_ada"])[Ls], b_ada=f(inputs["b_ada"])[Ls],
                  ln1_g=f(inputs["ln1_g"])[Ls], ln1_b=f(inputs["ln1_b"])[Ls], ln2_g=f(inputs["ln2_g"])[Ls], ln2_b=f(inputs["ln2_b"])[Ls],
                  w1=f(inputs["w1"])[Ls], w3=f(inputs["w3"])[Ls], w2=f(inputs["w2"])[Ls])
    common.update(_consts())
    x = f(inputs["x"]) if x_override is None else x_override
    c = f(inputs["c"])
    maps = []
    for b in range(x.shape[0]):
        m = dict(common)
        m["x"] = np.ascontiguousarray(x[b])
        m["ccol"] = np.ascontiguousarray(c[b].reshape(KC, 128).T)
        maps.append(m)
    return maps


def kernel(**inputs):
    nc = build(DEPTH)
    maps = make_in_maps(inputs)
    res = run_bass_kernel_spmd(nc, maps, core_ids=list(range(len(maps))))
    return np.stack([np.asarray(r["out"], dtype=np.float32) for r in res.results], axis=0)
```
